# Optimizing a Trainium2 kernel written in Bass

```python
import math
import jax, jax.numpy as jnp
from jax import lax
import numpy as np

D_MODEL = 1024
BATCH = 8
SEQ = 2048
DEPTH = 4

GRID_W = 64
CTX_LEN = 256
N_EVEN = (DEPTH + 1) // 2
N_ODD = DEPTH // 2
EPS = 1e-6

MLA_HEADS = 8
MLA_Q_RANK = 384
MLA_KV_RANK = 256
MLA_NOPE = 64
MLA_ROPE = 32
MLA_V = 64
MLA_SCALE = (MLA_NOPE + MLA_ROPE) ** -0.5
ROPE_THETA = 10000.0
Q_BLOCK = 128
SC_WIDTH = 512
SC_K = 3
HYB_SPLITS = [MLA_Q_RANK, MLA_Q_RANK + MLA_KV_RANK, MLA_Q_RANK + MLA_KV_RANK + MLA_ROPE,
              MLA_Q_RANK + MLA_KV_RANK + MLA_ROPE + SC_WIDTH,
              MLA_Q_RANK + MLA_KV_RANK + MLA_ROPE + 2 * SC_WIDTH]
HYB_IN = MLA_Q_RANK + MLA_KV_RANK + MLA_ROPE + 3 * SC_WIDTH
HYB_MIX = MLA_HEADS * MLA_V + SC_WIDTH
SSD_INNER = 2 * D_MODEL
SSD_HEADDIM = 64
SSD_HEADS = SSD_INNER // SSD_HEADDIM
SSD_GROUPS = 4
SSD_STATE = 128
SSD_CONV_K = 3
SSD_CHUNK = 128
SSD_CONV_DIM = SSD_INNER + 2 * SSD_GROUPS * SSD_STATE
SSD_IN = SSD_INNER + SSD_CONV_DIM + 2 * SSD_HEADS
D_FF = 2816
FFN_K = 3

kernel_name = 'hybrid_mla_shortconv_ssd_convffn_prefix_dit'


def _rmsnorm(x, w):
    x32 = x.astype(jnp.float32)
    y = x32 * lax.rsqrt(jnp.mean(x32 * x32, axis=-1, keepdims=True) + EPS)
    return (y * w.astype(jnp.float32)).astype(x.dtype)


def _dwconv(u, w, b=None):
    k = w.shape[0]
    pad = k // 2
    n = u.shape[1]
    up = jnp.pad(u, ((0, 0), (pad, pad), (0, 0)))
    y = up[:, 0:n] * w[0]
    for i in range(1, k):
        y = y + up[:, i:i + n] * w[i]
    if b is not None:
        y = y + b
    return y


def _axial_angles(rows):
    row = jnp.repeat(jnp.arange(rows, dtype=jnp.float32), GRID_W)
    col = jnp.tile(jnp.arange(GRID_W, dtype=jnp.float32), rows)
    nf = MLA_ROPE // 4
    inv = ROPE_THETA ** (-jnp.arange(nf, dtype=jnp.float32) / nf)
    ang = jnp.concatenate([row[:, None] * inv, col[:, None] * inv], axis=-1)
    return jnp.cos(ang), jnp.sin(ang)


def _rope(x, cos, sin):
    half = x.shape[-1] // 2
    x1, x2 = x[..., :half], x[..., half:]
    cos = cos.astype(x.dtype)
    sin = sin.astype(x.dtype)
    return jnp.concatenate([x1 * cos - x2 * sin, x1 * sin + x2 * cos], axis=-1)


def _attend(qn, qr, kn, kr, v):
    s = jnp.einsum('bqhd,bkhd->bhqk', qn, kn) + jnp.einsum('bqhr,bkr->bhqk', qr, kr)
    p = jax.nn.softmax(s.astype(jnp.float32) * MLA_SCALE, axis=-1).astype(v.dtype)
    return jnp.einsum('bhqk,bkhd->bqhd', p, v)


def _attend_blocked(qn, qr, kn, kr, v):
    b, n, h, _ = qn.shape
    nb = n // Q_BLOCK
    qn_b = qn.reshape(b, nb, Q_BLOCK, h, MLA_NOPE).transpose(1, 0, 2, 3, 4)
    qr_b = qr.reshape(b, nb, Q_BLOCK, h, MLA_ROPE).transpose(1, 0, 2, 3, 4)
    out = lax.map(lambda qs: _attend(qs[0], qs[1], kn, kr, v), (qn_b, qr_b))
    return out.transpose(1, 0, 2, 3, 4).reshape(b, n, h, MLA_V)


def _mla_qkv(cq, ckv, kr, q_norm, kv_norm, w_uq, w_ukv, cos, sin):
    b, n, _ = cq.shape
    q = (_rmsnorm(cq, q_norm) @ w_uq).reshape(b, n, MLA_HEADS, MLA_NOPE + MLA_ROPE)
    kv = (_rmsnorm(ckv, kv_norm) @ w_ukv).reshape(b, n, MLA_HEADS, MLA_NOPE + MLA_V)
    qn, qr = q[..., :MLA_NOPE], q[..., MLA_NOPE:]
    kn, v = kv[..., :MLA_NOPE], kv[..., MLA_NOPE:]
    if cos is not None:
        qr = _rope(qr, cos[:, None, :], sin[:, None, :])
        kr = _rope(kr, cos, sin)
    return qn, qr, kn, kr, v


def _mixer_attn_conv(h_lat, h_ctx, w_in, q_norm, kv_norm, w_uq, w_ukv, sconv_w, w_out, cos, sin, need_ctx_out):
    b, n, _ = h_lat.shape
    cq_l, ckv_l, kr_l, gb_l, gc_l, xv_l = jnp.split(h_lat @ w_in, HYB_SPLITS, axis=-1)
    cq_c, ckv_c, kr_c, gb_c, gc_c, xv_c = jnp.split(h_ctx @ w_in, HYB_SPLITS, axis=-1)
    qn_l, qr_l, kn_l, kr_l, v_l = _mla_qkv(cq_l, ckv_l, kr_l, q_norm, kv_norm, w_uq, w_ukv, cos, sin)
    qn_c, qr_c, kn_c, kr_c, v_c = _mla_qkv(cq_c, ckv_c, kr_c, q_norm, kv_norm, w_uq, w_ukv, None, None)
    kn_all = jnp.concatenate([kn_c, kn_l], axis=1)
    kr_all = jnp.concatenate([kr_c, kr_l], axis=1)
    v_all = jnp.concatenate([v_c, v_l], axis=1)
    attn_l = _attend_blocked(qn_l, qr_l, kn_all, kr_all, v_all).reshape(b, n, MLA_HEADS * MLA_V)
    sc_l = gb_l * _dwconv(gc_l * xv_l, sconv_w)
    out_l = jnp.concatenate([attn_l, sc_l], axis=-1) @ w_out
    if not need_ctx_out:
        return out_l, None
    attn_c = _attend(qn_c, qr_c, kn_c, kr_c, v_c).reshape(b, h_ctx.shape[1], MLA_HEADS * MLA_V)
    sc_c = gb_c * _dwconv(gc_c * xv_c, sconv_w)
    out_c = jnp.concatenate([attn_c, sc_c], axis=-1) @ w_out
    return out_l, out_c


def _segsum(a):
    q = a.shape[-1]
    cs = jnp.cumsum(a, axis=-1)
    diff = cs[..., :, None] - cs[..., None, :]
    mask = jnp.tril(jnp.ones((q, q), dtype=bool))
    return jnp.where(mask, diff, -jnp.inf)


def _ssd_scan(xdt, a, bm, cm, h0):
    b, l, h, p = xdt.shape
    g, n = bm.shape[2], bm.shape[3]
    r = h // g
    c = l // SSD_CHUNK
    x = xdt.reshape(b, c, SSD_CHUNK, g, r, p)
    a = a.astype(jnp.float32).reshape(b, c, SSD_CHUNK, g, r).transpose(0, 1, 3, 4, 2)
    bc = bm.reshape(b, c, SSD_CHUNK, g, n)
    cc = cm.reshape(b, c, SSD_CHUNK, g, n)
    a_cum = jnp.cumsum(a, axis=-1)
    decay_in = jnp.exp(_segsum(a))
    cb = jnp.einsum('bclgn,bcsgn->bcgls', cc, bc)
    y_diag = jnp.einsum('bcgls,bcgrls,bcsgrp->bclgrp', cb, decay_in, x)
    decay_states = jnp.exp(a_cum[..., -1:] - a_cum)
    states = jnp.einsum('bcsgn,bcgrs,bcsgrp->bcgrpn', bc, decay_states, x)
    states = jnp.concatenate([h0.reshape(b, 1, g, r, p, n).astype(states.dtype), states], axis=1)
    tot = jnp.pad(a_cum[..., -1], ((0, 0), (1, 0), (0, 0), (0, 0))).transpose(0, 2, 3, 1)
    decay_chunk = jnp.exp(_segsum(tot))
    new_states = jnp.einsum('bgrzc,bcgrpn->bzgrpn', decay_chunk, states)
    states_in, final = new_states[:, :-1], new_states[:, -1]
    y_off = jnp.einsum('bclgn,bcgrpn,bcgrl->bclgrp', cc, states_in, jnp.exp(a_cum))
    y = (y_diag + y_off).reshape(b, l, h, p)
    return y, final.reshape(b, h, p, n)


def _mixer_ssd(hs, h0_f, h0_b, w_in, conv_w, conv_b, a_log, dt_bias, d_skip, norm_w, w_out, need_out):
    b, n, _ = hs.shape
    z, xbc, dt = jnp.split(hs @ w_in, [SSD_INNER, SSD_INNER + SSD_CONV_DIM], axis=-1)
    xbc = jax.nn.silu(_dwconv(xbc, conv_w, conv_b))
    xs, bm, cm = jnp.split(xbc, [SSD_INNER, SSD_INNER + SSD_GROUPS * SSD_STATE], axis=-1)
    xs = xs.reshape(b, n, SSD_HEADS, SSD_HEADDIM)
    bm = bm.reshape(b, n, SSD_GROUPS, SSD_STATE)
    cm = cm.reshape(b, n, SSD_GROUPS, SSD_STATE)
    dt = jax.nn.softplus(dt.astype(jnp.float32).reshape(b, n, 2, SSD_HEADS) + dt_bias.astype(jnp.float32))
    a = -jnp.exp(a_log.astype(jnp.float32)) * dt
    flip = lambda t: jnp.flip(t, axis=1)
    y_f, hf = _ssd_scan(xs * dt[:, :, 0, :, None], a[:, :, 0], bm, cm, h0_f)
    y_b, hb = _ssd_scan(flip(xs * dt[:, :, 1, :, None]), flip(a[:, :, 1]), flip(bm), flip(cm), h0_b)
    if not need_out:
        return None, hf, hb
    y = (y_f + flip(y_b)).astype(xs.dtype) + d_skip[:, None] * xs
    y = _rmsnorm(y.reshape(b, n, SSD_INNER) * jax.nn.silu(z), norm_w)
    return y @ w_out, hf, hb


def _conv_ffn(h, w_up, conv_w, w_down):
    u = _dwconv(h @ w_up, conv_w)
    act, gate = jnp.split(u, 2, axis=-1)
    return (jax.nn.silu(act) * gate) @ w_down


def _modulation(cond, w, b):
    return jnp.split(jax.nn.silu(cond) @ w + b, 6, axis=-1)


def setup_inputs(seed: int = 0) -> dict:
    key = jax.random.key(seed)
    ks = iter(jax.random.split(key, 40))
    nrm = lambda shape, scale: jax.random.normal(next(ks), shape, jnp.float32) * scale
    gain = lambda shape: 1.0 + 0.05 * jax.random.normal(next(ks), shape, jnp.float32)
    d = D_MODEL
    inputs = {
        'x': nrm((BATCH, SEQ, d), 1.0),
        'c': nrm((BATCH, d), 1.0),
        'ctx': nrm((BATCH, CTX_LEN, d), 1.0),
        'c_ctx': nrm((d,), 1.0),
        'mod_w': nrm((DEPTH, d, 6 * d), d ** -0.5),
        'mod_b': nrm((DEPTH, 6 * d), 0.01),
        'norm_w': gain((DEPTH, 4, d)),
        'ffn_w_up': nrm((DEPTH, d, 2 * D_FF), d ** -0.5),
        'ffn_conv_w': nrm((DEPTH, FFN_K, 2 * D_FF), FFN_K ** -0.5),
        'ffn_w_down': nrm((DEPTH, D_FF, d), D_FF ** -0.5),
        'hyb_w_in': nrm((N_EVEN, d, HYB_IN), d ** -0.5),
        'mla_q_norm': gain((N_EVEN, MLA_Q_RANK)),
        'mla_kv_norm': gain((N_EVEN, MLA_KV_RANK)),
        'mla_w_uq': nrm((N_EVEN, MLA_Q_RANK, MLA_HEADS * (MLA_NOPE + MLA_ROPE)), MLA_Q_RANK ** -0.5),
        'mla_w_ukv': nrm((N_EVEN, MLA_KV_RANK, MLA_HEADS * (MLA_NOPE + MLA_V)), MLA_KV_RANK ** -0.5),
        'sconv_w': nrm((N_EVEN, SC_K, SC_WIDTH), SC_K ** -0.5),
        'hyb_w_out': nrm((N_EVEN, HYB_MIX, d), HYB_MIX ** -0.5),
        'ssd_w_in': nrm((N_ODD, d, SSD_IN), d ** -0.5),
        'ssd_conv_w': nrm((N_ODD, SSD_CONV_K, SSD_CONV_DIM), SSD_CONV_K ** -0.5),
        'ssd_conv_b': nrm((N_ODD, SSD_CONV_DIM), 0.02),
    }
    a_log = jnp.log(jax.random.uniform(next(ks), (N_ODD, 2, SSD_HEADS), jnp.float32, 1.0, 16.0))
    dt0 = jnp.exp(jax.random.uniform(next(ks), (N_ODD, 2, SSD_HEADS), jnp.float32, math.log(1e-3), math.log(1e-1)))
    inputs['ssd_a_log'] = a_log
    inputs['ssd_dt_bias'] = dt0 + jnp.log(-jnp.expm1(-dt0))
    inputs['ssd_d'] = gain((N_ODD, SSD_HEADS))
    inputs['ssd_norm'] = gain((N_ODD, SSD_INNER))
    inputs['ssd_w_out'] = nrm((N_ODD, SSD_INNER, d), SSD_INNER ** -0.5)
    return inputs


def reference(x, c, ctx, c_ctx, mod_w, mod_b, norm_w, ffn_w_up, ffn_conv_w, ffn_w_down,
              hyb_w_in, mla_q_norm, mla_kv_norm, mla_w_uq, mla_w_ukv, sconv_w, hyb_w_out,
              ssd_w_in, ssd_conv_w, ssd_conv_b, ssd_a_log, ssd_dt_bias, ssd_d, ssd_norm, ssd_w_out):
    b = x.shape[0]
    rows = x.shape[1] // GRID_W
    cos, sin = _axial_angles(rows)
    for l in range(DEPTH):
        last = l == DEPTH - 1
        i = l // 2
        sh1, sc1, g1, sh2, sc2, g2 = _modulation(c[:, None, :], mod_w[l], mod_b[l])
        sh1c, sc1c, g1c, sh2c, sc2c, g2c = _modulation(c_ctx, mod_w[l], mod_b[l])
        h_l = _rmsnorm(x, norm_w[l, 0]) * (1.0 + sc1) + sh1
        h_c = _rmsnorm(ctx, norm_w[l, 0]) * (1.0 + sc1c) + sh1c
        if l % 2 == 0:
            m_l, m_c = _mixer_attn_conv(h_l, h_c, hyb_w_in[i], mla_q_norm[i], mla_kv_norm[i], mla_w_uq[i],
                                        mla_w_ukv[i], sconv_w[i], hyb_w_out[i], cos, sin, not last)
        else:
            zeros = jnp.zeros((b, SSD_HEADS, SSD_HEADDIM, SSD_STATE), jnp.float32)
            m_c, hf, hb = _mixer_ssd(h_c, zeros, zeros, ssd_w_in[i], ssd_conv_w[i], ssd_conv_b[i], ssd_a_log[i],
                                     ssd_dt_bias[i], ssd_d[i], ssd_norm[i], ssd_w_out[i], not last)
            m_l, _, _ = _mixer_ssd(h_l, hf, hb, ssd_w_in[i], ssd_conv_w[i], ssd_conv_b[i], ssd_a_log[i],
                                   ssd_dt_bias[i], ssd_d[i], ssd_norm[i], ssd_w_out[i], True)
        x = x + g1 * _rmsnorm(m_l, norm_w[l, 1])
        f_l = _conv_ffn(_rmsnorm(x, norm_w[l, 2]) * (1.0 + sc2) + sh2, ffn_w_up[l], ffn_conv_w[l], ffn_w_down[l])
        x = x + g2 * _rmsnorm(f_l, norm_w[l, 3])
        if not last:
            ctx = ctx + g1c * _rmsnorm(m_c, norm_w[l, 1])
            f_c = _conv_ffn(_rmsnorm(ctx, norm_w[l, 2]) * (1.0 + sc2c) + sh2c, ffn_w_up[l], ffn_conv_w[l], ffn_w_down[l])
            ctx = ctx + g2c * _rmsnorm(f_c, norm_w[l, 3])
    return x
```

```python
import numpy as np
from contextlib import ExitStack
import concourse.bass as bass
import concourse.mybir as mybir
from concourse.bass_utils import run_bass_kernel_spmd

F32 = mybir.dt.float32
BF16 = mybir.dt.bfloat16
AF = mybir.ActivationFunctionType
ALU = mybir.AluOpType

ENGINES = ['tensor', 'vector', 'scalar', 'gpsimd', 'sync']
CHUNK = 16000

D = 1024
T = 2304
NCTX = 256
NLAT = 2048
TILE = 256
NT = T // TILE
EPS = 1e-6
DFF = 2816
NJ = DFF // 128
HYB_IN = 2208
MLA_SCALE = 96 ** -0.5


class Prog:
    def __init__(self, nc):
        self.nc = nc
        self.es = ExitStack()
        self.streams = {e: [] for e in ENGINES}
        self.count = {e: 0 for e in ENGINES}
        self.esems = {e: [] for e in ENGINES}
        self.dsems = {}
        self.dcount = {}
        self.lastw = {}
        self.readers = {}
        self.seen = {e: {} for e in ENGINES}
        self.nsem = 0

    def sb(self, name, shape, dtype):
        return self.es.enter_context(self.nc.sbuf_tensor(name, shape, dtype))

    def ps(self, name, shape, dtype):
        return self.es.enter_context(self.nc.psum_tensor(name, shape, dtype))

    def _newsem(self, name):
        self.nsem += 1
        return self.es.enter_context(self.nc.semaphore(name))

    def _semof(self, semkey, val):
        if semkey[0] == 'E':
            eng = semkey[1]
            ep = (val - 1) // CHUNK
            while len(self.esems[eng]) <= ep:
                self.esems[eng].append(self._newsem("s_%s_%d" % (eng, len(self.esems[eng]))))
            return self.esems[eng][ep], (val - 1) % CHUNK + 1
        return self.dsems[semkey[1]], val

    def _deps(self, eng, reads, writes):
        evs = {}

        def add(ev):
            if ev is None:
                return
            k, v = ev
            if evs.get(k, 0) < v:
                evs[k] = v
        for k in reads:
            add(self.lastw.get(k))
        for k in writes:
            add(self.lastw.get(k))
            for ev in self.readers.get(k, {}).items():
                add(ev)
        waits = []
        seen = self.seen[eng]
        for k, v in evs.items():
            if eng == 'tensor' and k == ('E', 'tensor'):
                continue
            if seen.get(k, 0) >= v:
                continue
            seen[k] = v
            waits.append(self._semof(k, v))
        return waits

    def _commit(self, ev, reads, writes):
        k, v = ev
        for r in reads:
            d = self.readers.setdefault(r, {})
            if d.get(k, 0) < v:
                d[k] = v
        for w in writes:
            self.lastw[w] = ev
            self.readers[w] = {}

    def op(self, eng, fn, reads=(), writes=()):
        waits = self._deps(eng, reads, writes)
        self.count[eng] += 1
        n = self.count[eng]
        ev = (('E', eng), n)
        sem, val = self._semof(ev[0], n)
        self.streams[eng].append((waits, fn, (sem, 1)))
        self._commit(ev, reads, writes)

    def dma(self, queue, out, in_, reads=(), writes=(), buf=None, **kw):
        if buf not in self.dsems:
            self.dsems[buf] = self._newsem("d_%s" % buf)
            self.dcount[buf] = 0
        waits = self._deps(queue, reads, writes)
        self.dcount[buf] += 16
        ev = (('D', buf), self.dcount[buf])
        sem = self.dsems[buf]
        self.streams[queue].append((waits, lambda e: e.dma_start(out=out, in_=in_, **kw), (sem, 16)))
        self._commit(ev, reads, writes)

    def dma_group(self, queue, items, buf, **kw):
        if buf not in self.dsems:
            self.dsems[buf] = self._newsem("d_%s" % buf)
            self.dcount[buf] = 0
        keys = [k for (_, _, k) in items]
        waits = self._deps(queue, [], keys)
        sem = self.dsems[buf]
        first = True
        for (out, in_, k) in items:
            self.dcount[buf] += 16
            self.streams[queue].append((waits if first else [], (lambda e, out=out, in_=in_: e.dma_start(out=out, in_=in_, **kw)), (sem, 16)))
            first = False
        ev = (('D', buf), self.dcount[buf])
        self._commit(ev, [], keys)

    def barrier(self):
        evs = [(('E', e), self.count[e]) for e in ENGINES if self.count[e] > 0]
        evs += [(('D', b), c) for b, c in self.dcount.items()]
        for eng in ENGINES:
            waits = []
            seen = self.seen[eng]
            for k, v in evs:
                if seen.get(k, 0) >= v:
                    continue
                seen[k] = v
                waits.append(self._semof(k, v))
            if waits:
                self.streams[eng].append((waits, None, None))
        self.lastw = {}
        self.readers = {}

    def finish(self, final_waits=()):
        self.barrier()
        nc = self.nc
        with nc.Block() as block:
            for eng in ENGINES:
                stream = self.streams[eng]
                if not stream:
                    continue

                def body(e, stream=stream):
                    for waits, fn, inc in stream:
                        for sem, val in waits:
                            e.wait_ge(sem, val)
                        if fn is None:
                            continue
                        ins = fn(e)
                        if inc is not None:
                            ins.then_inc(inc[0], inc[1])
                getattr(block, eng)(body)
        self.es.close()


def tile_info(i):
    t0 = i * TILE
    has_left = i >= 2
    has_right = 1 <= i <= NT - 2
    return t0, (0 if has_left else 1), (258 if has_right else 257), (1 if i == 0 else 0)


class Builder:
    ARENA = 164 * 1024

    def __init__(self, layers, do_mix=True, do_ffn=True, tiles=None, debug=False):
        self.layers = layers
        self.tiles = tiles
        nc = bass.Bass("TRN2", target_bir_lowering=False)
        self.nc = nc
        self.P = Prog(nc)
        P = self.P
        dt = nc.dram_tensor
        self.xin = dt("xin", [128, 8, T], F32, kind="ExternalInput").ap()
        self.cc = dt("cc", [128, 8, 2], F32, kind="ExternalInput").ap()
        self.modw = dt("modw", [4, 12, 128, 8, 512], F32, kind="ExternalInput").ap()
        self.modb = dt("modb", [4, 128, 48], F32, kind="ExternalInput").ap()
        self.normw = dt("normw", [4, 128, 4, 8], F32, kind="ExternalInput").ap()
        self.wup = dt("wup", [4, NJ, 128, 8, 256], F32, kind="ExternalInput").ap()
        self.wdn = dt("wdn", [4, 128, NJ, 1024], F32, kind="ExternalInput").ap()
        self.fcw = dt("fcw", [4, 128, NJ, 2, 3], F32, kind="ExternalInput").ap()
        self.hwin = dt("hwin", [2, 128, 8, 2240], F32, kind="ExternalInput").ap()
        self.hwuq = dt("hwuq", [2, 128, 3, 1024], F32, kind="ExternalInput").ap()
        self.hwukv = dt("hwukv", [2, 128, 2, 1024], F32, kind="ExternalInput").ap()
        self.hwout = dt("hwout", [2, 128, 8, 1024], F32, kind="ExternalInput").ap()
        self.hqn = dt("hqn", [2, 128, 3], F32, kind="ExternalInput").ap()
        self.hkvn = dt("hkvn", [2, 128, 2], F32, kind="ExternalInput").ap()
        self.hscw = dt("hscw", [2, 128, 4, 3], F32, kind="ExternalInput").ap()
        self.rope = dt("rope", [32, 4, T], F32, kind="ExternalInput").ap()
        self.swin = dt("swin", [2, 128, 8, 5184], F32, kind="ExternalInput").ap()
        self.scv = dt("scv", [2, 128, 24, 4], F32, kind="ExternalInput").ap()
        self.sbc = dt("sbc", [2, 128, 160], F32, kind="ExternalInput").ap()
        self.snorm = dt("snorm", [2, 128, 16], F32, kind="ExternalInput").ap()
        self.swout = dt("swout", [2, 128, 16, 1024], F32, kind="ExternalInput").ap()
        self.ydram = dt("ydram", [18, 128, 2048], F32, kind="Internal").ap()
        self.sxs = dt("sxs", [NT, 128, 2, 2048], F32, kind="Internal").ap()
        self.sbt = dt("sbt", [NT, 128, 3072], BF16, kind="Internal").ap()
        self.sdt = dt("sdt", [NT, 128, 256], F32, kind="Internal").ap()
        self.xdA = dt("xdA", [128, 8, T], F32, kind="ExternalOutput" if debug else "Internal").ap()
        self.xdB = dt("xdB", [128, 8, T], F32, kind="ExternalOutput" if debug else "Internal").ap()
        self.yout = dt("yout", [128, 8, NLAT], F32, kind="ExternalOutput").ap()
        self.bank = [P.ps("bk%d" % i, [128, 512], F32) for i in range(8)]
        self.ones = P.sb("ones", [128, 128], F32)
        self.ones_bf = P.sb("ones_bf", [128, 128], BF16)
        self.ident = P.sb("ident", [128, 128], BF16)
        self.der = P.sb("der", [128, 4, 4, 8, 2], F32)
        self.modv = P.sb("modv", [128, 4, 48, 2], F32)
        self.xt = [P.sb("xt%d" % i, [128, 8, 258], F32) for i in range(2)]
        self.hT = [P.sb("hT%d" % i, [128, 8, 258], BF16) for i in range(2)]
        self.sq = [P.sb("sq%d" % i, [128, 258], BF16) for i in range(4)]
        self.sqp = P.sb("sqp", [128, 8, 258], BF16)
        self.rstd = [P.sb("rstd%d" % i, [128, 258], F32) for i in range(2)]
        self.tmpn = [P.sb("tmpn%d" % i, [128, 258], F32) for i in range(3)]
        self.epsc = P.sb("epsc", [128, 1], F32)
        self.arena = P.sb("arena", [128, self.ARENA // 2], BF16)
        self.cnt = {}
        self.xsrc = (self.xin, 'xi')
        self.xnext = [(self.xdA, 'xa'), (self.xdB, 'xb')]
        self.final_keys = []
        self.emit_consts()
        self.prologue()
        for l in layers:
            last = (l == 3)
            if do_mix:
                if l % 2 == 0:
                    self.even_mixer(l, last)
                else:
                    self.odd_mixer(l, last)
            if do_ffn:
                self.ffn(l, last)
        P.finish()

    def phase(self, resid=True):
        self.P.barrier()
        self.aoff = 0
        if resid:
            self.mtmp = self.carve([128, 8, 256], F32)
            self.rs2 = self.carve([128, 256], F32)

    def carve(self, shape, dtype):
        n = 1
        for d in shape[1:]:
            n *= d
        nb = n * (4 if dtype == F32 else 2)
        nb = (nb + 31) // 32 * 32
        assert self.aoff + nb <= self.ARENA, ("arena overflow", self.aoff, nb)
        ap = self.arena[:, self.aoff // 2:(self.aoff + nb) // 2]
        self.aoff += nb
        if dtype == F32:
            ap = ap.bitcast(F32)
        ap = ap[:, 0:n]
        if len(shape) == 3:
            ap = ap.rearrange("p (a b) -> p a b", a=shape[1])
        elif len(shape) == 4:
            ap = ap.rearrange("p (a b c) -> p a b c", a=shape[1], b=shape[2])
        return ap

    def E(self, eng, method, reads, writes, **kw):
        self.P.op(eng, lambda e: getattr(e, method)(**kw), reads, writes)

    def MM(self, out, pairs, reads, writes, first=True, **kw):
        pairs = list(pairs)

        def fn(e):
            n = len(pairs)
            ins = None
            for i, (l, r) in enumerate(pairs):
                ins = e.matmul(out, lhsT=l, rhs=r, start=(first and i == 0), stop=(i == n - 1), **kw)
            return ins
        self.P.op('tensor', fn, reads, writes)

    def rot(self, name, n):
        i = self.cnt.get(name, 0)
        self.cnt[name] = i + 1
        return i % n

    def emit_consts(self):
        self.E('gpsimd', 'memset', [], ['ones'], ap=self.ones[:], constant=1.0)
        self.E('gpsimd', 'memset', [], ['ones'], ap=self.ones_bf[:], constant=1.0)
        self.E('gpsimd', 'memset', [], ['epsc'], ap=self.epsc[:], constant=EPS)
        self.E('gpsimd', 'memset', [], ['ident'], ap=self.ident[:], constant=1.0)
        self.E('gpsimd', 'affine_select', ['ident'], ['ident'], out=self.ident[:], in_=self.ident[:],
               pattern=[[-1, 128]], compare_op=ALU.is_equal, fill=0.0, base=0, channel_multiplier=1)

    def prologue(self):
        P = self.P
        self.phase()
        cs = self.carve([128, 8, 2], F32)
        sc = self.carve([128, 8, 2], F32)
        wb = [self.carve([128, 8, 512], F32) for i in range(2)]
        mb = self.carve([128, 4, 48], F32)
        nw = self.carve([128, 4, 4, 8], F32)
        P.dma('sync', cs, self.cc, [], ['cs'], buf='cs')
        self.E('scalar', 'activation', ['cs'], ['scs'], out=sc, in_=cs, func=AF.Silu)
        for l in self.layers:
            P.dma('sync', mb[:, l, :], self.modb[l], [], ['mbias%d' % l], buf='mbias%d' % l)
            P.dma('sync', nw[:, l], self.normw[l], [], ['nw%d' % l], buf='nw%d' % l)
        k = 0
        for l in self.layers:
            for nb in range(12):
                w = wb[k % 2]
                wk = 'mwb%d' % (k % 2)
                P.dma('sync', w, self.modw[l, nb], [], [wk], buf=wk)
                bk = 'B%d' % (k % 2)
                for m in range(4):
                    self.MM(self.bank[k % 2][:, 2 * m:2 * m + 2],
                            [(w[:, kc, m * 128:(m + 1) * 128], sc[:, kc, :]) for kc in range(8)],
                            [wk, 'scs'], [bk])
                self.E('vector', 'tensor_tensor', ['mbias%d' % l], [bk, 'modv%d' % l],
                       out=self.modv[:, l, nb * 4:nb * 4 + 4, :],
                       in0=self.bank[k % 2][:, 0:8].rearrange("p (a b) -> p a b", b=2),
                       in1=mb[:, l, nb * 4:nb * 4 + 4].unsqueeze(2).to_broadcast([128, 4, 2]), op=ALU.add)
                k += 1
            mv = self.modv
            dk = 'der%d' % l
            rk = ['modv%d' % l, 'nw%d' % l]

            def nwb(i):
                return nw[:, l, i, :].unsqueeze(2).to_broadcast([128, 8, 2])
            self.E('vector', 'scalar_tensor_tensor', rk, [dk], out=self.der[:, l, 0], in0=mv[:, l, 8:16, :],
                   scalar=1.0, in1=nwb(0), op0=ALU.add, op1=ALU.mult)
            self.E('vector', 'tensor_tensor', rk, [dk], out=self.der[:, l, 1], in0=mv[:, l, 16:24, :], in1=nwb(1), op=ALU.mult)
            self.E('vector', 'scalar_tensor_tensor', rk, [dk], out=self.der[:, l, 2], in0=mv[:, l, 32:40, :],
                   scalar=1.0, in1=nwb(2), op0=ALU.add, op1=ALU.mult)
            self.E('vector', 'tensor_tensor', rk, [dk], out=self.der[:, l, 3], in0=mv[:, l, 40:48, :], in1=nwb(3), op=ALU.mult)

    def next_stream(self):
        return self.xnext[0]

    def swap_stream(self):
        d = self.xnext.pop(0)
        if self.xsrc[1] != 'xi':
            self.xnext.append(self.xsrc)
        self.xsrc = d

    def load_x(self, i, src, halo=True):
        t0, lo, hi, s = tile_info(i)
        if not halo:
            lo, hi = 1, 257
        b = self.rot('xt', getattr(self, 'nxt', 2))
        self.P.dma('sync', self.xt[b][:, :, lo:hi], src[0][:, :, t0 - 1 + lo:t0 - 1 + hi],
                   [src[1] + str(j) for j in (i - 1, i, i + 1) if 0 <= j < NT], ['xt%d' % b], buf='xt%d' % b)
        return b

    def rstd_from_bank(self, bk, bkey, lo, hi, out, okey, n, extra_w=()):
        self.E('scalar', 'activation', ['epsc'], [bkey, okey] + list(extra_w), out=out[:, lo:hi], in_=bk[:, lo:hi],
               func=AF.Sqrt, scale=1.0 / n, bias=self.epsc[:, 0:1])
        self.E('vector', 'reciprocal', [okey], [okey], out=out[:, lo:hi], in_=out[:, lo:hi])

    def prenorm(self, l, i, xb, which, halo=True, part=None):
        t0, lo, hi, s = tile_info(i)
        if not halo:
            lo, hi = 1, 257
        xt = self.xt[xb]
        hb = self.rot('hT', 2)
        hT = self.hT[hb]
        rb = self.rot('rstd', 2)
        rstd = self.rstd[rb]
        bk = self.bank[0]
        if part == 'A':
            for c in range(8):
                self.E('scalar', 'activation', ['xt%d' % xb], ['sqp'], out=self.sqp[:, c, lo:hi], in_=xt[:, c, lo:hi], func=AF.Square)
            return None
        for c in range(8):
            if part == 'B':
                r, rk = self.sqp[:, c, lo:hi], 'sqp'
            else:
                q = self.rot('sq', 4)
                self.E('scalar', 'activation', ['xt%d' % xb], ['sq%d' % q], out=self.sq[q][:, lo:hi], in_=xt[:, c, lo:hi], func=AF.Square)
                r, rk = self.sq[q][:, lo:hi], 'sq%d' % q
            self.P.op('tensor', (lambda e, o=bk[:, lo:hi], r=r, c=c: e.matmul(o, lhsT=self.ones_bf[:], rhs=r, start=(c == 0), stop=(c == 7))),
                      ['ones', rk], ['B0'])
        self.rstd_from_bank(bk, 'B0', lo, hi, rstd, 'rstd%d' % rb, D)
        Ai = 0 if which == 0 else 2
        Boff = 0 if which == 0 else 24
        for c in range(8):
            q = self.rot('tmpn', 3)
            self.E('vector', 'tensor_tensor', ['xt%d' % xb, 'rstd%d' % rb], ['tmpn%d' % q], out=self.tmpn[q][:, lo:hi],
                   in0=xt[:, c, lo:hi], in1=rstd[:, lo:hi], op=ALU.mult)
            self.E('scalar', 'activation', ['tmpn%d' % q, 'der%d' % l, 'modv%d' % l], ['hT%d' % hb], out=hT[:, c, lo:hi], in_=self.tmpn[q][:, lo:hi],
                   func=AF.Identity, scale=self.der[:, l, Ai, c, s:s + 1], bias=self.modv[:, l, Boff + c, s:s + 1])
        return hb

    def res_tail(self, l, i, which, xb, to_out):
        t0, lo, hi, s = tile_info(i)
        Gi = 1 if which == 0 else 3
        xt = self.xt[xb]
        tail = []
        tail.append(lambda: self.rstd_from_bank(self.bank[7], 'B7', 0, 256, self.rs2, 'rs2', D))

        def pair(c):
            q = self.rot('tmpn', 3)
            self.E('vector', 'scalar_tensor_tensor', ['mtmp', 'rs2', 'der%d' % l], ['tmpn%d' % q], out=self.tmpn[q][:, 0:256], in0=self.mtmp[:, c, :],
                   scalar=self.der[:, l, Gi, c, s:s + 1], in1=self.rs2, op0=ALU.mult, op1=ALU.mult)
            self.E('gpsimd', 'tensor_tensor', ['tmpn%d' % q, 'xt%d' % xb], ['xt%d' % xb], out=xt[:, c, 1:257], in0=xt[:, c, 1:257], in1=self.tmpn[q][:, 0:256], op=ALU.add)
        for c in range(8):
            tail.append(lambda c=c: pair(c))

        def store():
            if to_out:
                self.P.dma('sync', self.yout[:, :, t0 - NCTX:t0 - NCTX + TILE], xt[:, :, 1:257], ['xt%d' % xb], ['yo%d' % i], buf='yo')
            else:
                dst = self.next_stream()
                self.P.dma('sync', dst[0][:, :, t0:t0 + TILE], xt[:, :, 1:257], ['xt%d' % xb], [dst[1] + str(i)], buf='xst%d' % xb)
        tail.append(store)
        return tail

    def residual(self, l, i, which, get_chunk, xb, to_out, hook=None, defer=False):
        t0, lo, hi, s = tile_info(i)
        Gi = 1 if which == 0 else 3
        xt = self.xt[xb]
        pend = None

        def stat(q, c):
            self.P.op('tensor', (lambda e, o=self.bank[7][:, 0:256], r=self.sq[q][:, 0:256], c=c: e.matmul(o, lhsT=self.ones_bf[:], rhs=r, start=(c == 0), stop=(c == 7))),
                      ['ones', 'sq%d' % q], ['B7'])
        for c in range(8):
            bap, bkey = get_chunk(c)
            if pend is not None:
                stat(*pend)
            self.E('vector', 'tensor_copy', [], [bkey, 'mtmp'], out=self.mtmp[:, c, :], in_=bap)
            q = self.rot('sq', 4)
            self.E('scalar', 'activation', [], [bkey, 'sq%d' % q], out=self.sq[q][:, 0:256], in_=bap, func=AF.Square)
            pend = (q, c)
            if hook is not None and c == 1:
                stat(*pend)
                pend = None
                hook()
        stat(*pend)
        tail = self.res_tail(l, i, which, xb, to_out)
        if defer:
            return tail
        for f in tail:
            f()
        return []

    def ffn(self, l, last):
        P = self.P
        self.phase()
        wup_sb = self.carve([128, NJ, 8, 256], BF16)
        wdn_sb = self.carve([128, NJ, 1024], BF16)
        fcw_sb = self.carve([128, NJ * 6], F32).rearrange("p (j a k) -> p j a k", j=NJ, a=2)
        abuf = self.carve([128, NJ, 256], BF16)
        ya = [self.carve([128, 256], F32) for i in range(2)]
        yg = [self.carve([128, 256], F32) for i in range(2)]
        sa = [self.carve([128, 256], F32) for i in range(2)]
        P.dma('sync', fcw_sb, self.fcw[l], [], ['fcw'], buf='fcw')
        for g0 in range(0, NJ, 4):
            P.dma_group('gpsimd', [(wup_sb[:, j], self.wup[l, j], 'wup%d' % j) for j in range(g0, min(NJ, g0 + 4))], 'wupg%d' % (g0 // 4))
        for g0 in range(0, NJ, 6):
            P.dma_group('gpsimd', [(wdn_sb[:, j], self.wdn[l, :, j], 'wdn%d' % j) for j in range(g0, min(NJ, g0 + 6))], 'wdng%d' % (g0 // 6))
        tiles = self.tiles if self.tiles is not None else list(range(NT))
        tiles = [i for i in tiles if not (last and i == 0)]

        def do_pre(i):
            xb = self.load_x(i, self.xsrc)
            return xb, self.prenorm(l, i, xb, 1)
        cur = do_pre(tiles[0])
        deferred = []
        for idx, i in enumerate(tiles):
            t0, lo, hi, s = tile_info(i)
            xb, hb = cur
            hT = self.hT[hb]
            for j in range(NJ):
                pb = j % 2
                banks = (1 + 2 * pb, 2 + 2 * pb)
                for ag in range(2):
                    bi = banks[ag]
                    bk = self.bank[bi]
                    bkey = 'B%d' % bi
                    self.MM(bk[:, lo:hi], [(wup_sb[:, j, kc, ag * 128:(ag + 1) * 128], hT[:, kc, lo:hi]) for kc in range(8)],
                            ['wup%d' % j, 'hT%d' % hb], [bkey])
                    y = (ya if ag == 0 else yg)[pb]
                    ykey = ('ya%d' if ag == 0 else 'yg%d') % pb
                    w = fcw_sb[:, j, ag, :]
                    self.E('scalar', 'activation', ['fcw'], [bkey, ykey], out=y[:, 0:256], in_=bk[:, 1:257], func=AF.Identity, scale=w[:, 1:2])
                    o0 = 0 if lo == 0 else 1
                    self.E('vector', 'scalar_tensor_tensor', ['fcw'], [bkey, ykey], out=y[:, o0:256], in0=bk[:, o0:256], scalar=w[:, 0:1],
                           in1=y[:, o0:256], op0=ALU.mult, op1=ALU.add)
                    o1 = 256 if hi == 258 else 255
                    self.E('vector', 'scalar_tensor_tensor', ['fcw'], [bkey, ykey], out=y[:, 0:o1], in0=bk[:, 2:2 + o1], scalar=w[:, 2:3],
                           in1=y[:, 0:o1], op0=ALU.mult, op1=ALU.add)
                self.E('scalar', 'activation', ['ya%d' % pb], ['sa%d' % pb], out=sa[pb], in_=ya[pb], func=AF.Silu)
                self.E('gpsimd', 'tensor_tensor', ['sa%d' % pb, 'yg%d' % pb], ['abuf%d' % j], out=abuf[:, j, :], in0=sa[pb], in1=yg[pb], op=ALU.mult)
                if deferred and j >= 2:
                    deferred.pop(0)()
                if j == 17 and idx + 1 < len(tiles):
                    nxb = self.load_x(tiles[idx + 1], self.xsrc)
                    self.prenorm(l, tiles[idx + 1], nxb, 1, part='A')
            while deferred:
                deferred.pop(0)()
            nxt = None
            if idx + 1 < len(tiles):
                nxt = (nxb, self.prenorm(l, tiles[idx + 1], nxb, 1, part='B'))
            NE = NJ - 2

            def dgroup(m, j0, j1):
                bi = (5, 6, 1, 2, 3, 4)[m % 6]

                def fn(e, m=m, j0=j0, j1=j1, bi=bi):
                    ins = None
                    for j in range(j0, j1):
                        ins = e.matmul(self.bank[bi][:, 0:256], lhsT=wdn_sb[:, j, m * 128:(m + 1) * 128], rhs=abuf[:, j, :], start=(j == 0), stop=(j == NJ - 1))
                    return ins
                self.P.op('tensor', fn, ['wdn%d' % j for j in range(j0, j1)] + ['abuf%d' % j for j in range(j0, j1)], ['B%d' % bi])
                return bi

            def evac(c):
                bi = (5, 6, 1, 2, 3, 4)[c % 6]
                self.E('vector', 'tensor_copy', [], ['B%d' % bi, 'mtmp'], out=self.mtmp[:, c, :], in_=self.bank[bi][:, 0:256])
                self.E('scalar', 'activation', [], ['B%d' % bi, 'sqp'], out=self.sqp[:, c, 0:256], in_=self.bank[bi][:, 0:256], func=AF.Square)
            dgroup(0, 0, NE)
            dgroup(1, 0, NE)
            dgroup(0, NE, NJ)
            dgroup(1, NE, NJ)
            for m in range(2, 6):
                dgroup(m, 0, NJ)
            evac(0)
            dgroup(6, 0, NJ)
            evac(1)
            dgroup(7, 0, NJ)
            for c in range(2, 8):
                evac(c)
            for c in range(8):
                self.P.op('tensor', (lambda e, o=self.bank[7][:, 0:256], r=self.sqp[:, c, 0:256], c=c: e.matmul(o, lhsT=self.ones_bf[:], rhs=r, start=(c == 0), stop=(c == 7))),
                          ['ones', 'sqp'], ['B7'])
            deferred = self.res_tail(l, i, 1, xb, last)
            cur = nxt
        while deferred:
            deferred.pop(0)()
        if not last:
            self.swap_stream()

    def even_mixer(self, l, last):
        P = self.P
        E = self.E
        i2 = l // 2
        self.phase()
        bank = self.bank
        cqnT = self.carve([128, 3, T], BF16)
        ckvnT = self.carve([128, 2, T], BF16)
        krT = self.carve([128, T], BF16)
        scT = self.carve([128, 4, T], BF16)
        attnT = self.carve([128, 4, T], BF16)
        wuq_sb = self.carve([128, 3, 1024], BF16)
        wukv_sb = self.carve([128, 2, 1024], BF16)
        qn_sb = self.carve([128, 3], F32)
        kvn_sb = self.carve([128, 2], F32)
        scw_sb = self.carve([128, 4, 3], F32)
        mark = self.aoff
        win_sb = self.carve([128, 8, 2240], BF16)
        rt = self.carve([128, 4, 256], F32)
        craw = self.carve([128, 3, 256], F32)
        t1 = self.carve([128, 256], F32)
        t2 = self.carve([128, 256], F32)
        gcs = self.carve([128, 258], F32)
        vb = self.carve([128, 258], F32)
        ysc = self.carve([128, 256], F32)
        rq = self.carve([128, 256], F32)
        P.dma('sync', qn_sb, self.hqn[i2], [], ['qnw'], buf='qnw')
        P.dma('sync', kvn_sb, self.hkvn[i2], [], ['kvnw'], buf='kvnw')
        P.dma('sync', scw_sb, self.hscw[i2], [], ['scw'], buf='scw')
        P.dma_group('gpsimd', [(win_sb[:, kc], self.hwin[i2, :, kc], 'win%d' % kc) for kc in range(8)], 'wing')
        P.dma('gpsimd', wuq_sb, self.hwuq[i2], [], ['wuq'], buf='wuq')
        P.dma('gpsimd', wukv_sb, self.hwukv[i2], [], ['wukv'], buf='wukv')
        WIN = ['win%d' % kc for kc in range(8)]
        def do_pre(i):
            xb = self.load_x(i, self.xsrc)
            return xb, self.prenorm(l, i, xb, 0)
        cur = do_pre(0)
        for i in range(NT):
            t0, lo, hi, s = tile_info(i)
            xb, hb = cur
            hT = self.hT[hb]
            hk = 'hT%d' % hb
            P.dma('sync', rt[64:96, :, :], self.rope[:, :, t0:t0 + 256], [], ['rt'], buf='rt')
            for (col0, nch, dst, nrm, bi) in ((0, 3, cqnT, qn_sb, 1), (384, 2, ckvnT, kvn_sb, 2)):
                bst = bank[3]
                pend = None

                def stat(q, c, n):
                    self.P.op('tensor', (lambda e, o=bst[:, 0:256], r=self.sq[q][:, 0:256], c=c, n=n: e.matmul(o, lhsT=self.ones_bf[:], rhs=r, start=(c == 0), stop=(c == n - 1))),
                              ['ones', 'sq%d' % q], ['B3'])
                for c in range(nch):
                    pb = bank[bi] if c % 2 == 0 else bank[7]
                    pk = ('B%d' % bi) if c % 2 == 0 else 'B7'
                    self.MM(pb[:, 0:256], [(win_sb[:, kc, col0 + c * 128:col0 + (c + 1) * 128], hT[:, kc, 1:257]) for kc in range(8)],
                            WIN + [hk], [pk])
                    if pend is not None:
                        stat(*pend)
                    E('vector', 'tensor_copy', [], [pk, 'craw'], out=craw[:, c, :], in_=pb[:, 0:256])
                    q = self.rot('sq', 4)
                    E('scalar', 'activation', [], [pk, 'sq%d' % q], out=self.sq[q][:, 0:256], in_=pb[:, 0:256], func=AF.Square)
                    pend = (q, c, nch)
                stat(*pend)
                self.rstd_from_bank(bst, 'B3', 0, 256, rq, 'rq', nch * 128)
                for c in range(nch):
                    E('vector', 'scalar_tensor_tensor', ['craw', 'rq', 'qnw', 'kvnw'], ['cn%d_%d' % (bi, i)], out=dst[:, c, t0:t0 + 256], in0=craw[:, c, :],
                      scalar=nrm[:, c:c + 1], in1=rq, op0=ALU.mult, op1=ALU.mult)
            if i + 1 < NT:
                cur = do_pre(i + 1)
            pb = bank[4]
            self.MM(pb[64:96, 0:256], [(win_sb[:, kc, 640:672], hT[:, kc, 1:257]) for kc in range(8)], WIN + [hk], ['B4'])
            self.MM(pb[64:96, 256:512], [(win_sb[:, kc, 2208:2240], hT[:, kc, 1:257]) for kc in range(8)], WIN + [hk], ['B4'])
            E('vector', 'tensor_tensor', ['rt'], ['B4', 't1'], out=t1[64:96, :], in0=pb[64:96, 0:256], in1=rt[64:96, 0, :], op=ALU.mult)
            E('vector', 'tensor_tensor', ['rt'], ['B4', 't2'], out=t2[64:96, :], in0=pb[64:96, 256:512], in1=rt[64:96, 1, :], op=ALU.mult)
            E('gpsimd', 'tensor_tensor', ['t1', 't2'], ['kr%d' % i], out=krT[64:96, t0:t0 + 256], in0=t1[64:96, :], in1=t2[64:96, :], op=ALU.add)
            for c in range(4):
                b5, b6 = ((5, 6), (3, 4))[c % 2]
                k5, k6 = 'B%d' % b5, 'B%d' % b6
                pgc, pxv, pgb = bank[b5], bank[b6], bank[1 + (c % 2)]
                kgb = 'B%d' % (1 + (c % 2))
                self.MM(pgc[:, lo:hi], [(win_sb[:, kc, 1184 + c * 128:1184 + (c + 1) * 128], hT[:, kc, lo:hi]) for kc in range(8)], WIN + [hk], [k5])
                self.MM(pxv[:, lo:hi], [(win_sb[:, kc, 1696 + c * 128:1696 + (c + 1) * 128], hT[:, kc, lo:hi]) for kc in range(8)], WIN + [hk], [k6])
                self.MM(pgb[:, 0:256], [(win_sb[:, kc, 672 + c * 128:672 + (c + 1) * 128], hT[:, kc, 1:257]) for kc in range(8)], WIN + [hk], [kgb])
                E('scalar', 'activation', [], [k5, 'gcs'], out=gcs[:, lo:hi], in_=pgc[:, lo:hi], func=AF.Identity)
                E('vector', 'tensor_tensor', ['gcs'], [k6, 'vb'], out=vb[:, lo:hi], in0=pxv[:, lo:hi], in1=gcs[:, lo:hi], op=ALU.mult)
                w = scw_sb[:, c, :]
                E('vector', 'tensor_scalar', ['vb', 'scw'], ['ysc'], out=ysc[:, 0:256], in0=vb[:, 1:257], scalar1=w[:, 1:2], scalar2=None, op0=ALU.mult)
                o0 = 0 if lo == 0 else 1
                E('vector', 'scalar_tensor_tensor', ['vb', 'scw'], ['ysc'], out=ysc[:, o0:256], in0=vb[:, o0:256], scalar=w[:, 0:1],
                  in1=ysc[:, o0:256], op0=ALU.mult, op1=ALU.add)
                o1 = 256 if hi == 258 else 255
                E('vector', 'scalar_tensor_tensor', ['vb', 'scw'], ['ysc'], out=ysc[:, 0:o1], in0=vb[:, 2:2 + o1], scalar=w[:, 2:3],
                  in1=ysc[:, 0:o1], op0=ALU.mult, op1=ALU.add)
                E('vector', 'tensor_tensor', ['ysc'], [kgb, 'sc%d' % i], out=scT[:, c, t0:t0 + 256], in0=pgb[:, 0:256], in1=ysc, op=ALU.mult)
        self.P.barrier()
        self.aoff = mark
        K_all = self.carve([128, 8, T], BF16)
        V_all = self.carve([128, 18, 8, 65], BF16)
        Qt = [self.carve([128, 8, 512], BF16) for _ in range(1)]
        PT = [self.carve([128, 512], BF16) for _ in range(4)]
        atok = self.carve([128, 4, 512], BF16)
        rinv = self.carve([128, 4], F32)
        rtq = self.carve([128, 2, 512], F32)
        u1 = self.carve([128, 512], F32)
        u2 = self.carve([128, 512], F32)
        E('gpsimd', 'memset', [], ['V_all'], ap=V_all, constant=1.0)
        for h in range(8):
            E('gpsimd', 'tensor_copy', [], ['K%d' % h], out=K_all[64:96, h, :], in_=krT[64:96, :])
            for n0 in range(0, T, 512):
                n1 = min(T, n0 + 512)
                bi = 1 + (h * 5 + n0 // 512) % 2
                self.MM(bank[bi][0:64, 0:n1 - n0], [(wukv_sb[:, kc, h * 128:h * 128 + 64], ckvnT[:, kc, n0:n1]) for kc in range(2)], ['wukv'], ['B%d' % bi])
                E('scalar', 'activation', [], ['B%d' % bi, 'K%d' % h], out=K_all[0:64, h, n0:n1], in_=bank[bi][0:64, 0:n1 - n0], func=AF.Identity)
        wv = wukv_sb.rearrange("p k (h x) -> p k h x", x=128)
        for ch in range(18):
            bi = 3 + ch % 2
            self.MM(bank[bi][:, :].rearrange("p (h x) -> p h x", x=64), [(ckvnT[:, kc, ch * 128:(ch + 1) * 128], wv[:, kc, :, 64:128]) for kc in range(2)], ['wukv'], ['B%d' % bi])
            E('vector', 'tensor_copy', [], ['B%d' % bi, 'V_all'], out=V_all[:, ch, :, 0:64], in_=bank[bi][:, :].rearrange("p (h x) -> p h x", x=64))
        qtiles = [(0, 256, 2)] + [(NCTX + 512 * k, 512, 18) for k in range(4)]
        for (q0, nq, nkc) in qtiles:
            qb = 0
            Q = Qt[qb]
            qk = 'Qt%d' % qb
            P.dma('sync', rtq[64:96, :, 0:nq], self.rope[:, 2:4, q0:q0 + nq], [], ['rtq'], buf='rtq')
            for h in range(8):
                pq = bank[1 + h % 2]
                kq = 'B%d' % (1 + h % 2)
                pq2 = bank[3]
                self.MM(pq[0:96, 0:nq], [(wuq_sb[:, kc, h * 96:(h + 1) * 96], cqnT[:, kc, q0:q0 + nq]) for kc in range(3)], ['wuq'], [kq])
                self.MM(pq2[64:96, 0:nq], [(wuq_sb[:, kc, 768 + h * 32:768 + (h + 1) * 32], cqnT[:, kc, q0:q0 + nq]) for kc in range(3)], ['wuq'], ['B3'])
                E('scalar', 'activation', [], [kq, qk], out=Q[0:64, h, 0:nq], in_=pq[0:64, 0:nq], func=AF.Identity, scale=MLA_SCALE)
                E('vector', 'tensor_tensor', ['rtq'], [kq, 'u1'], out=u1[64:96, 0:nq], in0=pq[64:96, 0:nq], in1=rtq[64:96, 0, 0:nq], op=ALU.mult)
                E('vector', 'tensor_tensor', ['rtq'], ['B3', 'u2'], out=u2[64:96, 0:nq], in0=pq2[64:96, 0:nq], in1=rtq[64:96, 1, 0:nq], op=ALU.mult)
                E('gpsimd', 'tensor_tensor', ['u1', 'u2'], [qk], out=Q[64:96, h, 0:nq], in0=u1[64:96, 0:nq], in1=u2[64:96, 0:nq], op=ALU.add)
            nsub = nq // 128
            for h in range(8):
                ob = bank[6 + h % 2]
                ok = 'B%d' % (6 + h % 2)
                pvq = []

                def pv_emit(pi, kc, h=h, ob=ob, ok=ok, nsub=nsub, nkc=nkc):
                    def pv(e, ob=ob, pt=PT[pi], ch=kc, h=h, nsub=nsub, first=(kc == 0), lastk=(kc == nkc - 1)):
                        ins = None
                        for sidx in range(nsub):
                            ins = e.matmul(ob[:, sidx * 65:(sidx + 1) * 65], lhsT=pt[:, sidx * 128:(sidx + 1) * 128], rhs=V_all[:, ch, h, :],
                                           start=(first and sidx == 0), stop=lastk, skip_group_check=True)
                        return ins
                    P.op('tensor', pv, ['PT%d' % pi, 'V_all'], [ok])
                for kc in range(nkc):
                    sb_i = 1 + (kc % 3)
                    ps = bank[sb_i]
                    self.MM(ps[:, 0:nq], [(K_all[0:96, h, kc * 128:(kc + 1) * 128], Q[0:96, h, 0:nq])], ['K%d' % h, qk], ['B%d' % sb_i])
                    if len(pvq) >= 2:
                        pv_emit(*pvq.pop(0))
                    pi = self.rot('PT', 4)
                    E('scalar', 'activation', [], ['B%d' % sb_i, 'PT%d' % pi], out=PT[pi][:, 0:nq], in_=ps[:, 0:nq], func=AF.Exp)
                    pvq.append((pi, kc))
                while pvq:
                    pv_emit(*pvq.pop(0))
                ov = ob[:, 0:nsub * 65].rearrange("p (s x) -> p s x", x=65)
                E('vector', 'reciprocal', [], [ok, 'rinv'], out=rinv[:, 0:nsub], in_=ov[:, :, 64])
                E('vector', 'tensor_tensor', ['rinv'], [ok, 'atok'], out=atok[:, 0:nsub, h * 64:(h + 1) * 64], in0=ov[:, :, 0:64],
                  in1=rinv[:, 0:nsub].unsqueeze(2).to_broadcast([128, nsub, 64]), op=ALU.mult)
            for sidx in range(nsub):
                pt = bank[4][:, :].bitcast(BF16)
                for fc in range(4):
                    E('tensor', 'transpose', ['atok', 'ident'], ['B4'], out=pt[:, fc * 128:(fc + 1) * 128], in_=atok[:, sidx, fc * 128:(fc + 1) * 128], identity=self.ident[:])
                E('vector', 'tensor_copy', [], ['B4', 'attnT'], out=attnT[:, :, q0 + sidx * 128:q0 + (sidx + 1) * 128],
                  in_=pt[:, 0:512].rearrange("p (f x) -> p f x", x=128))
        self.P.barrier()
        self.aoff = mark
        wout_sb = self.carve([128, 8, 1024], BF16)
        P.dma('gpsimd', wout_sb, self.hwout[i2], [], ['wout'], buf='wout')
        nxb = self.load_x(0, self.xsrc, halo=False)
        deferred = []
        obanks = (0, 1, 2, 3, 4, 5, 6)
        for i in range(NT):
            t0, lo, hi, s = tile_info(i)
            xb = nxb

            def ogroup(m, t0=t0):
                bi = obanks[m % 7]
                pairs = [(wout_sb[:, kc, m * 128:(m + 1) * 128], attnT[:, kc, t0:t0 + 256]) for kc in range(4)]
                pairs += [(wout_sb[:, 4 + kc, m * 128:(m + 1) * 128], scT[:, kc, t0:t0 + 256]) for kc in range(4)]
                self.MM(self.bank[bi][:, 0:256], pairs, ['wout'], ['B%d' % bi])

            def evac(c):
                bi = obanks[c % 7]
                self.E('vector', 'tensor_copy', [], ['B%d' % bi, 'mtmp'], out=self.mtmp[:, c, :], in_=self.bank[bi][:, 0:256])
                self.E('scalar', 'activation', [], ['B%d' % bi, 'sqp'], out=self.sqp[:, c, 0:256], in_=self.bank[bi][:, 0:256], func=AF.Square)
            for m in range(7):
                ogroup(m)
            while deferred:
                deferred.pop(0)()
            if i + 1 < NT:
                nxb = self.load_x(i + 1, self.xsrc, halo=False)
            evac(0)
            ogroup(7)
            for c in range(1, 8):
                evac(c)
            for c in range(8):
                self.P.op('tensor', (lambda e, o=self.bank[7][:, 0:256], r=self.sqp[:, c, 0:256], c=c: e.matmul(o, lhsT=self.ones_bf[:], rhs=r, start=(c == 0), stop=(c == 7))),
                          ['ones', 'sqp'], ['B7'])
            deferred = self.res_tail(l, i, 0, xb, False)
        while deferred:
            deferred.pop(0)()
        self.swap_stream()

    def mask(self, ap, key, cm, pat, cmp):
        self.E('gpsimd', 'memset', [], [key], ap=ap, constant=1.0)
        self.E('gpsimd', 'affine_select', [key], [key], out=ap, in_=ap, pattern=[[pat, 128]], compare_op=cmp, fill=0.0, base=0, channel_multiplier=cm)

    def odd_mixer(self, l, last):
        i2 = l // 2
        self.ssd_projpass(l, i2)
        for d in (0, 1):
            self.ssd_sweep(l, i2, d)
        self.ssd_out(l, i2, last)
        self.swap_stream()

    def ssd_projpass(self, l, i2):
        P = self.P
        E = self.E
        self.phase(resid=False)
        wx = self.carve([128, 8, 3136], BF16)
        cv = self.carve([128, 24, 4], F32)
        sbc = self.carve([128, 160], F32)
        negA = self.carve([128, 64], F32)
        identf = self.carve([128, 128], F32)
        xs_toks = [self.carve([128, 2, 2048], F32) for _ in range(2)]
        bcts = [self.carve([128, 3072], BF16) for _ in range(2)]
        dtcs = [self.carve([128, 256], F32) for _ in range(2)]
        yc = [self.carve([128, 256], F32) for _ in range(4)]
        sil = [self.carve([128, 256], F32) for _ in range(6)]
        dtr = self.carve([128, 2, 64], F32)
        dta = self.carve([128, 2, 64], F32)
        dte = self.carve([128, 2, 64], F32)
        P.dma_group('gpsimd', [(wx[:, kc], self.swin[i2, :, kc, 2048:5184], 'wx%d' % kc) for kc in range(8)], 'wxg')
        P.dma('sync', cv, self.scv[i2], [], ['cv'], buf='cv')
        P.dma('sync', sbc, self.sbc[i2], [], ['sbc'], buf='sbc')
        E('scalar', 'activation', ['sbc'], ['negA'], out=negA, in_=sbc[:, 0:64], func=AF.Exp)
        E('vector', 'tensor_scalar', ['negA'], ['negA'], out=negA, in0=negA, scalar1=-1.0, scalar2=None, op0=ALU.mult)
        self.mask(identf, 'identf', 1, -1, ALU.is_equal)
        WX = ['wx%d' % kc for kc in range(8)]

        def do_pre(i):
            xb = self.load_x(i, self.xsrc)
            return xb, self.prenorm(l, i, xb, 0)
        cur = do_pre(0)
        for i in range(NT):
            p = i % 2
            xb, hb = cur
            if i + 1 < NT:
                cur = do_pre(i + 1)
            b = bcts[p]
            dc = dtcs[p]
            self.ssd_proj(l, i, self.hT[hb], 'hT%d' % hb, wx, cv, sbc, negA, identf, yc, sil, dtr, dta, dte, xs_toks[p],
                          b[:, 0:1024].rearrange("p (c x) -> p c x", c=2), b[:, 1024:2048].rearrange("p (g x) -> p g x", g=4),
                          b[:, 2048:3072].rearrange("p (g x) -> p g x", g=4), dc[:, 0:128].rearrange("p (c x) -> p c x", c=2),
                          dc[:, 128:256].rearrange("p (c x) -> p c x", c=2), WX, p)
            P.dma('sync', self.sxs[i], xs_toks[p], ['xs_tok%d' % p], ['sxs%d' % i], buf='sxs%d' % p)
            P.dma('sync', self.sbt[i], bcts[p], ['B_tok%d' % p, 'BT%d' % p, 'CT%d' % p], ['sbt%d' % i], buf='sbt%d' % p)
            P.dma('sync', self.sdt[i], dtcs[p], ['dtv%d' % p, 'a_tok%d' % p], ['sdt%d' % i], buf='sdt%d' % p)

    def ssd_sweep(self, l, i2, d):
        P = self.P
        E = self.E
        bank = self.bank
        self.phase(resid=False)
        sbc = self.carve([128, 160], F32)
        negA = self.carve([128, 64], F32)
        Minc = self.carve([128, 128], F32)
        Mstr = self.carve([128, 128], F32)
        nset = 2
        xs_toks = [self.carve([128, 2, 2048], F32) for _ in range(nset)]
        bcts = [self.carve([128, 3072], BF16) for _ in range(nset)]
        dtcs = [self.carve([128, 256], F32) for _ in range(nset)]
        nbs = 2
        ed2 = [self.carve([128, 96], F32) for _ in range(nbs)]
        w22 = [self.carve([128, 32], F32) for _ in range(nbs)]
        RE2 = [[self.carve([128, 8, 128], F32) for _ in range(4)] for _ in range(nbs)]
        MT82 = [[self.carve([128, 8, 128], BF16) for _ in range(4)] for _ in range(nbs)]
        cbm2 = [self.carve([128, 4, 128], F32) for _ in range(nbs)]
        xdt2 = [[self.carve([128, 8, 64], BF16) for _ in range(4)] for _ in range(nbs)]
        xd2 = [[self.carve([128, 8, 64], BF16) for _ in range(4)] for _ in range(nbs)]
        tmpy = [self.carve([128, 512], F32) for _ in range(2)]
        ych = [self.carve([128, 2048], F32) for _ in range(2)]
        S = self.carve([128, 2048], F32)
        S_bf = self.carve([128, 2048], BF16)
        tmps = [self.carve([128, 512], F32) for _ in range(2)]
        yfb = self.carve([128, 2048], F32) if d == 1 else None
        P.dma('sync', sbc, self.sbc[i2], [], ['sbc'], buf='sbc')
        E('scalar', 'activation', ['sbc'], ['negA'], out=negA, in_=sbc[:, 0:64], func=AF.Exp)
        E('vector', 'tensor_scalar', ['negA'], ['negA'], out=negA, in0=negA, scalar1=-1.0, scalar2=None, op0=ALU.mult)
        if d == 0:
            self.mask(Minc, 'Minc', -1, 1, ALU.is_ge)
            self.mask(Mstr, 'Mstr', 1, -1, ALU.is_gt)
        else:
            self.mask(Minc, 'Minc', 1, -1, ALU.is_ge)
            self.mask(Mstr, 'Mstr', -1, 1, ALU.is_gt)
        E('gpsimd', 'memset', [], ['S%d' % g for g in range(4)], ap=S, constant=0.0)
        E('gpsimd', 'memset', [], ['S_bf%d' % g for g in range(4)], ap=S_bf, constant=0.0)
        WX = ['wx%d' % kc for kc in range(8)]
        order = list(range(NT)) if d == 0 else [0] + list(range(NT - 1, 0, -1))
        def do_pre(i):
            xb = self.load_x(i, self.xsrc)
            return xb, self.prenorm(l, i, xb, 0)

        def views(p):
            b = bcts[p]
            dc = dtcs[p]
            return (xs_toks[p], b[:, 0:1024].rearrange("p (c x) -> p c x", c=2), b[:, 1024:2048].rearrange("p (g x) -> p g x", g=4),
                    b[:, 2048:3072].rearrange("p (g x) -> p g x", g=4), dc[:, 0:128].rearrange("p (c x) -> p c x", c=2), dc[:, 128:256].rearrange("p (c x) -> p c x", c=2))

        def do_load(i, p):
            P.dma('sync', xs_toks[p], self.sxs[i], ['sxs%d' % i], ['xs_tok%d' % p], buf='lxs%d' % p)
            P.dma('sync', bcts[p], self.sbt[i], ['sbt%d' % i], ['B_tok%d' % p, 'BT%d' % p, 'CT%d' % p], buf='lbt%d' % p)
            P.dma('sync', dtcs[p], self.sdt[i], ['sdt%d' % i], ['dtv%d' % p, 'a_tok%d' % p], buf='ldt%d' % p)
        do_load(order[0], 0)
        for oi, i in enumerate(order):
            t0, lo, hi, s = tile_info(i)
            p = oi % 2
            xs_tok, B_tok, BT, CT, dtv, a_tok = views(p)
            KXS, KBTOK, KBT, KCT, KDTV, KATOK = 'xs_tok%d' % p, 'B_tok%d' % p, 'BT%d' % p, 'CT%d' % p, 'dtv%d' % p, 'a_tok%d' % p
            if oi + 1 < len(order):
                do_load(order[oi + 1], (oi + 1) % 2)
            def stage_ab(chi, ch, bs):
                    cidx = i * 2 + ch
                    c0 = ch * 128
                    a_c = a_tok[:, ch, d * 32:(d + 1) * 32]
                    dt_c = dtv[:, ch, d * 32:(d + 1) * 32]
                    E('tensor', 'matmul', [KATOK, 'Minc'], ['B0'], out=bank[0][:, 0:32], lhsT=Minc, rhs=a_c, start=True, stop=True)
                    E('tensor', 'matmul', [KATOK, 'Mstr'], ['B0'], out=bank[0][:, 32:64], lhsT=Mstr, rhs=a_c, start=True, stop=True)
                    E('tensor', 'matmul', [KATOK, 'ones'], ['B0'], out=bank[0][:, 64:96], lhsT=self.ones[:], rhs=a_c, start=True, stop=True)
                    for g in range(4):
                        E('tensor', 'matmul', [KBT, KCT], ['B4'], out=bank[4][:, g * 128:(g + 1) * 128], lhsT=BT[:, g, c0:c0 + 128], rhs=CT[:, g, c0:c0 + 128], start=True, stop=True)
                    E('scalar', 'activation', [], ['B0', 'ed%d' % bs], out=ed2[bs], in_=bank[0][:, 0:96], func=AF.Exp)
                    for g in range(4):
                        h0 = g * 8
                        if g % 2 == 0:
                            E('gpsimd', 'tensor_tensor', ['Minc', KATOK], ['RE%d_%d' % (g, bs)], out=RE2[bs][g], in0=Minc.unsqueeze(1).to_broadcast([128, 8, 128]),
                              in1=a_c[:, h0:h0 + 8].unsqueeze(2).to_broadcast([128, 8, 128]), op=ALU.mult)
                        else:
                            for hh in range(8):
                                E('scalar', 'activation', ['Minc', KATOK], ['RE%d_%d' % (g, bs)], out=RE2[bs][g][:, hh, :], in_=Minc, func=AF.Identity, scale=a_c[:, h0 + hh:h0 + hh + 1])
                    E('vector', 'tensor_tensor', ['Minc'], ['B4', 'cbm%d' % bs], out=cbm2[bs], in0=bank[4][:, :].rearrange("p (g x) -> p g x", x=128),
                      in1=Minc.unsqueeze(1).to_broadcast([128, 4, 128]), op=ALU.mult)
                    E('vector', 'tensor_tensor', ['ed%d' % bs, KDTV], ['w2%d' % bs], out=w22[bs], in0=ed2[bs][:, 32:64], in1=dt_c, op=ALU.mult)
                    for g in range(4):
                        h0 = g * 8
                        pair = (1, 2) if g % 2 == 0 else (3, 7)
                        for half in range(2):
                            bi = pair[half]
                            E('tensor', 'matmul', ['RE%d_%d' % (g, bs), 'Mstr'], ['B%d' % bi], out=bank[bi][:, :], lhsT=Mstr, rhs=RE2[bs][g][:, half * 4:half * 4 + 4, :].rearrange("p h x -> p (h x)"), start=True, stop=True)
                        for half in range(2):
                            bi = pair[half]
                            E('scalar', 'activation', [], ['B%d' % bi, 'RE%d_%d' % (g, bs)], out=RE2[bs][g][:, half * 4:half * 4 + 4, :], in_=bank[bi][:, :].rearrange("p (h x) -> p h x", x=128), func=AF.Exp)
                        E('vector', 'tensor_tensor', ['RE%d_%d' % (g, bs), 'cbm%d' % bs], ['MT8%d_%d' % (g, bs)], out=MT82[bs][g], in0=RE2[bs][g], in1=cbm2[bs][:, g, :].unsqueeze(1).to_broadcast([128, 8, 128]), op=ALU.mult)
                        xsg = xs_tok[:, ch, h0 * 64:(h0 + 8) * 64].rearrange("p (h x) -> p h x", x=64)
                        E('gpsimd', 'tensor_tensor', [KXS, KDTV], ['xdt%d_%d' % (g, bs)], out=xdt2[bs][g], in0=xsg, in1=dt_c[:, h0:h0 + 8].unsqueeze(2).to_broadcast([128, 8, 64]), op=ALU.mult)
                        E('gpsimd', 'tensor_tensor', [KXS, 'w2%d' % bs], ['xd%d_%d' % (g, bs)], out=xd2[bs][g], in0=xsg, in1=w22[bs][:, h0:h0 + 8].unsqueeze(2).to_broadcast([128, 8, 64]), op=ALU.mult)

            def stage_c(chi, ch, bs):
                    cidx = i * 2 + ch
                    c0 = ch * 128
                    a_c = a_tok[:, ch, d * 32:(d + 1) * 32]
                    dt_c = dtv[:, ch, d * 32:(d + 1) * 32]
                    yb = cidx % 2
                    ycur = ych[yb]
                    yk = 'ych%d' % yb
                    for g in range(4):
                        h0 = g * 8
                        r = g % 2
                        bd, bo, bst = ((4, 5, 6), (0, 1, 2))[r]
                        for hh in range(8):
                            E('tensor', 'matmul', ['MT8%d_%d' % (g, bs), 'xdt%d_%d' % (g, bs)], ['B%d' % bd], out=bank[bd][:, hh * 64:(hh + 1) * 64], lhsT=MT82[bs][g][:, hh, :], rhs=xdt2[bs][g][:, hh, :], start=True, stop=True)
                        E('tensor', 'matmul', [KCT, 'S_bf%d' % g], ['B%d' % bo], out=bank[bo][:, :], lhsT=CT[:, g, c0:c0 + 128], rhs=S_bf[:, g * 512:(g + 1) * 512], start=True, stop=True)
                        E('tensor', 'matmul', [KBTOK, 'xd%d_%d' % (g, bs)], ['B%d' % bst], out=bank[bst][:, :], lhsT=B_tok[:, ch, g * 128:(g + 1) * 128], rhs=xd2[bs][g].rearrange("p h x -> p (h x)"), start=True, stop=True)
                        ty = tmpy[r]
                        E('vector', 'tensor_tensor', ['ed%d' % bs], ['B%d' % bo, 'tmpy%d' % r], out=ty.rearrange("p (h x) -> p h x", x=64), in0=bank[bo][:, :].rearrange("p (h x) -> p h x", x=64),
                          in1=ed2[bs][:, h0:h0 + 8].unsqueeze(2).to_broadcast([128, 8, 64]), op=ALU.mult)
                        E('vector', 'tensor_tensor', ['tmpy%d' % r], ['B%d' % bd, yk], out=ycur[:, g * 512:(g + 1) * 512], in0=bank[bd][:, :], in1=ty, op=ALU.add)
                        E('gpsimd', 'tensor_tensor', ['S%d' % g, 'ed%d' % bs], ['tmps%d' % r], out=tmps[r].rearrange("p (h x) -> p h x", x=64), in0=S[:, g * 512:(g + 1) * 512].rearrange("p (h x) -> p h x", x=64),
                          in1=ed2[bs][:, 64 + h0:64 + h0 + 8].unsqueeze(2).to_broadcast([128, 8, 64]), op=ALU.mult)
                        E('vector', 'tensor_tensor', ['tmps%d' % r], ['B%d' % bst, 'S%d' % g], out=S[:, g * 512:(g + 1) * 512], in0=bank[bst][:, :], in1=tmps[r], op=ALU.add)
                        E('scalar', 'activation', ['S%d' % g], ['S_bf%d' % g], out=S_bf[:, g * 512:(g + 1) * 512], in_=S[:, g * 512:(g + 1) * 512], func=AF.Identity)
                    if d == 0:
                        P.dma('sync', self.ydram[cidx], ycur, [yk], ['yd%d' % cidx], buf='yst%d' % yb)
                    else:
                        E('vector', 'tensor_tensor', [KXS, 'sbc'], ['yfb'], out=yfb.rearrange("p (h x) -> p h x", x=64), in0=xs_tok[:, ch, :].rearrange("p (h x) -> p h x", x=64),
                          in1=sbc[:, 128:160].unsqueeze(2).to_broadcast([128, 32, 64]), op=ALU.mult)
                        E('vector', 'tensor_tensor', ['yfb', yk], [yk], out=ycur, in0=ycur, in1=yfb, op=ALU.add)
                        P.dma('gpsimd', self.ydram[cidx], ycur, [yk], ['yd%d' % cidx], buf='yst%d' % yb, accum_op=ALU.add)


            chs = (0, 1) if d == 0 else (1, 0)
            if nbs == 2:
                stage_ab(0, chs[0], 0)
                stage_ab(1, chs[1], 1)
                stage_c(0, chs[0], 0)
                stage_c(1, chs[1], 1)
            else:
                stage_ab(0, chs[0], 0)
                stage_c(0, chs[0], 0)
                if d == 0 and oi + 1 < len(order):
                    cur = do_pre(order[oi + 1])
                stage_ab(1, chs[1], 0)
                stage_c(1, chs[1], 0)

    def ssd_proj(self, l, i, hT, hk, wx, cv, sbc, negA, identf, yc, sil, dtr, dta, dte, xs_tok, B_tok, BT, CT, dtv, a_tok, WX, p=0):
        P = self.P
        E = self.E
        bank = self.bank
        t0, lo, hi, s = tile_info(i)
        pend = []

        def post(cch, q):
            sl = sil[q]
            tb = (3, 7, 6)[cch % 3]
            for ch in range(2):
                E('tensor', 'matmul', ['sil%d' % q, 'identf'], ['B%d' % tb], out=bank[tb][:, ch * 128:(ch + 1) * 128], lhsT=sl[:, ch * 128:(ch + 1) * 128], rhs=identf, start=True, stop=True)
            src = bank[tb][:, 0:256].rearrange("p (c x) -> p c x", x=128)
            if cch < 16:
                E('scalar', 'activation', [], ['B%d' % tb, 'xs_tok%d' % p], out=xs_tok[:, :, cch * 128:(cch + 1) * 128], in_=src, func=AF.Identity)
            else:
                g = cch - 16
                E('vector', 'tensor_copy', [], ['B%d' % tb, 'B_tok%d' % p], out=B_tok[:, :, g * 128:(g + 1) * 128], in_=src)
        spend = []

        def do_silu(cch, y, yk):
            if cch < 20:
                q = cch % 6
                E('scalar', 'activation', [yk], ['sil%d' % q], out=sil[q], in_=y, func=AF.Silu)
                if cch >= 16:
                    E('gpsimd', 'tensor_copy', ['sil%d' % q], ['BT%d' % p], out=BT[:, cch - 16, :], in_=sil[q])
                pend.append((cch, q))
            else:
                E('scalar', 'activation', [yk], ['CT%d' % p], out=CT[:, cch - 20, :], in_=y, func=AF.Silu)
        for cch in range(24):
            bi = (1, 2, 4, 5)[cch % 4]
            pb = bank[bi]
            bkey = 'B%d' % bi
            self.MM(pb[:, lo:hi], [(wx[:, kc, cch * 128:(cch + 1) * 128], hT[:, kc, lo:hi]) for kc in range(8)], WX + [hk], [bkey])
            if len(pend) >= 3:
                post(*pend.pop(0))
            q2 = cch % 4
            y = yc[q2]
            yk = 'yc%d' % q2
            w = cv[:, cch, :]
            E('scalar', 'activation', ['cv'], [bkey, yk], out=y[:, 0:256], in_=pb[:, 1:257], func=AF.Identity, scale=w[:, 1:2], bias=w[:, 3:4])
            o0 = 0 if lo == 0 else 1
            E('vector', 'scalar_tensor_tensor', ['cv'], [bkey, yk], out=y[:, o0:256], in0=pb[:, o0:256], scalar=w[:, 0:1], in1=y[:, o0:256], op0=ALU.mult, op1=ALU.add)
            o1 = 256 if hi == 258 else 255
            E('vector', 'scalar_tensor_tensor', ['cv'], [bkey, yk], out=y[:, 0:o1], in0=pb[:, 2:2 + o1], scalar=w[:, 2:3], in1=y[:, 0:o1], op0=ALU.mult, op1=ALU.add)
            if len(spend) >= 2:
                do_silu(*spend.pop(0))
            spend.append((cch, y, yk))
        while spend:
            do_silu(*spend.pop(0))
        while pend:
            post(*pend.pop(0))
        for ch in range(2):
            self.MM(bank[0][:, ch * 64:(ch + 1) * 64], [(hT[:, kc, 1 + ch * 128:1 + (ch + 1) * 128], wx[:, kc, 3072:3136]) for kc in range(8)], WX + [hk], ['B0'])
        E('vector', 'tensor_tensor', ['sbc'], ['B0', 'dtr'], out=dtr, in0=bank[0][:, 0:128].rearrange("p (c x) -> p c x", x=64),
          in1=sbc[:, 64:128].unsqueeze(1).to_broadcast([128, 2, 64]), op=ALU.add)
        E('scalar', 'activation', ['dtr'], ['dta'], out=dta, in_=dtr, func=AF.Abs)
        E('scalar', 'activation', ['dta'], ['dte'], out=dte, in_=dta, func=AF.Exp, scale=-1.0)
        E('scalar', 'activation', ['dte', 'ones'], ['dte'], out=dte, in_=dte, func=AF.Ln, bias=self.ones[:, 0:1])
        E('scalar', 'activation', ['dtr'], ['dta'], out=dta, in_=dtr, func=AF.Relu)
        E('vector', 'tensor_tensor', ['dta', 'dte'], ['dtv%d' % p], out=dtv, in0=dta, in1=dte, op=ALU.add)
        E('vector', 'tensor_tensor', ['dtv%d' % p, 'negA'], ['a_tok%d' % p], out=a_tok, in0=dtv, in1=negA.unsqueeze(1).to_broadcast([128, 2, 64]), op=ALU.mult)

    def ssd_out(self, l, i2, last):
        P = self.P
        E = self.E
        bank = self.bank
        self.phase()
        wz = self.carve([128, 8, 2048], BF16)
        wo = self.carve([128, 16, 1024], BF16)
        stg = [self.carve([128, 1024], F32) for _ in range(2)]
        nrm = self.carve([128, 16], F32)
        yb = [self.carve([128, 2048], F32) for _ in range(2)]
        sz = [self.carve([128, 512], F32) for _ in range(2)]
        gq = self.carve([128, 2048], F32)
        gn = self.carve([128, 2048], BF16)
        gT = self.carve([128, 16, 256], BF16)
        ssq = self.carve([128, 4], F32)
        rt = self.carve([128, 1], F32)
        junk = self.carve([128, 512], F32)
        P.dma_group('gpsimd', [(wz[:, kc], self.swin[i2, :, kc, 0:2048], 'wz%d' % kc) for kc in range(8)], 'wzg')
        P.dma('sync', nrm, self.snorm[i2], [], ['nrm'], buf='nrm')
        for fc in range(16):
            st = stg[fc % 2]
            sk = 'stg%d' % (fc % 2)
            P.dma('sync', st, self.swout[i2, :, fc], [], [sk], buf=sk)
            E('vector', 'tensor_scalar', [sk, 'nrm'], ['wo'], out=wo[:, fc, :], in0=st, scalar1=nrm[:, fc:fc + 1], scalar2=None, op0=ALU.mult)
        otiles = [i for i in range(NT) if not (last and i == 0)]
        xt_save = self.xt
        self.xt = self.xt + [self.carve([128, 8, 258], F32)]
        gq2 = [gq, self.carve([128, 2048], F32)]
        gn2 = [gn, self.carve([128, 2048], BF16)]
        gT2 = [gT, self.carve([128, 16, 256], BF16)]
        ssq2 = [ssq, self.carve([128, 4], F32)]
        rt2 = [rt, self.carve([128, 1], F32)]

        def do_pre(i):
            xb = self.load_x(i, self.xsrc, halo=False)
            return xb, self.prenorm(l, i, xb, 0, halo=False)

        def stage_z(i, ch, hT, hk):
            cidx = i * 2 + ch
            b = cidx % 2
            P.dma('sync', yb[b], self.ydram[cidx], ['yd%d' % cidx], ['yb%d' % b], buf='yb%d' % b)
            E('gpsimd', 'memset', [], ['ssq%d' % b], ap=ssq2[b], constant=0.0)
            pend = None

            def sq(q):
                E('scalar', 'activation', ['gq%d' % b], ['junk', 'ssq%d' % b], out=junk, in_=gq2[b][:, q * 512:(q + 1) * 512], func=AF.Square, accum_out=ssq2[b][:, q:q + 1])
            for q in range(4):
                bi = 1 + q % 2
                self.MM(bank[bi][:, :], [(hT[:, kc, 1 + ch * 128:1 + (ch + 1) * 128], wz[:, kc, q * 512:(q + 1) * 512]) for kc in range(8)], ['wz%d' % kc for kc in range(8)] + [hk], ['B%d' % bi])
                E('scalar', 'activation', [], ['B%d' % bi, 'sz%d' % (q % 2)], out=sz[q % 2], in_=bank[bi][:, :], func=AF.Silu)
                if pend is not None:
                    sq(pend)
                E('vector', 'tensor_tensor', ['sz%d' % (q % 2), 'yb%d' % b], ['gq%d' % b], out=gq2[b][:, q * 512:(q + 1) * 512], in0=yb[b][:, q * 512:(q + 1) * 512], in1=sz[q % 2], op=ALU.mult)
                pend = q
            sq(pend)
            E('vector', 'tensor_reduce', ['ssq%d' % b], ['rt%d' % b], out=rt2[b], in_=ssq2[b], axis=mybir.AxisListType.X, op=ALU.add)
            E('scalar', 'activation', ['rt%d' % b, 'epsc'], ['rt%d' % b], out=rt2[b], in_=rt2[b], func=AF.Sqrt, scale=1.0 / 2048, bias=self.epsc[:, 0:1])
            E('vector', 'reciprocal', ['rt%d' % b], ['rt%d' % b], out=rt2[b], in_=rt2[b])
            E('scalar', 'activation', ['gq%d' % b, 'rt%d' % b], ['gn%d' % b], out=gn2[b], in_=gq2[b], func=AF.Identity, scale=rt2[b][:, 0:1])

        def stage_t(i, ch, gi):
            b = (i * 2 + ch) % 2
            for fb in range(4):
                pt = bank[3 + fb % 2][:, :].bitcast(BF16)
                pk = 'B%d' % (3 + fb % 2)
                for f4 in range(4):
                    fc = fb * 4 + f4
                    E('tensor', 'transpose', ['gn%d' % b, 'ident'], [pk], out=pt[:, f4 * 128:(f4 + 1) * 128], in_=gn2[b][:, fc * 128:(fc + 1) * 128], identity=self.ident[:])
                E('vector', 'tensor_copy', [], [pk, 'gT%d' % gi], out=gT2[gi][:, fb * 4:fb * 4 + 4, ch * 128:(ch + 1) * 128], in_=pt[:, 0:512].rearrange("p (f x) -> p f x", x=128))

        def outproj(i, xb, gi):
            def get_chunk(m):
                bi = 5 + (m % 2)
                self.MM(self.bank[bi][:, 0:256], [(wo[:, fc, m * 128:(m + 1) * 128], gT2[gi][:, fc, :]) for fc in range(16)], ['wo', 'gT%d' % gi], ['B%d' % bi])
                return self.bank[bi][:, 0:256], 'B%d' % bi
            self.residual(l, i, 0, get_chunk, xb, False)
        self.nxt = 3
        cur = do_pre(otiles[0])
        prev = None
        for oi, i in enumerate(otiles):
            xb, hb = cur
            hT = self.hT[hb]
            hk = 'hT%d' % hb
            gi = oi % 2
            stage_z(i, 0, hT, hk)
            stage_z(i, 1, hT, hk)
            if oi + 1 < len(otiles):
                cur = do_pre(otiles[oi + 1])
            stage_t(i, 0, gi)
            if prev is not None:
                outproj(*prev)
            stage_t(i, 1, gi)
            prev = (i, xb, gi)
        outproj(*prev)
        self.xt = xt_save
        self.nxt = 2


def host_layout(inp):
    f = np.float32
    out = {}
    mod_w = inp['mod_w']
    out['modw'] = np.ascontiguousarray(mod_w.reshape(4, 8, 128, 12, 512).transpose(0, 3, 2, 1, 4)).astype(f)
    out['modb'] = np.ascontiguousarray(inp['mod_b'].reshape(4, 48, 128).transpose(0, 2, 1)).astype(f)
    out['normw'] = np.ascontiguousarray(inp['norm_w'].reshape(4, 4, 8, 128).transpose(0, 3, 1, 2)).astype(f)
    wu = inp['ffn_w_up'].reshape(4, 8, 128, 2, NJ, 128)
    out['wup'] = np.ascontiguousarray(wu.transpose(0, 4, 2, 1, 3, 5)).reshape(4, NJ, 128, 8, 256).astype(f)
    out['wdn'] = np.ascontiguousarray(inp['ffn_w_down'].reshape(4, NJ, 128, 1024).transpose(0, 2, 1, 3)).astype(f)
    out['fcw'] = np.ascontiguousarray(inp['ffn_conv_w'].reshape(4, 3, 2, NJ, 128).transpose(0, 4, 3, 2, 1)).astype(f)
    w_in = inp['hyb_w_in']
    krsw = np.concatenate([w_in[:, :, 656:672], w_in[:, :, 640:656]], axis=2)
    w_in2 = np.concatenate([w_in, krsw], axis=2)
    out['hwin'] = np.ascontiguousarray(w_in2.reshape(2, 8, 128, 2240).transpose(0, 2, 1, 3)).astype(f)
    wuq = inp['mla_w_uq']
    sw = []
    for h in range(8):
        sw.append(wuq[:, :, h * 96 + 80:h * 96 + 96])
        sw.append(wuq[:, :, h * 96 + 64:h * 96 + 80])
    wuq2 = np.concatenate([wuq] + sw, axis=2)
    out['hwuq'] = np.ascontiguousarray(wuq2.reshape(2, 3, 128, 1024).transpose(0, 2, 1, 3)).astype(f)
    out['hwukv'] = np.ascontiguousarray(inp['mla_w_ukv'].reshape(2, 2, 128, 1024).transpose(0, 2, 1, 3)).astype(f)
    out['hwout'] = np.ascontiguousarray(inp['hyb_w_out'].reshape(2, 8, 128, 1024).transpose(0, 2, 1, 3)).astype(f)
    out['hqn'] = np.ascontiguousarray(inp['mla_q_norm'].reshape(2, 3, 128).transpose(0, 2, 1)).astype(f)
    out['hkvn'] = np.ascontiguousarray(inp['mla_kv_norm'].reshape(2, 2, 128).transpose(0, 2, 1)).astype(f)
    out['hscw'] = np.ascontiguousarray(inp['sconv_w'].reshape(2, 3, 4, 128).transpose(0, 3, 2, 1)).astype(f)
    out['rope'] = rope_tables()
    out['swin'] = np.ascontiguousarray(inp['ssd_w_in'].reshape(2, 8, 128, 5184).transpose(0, 2, 1, 3)).astype(f)
    cw = np.concatenate([inp['ssd_conv_w'], inp['ssd_conv_b'][:, None, :]], axis=1)
    out['scv'] = np.ascontiguousarray(cw.reshape(2, 4, 24, 128).transpose(0, 3, 2, 1)).astype(f)
    row = np.concatenate([inp['ssd_a_log'].reshape(2, 64), inp['ssd_dt_bias'].reshape(2, 64), inp['ssd_d'].reshape(2, 32)], axis=1)
    out['sbc'] = np.ascontiguousarray(np.broadcast_to(row[:, None, :], (2, 128, 160))).astype(f)
    out['snorm'] = np.ascontiguousarray(inp['ssd_norm'].reshape(2, 16, 128).transpose(0, 2, 1)).astype(f)
    out['swout'] = np.ascontiguousarray(inp['ssd_w_out'].reshape(2, 16, 128, 1024).transpose(0, 2, 1, 3)).astype(f)
    return out


def rope_tables():
    t = np.arange(NLAT)
    row = (t // 64).astype(np.float32)
    col = (t % 64).astype(np.float32)
    nf = 8
    inv = (np.float32(10000.0) ** (-np.arange(nf, dtype=np.float32) / nf)).astype(np.float32)
    ang = np.concatenate([row[:, None] * inv, col[:, None] * inv], axis=-1).astype(np.float32)
    cos = np.cos(ang).astype(np.float32).T
    sin = np.sin(ang).astype(np.float32).T
    tab = np.zeros((32, 4, T), np.float32)
    tab[:, 0, :NCTX] = 1.0
    tab[:, 2, :NCTX] = MLA_SCALE
    tab[0:16, 0, NCTX:] = cos
    tab[16:32, 0, NCTX:] = cos
    tab[0:16, 1, NCTX:] = -sin
    tab[16:32, 1, NCTX:] = sin
    tab[:, 2, NCTX:] = tab[:, 0, NCTX:] * np.float32(MLA_SCALE)
    tab[:, 3, NCTX:] = tab[:, 1, NCTX:] * np.float32(MLA_SCALE)
    return tab


def core_inputs(inp, b):
    xc = np.concatenate([inp['ctx'][b], inp['x'][b]], axis=0)
    xin = np.ascontiguousarray(xc.T.reshape(8, 128, T).transpose(1, 0, 2)).astype(np.float32)
    cc = np.stack([inp['c'][b], inp['c_ctx']], axis=-1)
    cc = np.ascontiguousarray(cc.reshape(8, 128, 2).transpose(1, 0, 2)).astype(np.float32)
    return {'xin': xin, 'cc': cc}


def kernel(**inputs):
    inp = {k: np.asarray(v) for k, v in inputs.items()}
    nb = inp['x'].shape[0]
    shared = host_layout(inp)
    in_maps = []
    for b in range(nb):
        m = dict(shared)
        m.update(core_inputs(inp, b))
        in_maps.append(m)
    B = Builder([0, 1, 2, 3])
    res = run_bass_kernel_spmd(B.nc, in_maps, core_ids=list(range(nb)))
    out = np.empty((nb, NLAT, D), np.float32)
    for b in range(nb):
        y = np.asarray(res.results[b]["yout"])
        out[b] = y.transpose(1, 0, 2).reshape(D, NLAT).T
    return out
```

```python
import numpy as np
from contextlib import ExitStack
import concourse.bass as bass
import concourse.mybir as mybir
from concourse.bass_utils import run_bass_kernel_spmd

F32 = mybir.dt.float32
BF16 = mybir.dt.bfloat16
AF = mybir.ActivationFunctionType
ALU = mybir.AluOpType

ENGINES = ['tensor', 'vector', 'scalar', 'gpsimd', 'sync']
CHUNK = 16000

D = 1024
T = 2304
NCTX = 256
NLAT = 2048
TILE = 256
NT = T // TILE
EPS = 1e-6
DFF = 2816
NJ = DFF // 128
HYB_IN = 2208
MLA_SCALE = 96 ** -0.5


class Prog:
    def __init__(self, nc):
        self.nc = nc
        self.es = ExitStack()
        self.streams = {e: [] for e in ENGINES}
        self.count = {e: 0 for e in ENGINES}
        self.esems = {e: [] for e in ENGINES}
        self.dsems = {}
        self.dcount = {}
        self.lastw = {}
        self.readers = {}
        self.seen = {e: {} for e in ENGINES}
        self.nsem = 0

    def sb(self, name, shape, dtype):
        return self.es.enter_context(self.nc.sbuf_tensor(name, shape, dtype))

    def ps(self, name, shape, dtype):
        return self.es.enter_context(self.nc.psum_tensor(name, shape, dtype))

    def _newsem(self, name):
        self.nsem += 1
        return self.es.enter_context(self.nc.semaphore(name))

    def _semof(self, semkey, val):
        if semkey[0] == 'E':
            eng = semkey[1]
            ep = (val - 1) // CHUNK
            while len(self.esems[eng]) <= ep:
                self.esems[eng].append(self._newsem("s_%s_%d" % (eng, len(self.esems[eng]))))
            return self.esems[eng][ep], (val - 1) % CHUNK + 1
        return self.dsems[semkey[1]], val

    def _deps(self, eng, reads, writes):
        evs = {}

        def add(ev):
            if ev is None:
                return
            k, v = ev
            if evs.get(k, 0) < v:
                evs[k] = v
        for k in reads:
            add(self.lastw.get(k))
        for k in writes:
            add(self.lastw.get(k))
            for ev in self.readers.get(k, {}).items():
                add(ev)
        waits = []
        seen = self.seen[eng]
        for k, v in evs.items():
            if eng == 'tensor' and k == ('E', 'tensor'):
                continue
            if seen.get(k, 0) >= v:
                continue
            seen[k] = v
            waits.append(self._semof(k, v))
        return waits

    def _commit(self, ev, reads, writes):
        k, v = ev
        for r in reads:
            d = self.readers.setdefault(r, {})
            if d.get(k, 0) < v:
                d[k] = v
        for w in writes:
            self.lastw[w] = ev
            self.readers[w] = {}

    def op(self, eng, fn, reads=(), writes=()):
        waits = self._deps(eng, reads, writes)
        self.count[eng] += 1
        n = self.count[eng]
        ev = (('E', eng), n)
        sem, val = self._semof(ev[0], n)
        self.streams[eng].append((waits, fn, (sem, 1)))
        self._commit(ev, reads, writes)

    def dma(self, queue, out, in_, reads=(), writes=(), buf=None, **kw):
        if buf not in self.dsems:
            self.dsems[buf] = self._newsem("d_%s" % buf)
            self.dcount[buf] = 0
        waits = self._deps(queue, reads, writes)
        self.dcount[buf] += 16
        ev = (('D', buf), self.dcount[buf])
        sem = self.dsems[buf]
        self.streams[queue].append((waits, lambda e: e.dma_start(out=out, in_=in_, **kw), (sem, 16)))
        self._commit(ev, reads, writes)

    def dma_group(self, queue, items, buf, **kw):
        if buf not in self.dsems:
            self.dsems[buf] = self._newsem("d_%s" % buf)
            self.dcount[buf] = 0
        keys = [k for (_, _, k) in items]
        waits = self._deps(queue, [], keys)
        sem = self.dsems[buf]
        first = True
        for (out, in_, k) in items:
            self.dcount[buf] += 16
            self.streams[queue].append((waits if first else [], (lambda e, out=out, in_=in_: e.dma_start(out=out, in_=in_, **kw)), (sem, 16)))
            first = False
        ev = (('D', buf), self.dcount[buf])
        self._commit(ev, [], keys)

    def barrier(self):
        evs = [(('E', e), self.count[e]) for e in ENGINES if self.count[e] > 0]
        evs += [(('D', b), c) for b, c in self.dcount.items()]
        for eng in ENGINES:
            waits = []
            seen = self.seen[eng]
            for k, v in evs:
                if seen.get(k, 0) >= v:
                    continue
                seen[k] = v
                waits.append(self._semof(k, v))
            if waits:
                self.streams[eng].append((waits, None, None))
        self.lastw = {}
        self.readers = {}

    def finish(self, final_waits=()):
        self.barrier()
        nc = self.nc
        with nc.Block() as block:
            for eng in ENGINES:
                stream = self.streams[eng]
                if not stream:
                    continue

                def body(e, stream=stream):
                    for waits, fn, inc in stream:
                        for sem, val in waits:
                            e.wait_ge(sem, val)
                        if fn is None:
                            continue
                        ins = fn(e)
                        if inc is not None:
                            ins.then_inc(inc[0], inc[1])
                getattr(block, eng)(body)
        self.es.close()


def tile_info(i):
    t0 = i * TILE
    has_left = i >= 2
    has_right = 1 <= i <= NT - 2
    return t0, (0 if has_left else 1), (258 if has_right else 257), (1 if i == 0 else 0)


class Builder:
    ARENA = 164 * 1024

    def __init__(self, layers, do_mix=True, do_ffn=True, tiles=None, debug=False):
        self.layers = layers
        self.tiles = tiles
        nc = bass.Bass("TRN2", target_bir_lowering=False)
        self.nc = nc
        self.P = Prog(nc)
        P = self.P
        dt = nc.dram_tensor
        self.xin = dt("xin", [128, 8, T], F32, kind="ExternalInput").ap()
        self.cc = dt("cc", [128, 8, 2], F32, kind="ExternalInput").ap()
        self.modw = dt("modw", [4, 12, 128, 8, 512], F32, kind="ExternalInput").ap()
        self.modb = dt("modb", [4, 128, 48], F32, kind="ExternalInput").ap()
        self.normw = dt("normw", [4, 128, 4, 8], F32, kind="ExternalInput").ap()
        self.wup = dt("wup", [4, NJ, 128, 8, 256], F32, kind="ExternalInput").ap()
        self.wdn = dt("wdn", [4, 128, NJ, 1024], F32, kind="ExternalInput").ap()
        self.fcw = dt("fcw", [4, 128, NJ, 2, 3], F32, kind="ExternalInput").ap()
        self.hwin = dt("hwin", [2, 128, 8, 2240], F32, kind="ExternalInput").ap()
        self.hwuq = dt("hwuq", [2, 128, 3, 1024], F32, kind="ExternalInput").ap()
        self.hwukv = dt("hwukv", [2, 128, 2, 1024], F32, kind="ExternalInput").ap()
        self.hwout = dt("hwout", [2, 128, 8, 1024], F32, kind="ExternalInput").ap()
        self.hqn = dt("hqn", [2, 128, 3], F32, kind="ExternalInput").ap()
        self.hkvn = dt("hkvn", [2, 128, 2], F32, kind="ExternalInput").ap()
        self.hscw = dt("hscw", [2, 128, 4, 3], F32, kind="ExternalInput").ap()
        self.rope = dt("rope", [32, 4, T], F32, kind="ExternalInput").ap()
        self.swin = dt("swin", [2, 128, 8, 5184], F32, kind="ExternalInput").ap()
        self.scv = dt("scv", [2, 128, 24, 4], F32, kind="ExternalInput").ap()
        self.sbc = dt("sbc", [2, 128, 160], F32, kind="ExternalInput").ap()
        self.snorm = dt("snorm", [2, 128, 16], F32, kind="ExternalInput").ap()
        self.swout = dt("swout", [2, 128, 16, 1024], F32, kind="ExternalInput").ap()
        self.ydram = dt("ydram", [18, 128, 2048], F32, kind="Internal").ap()
        self.sxs = dt("sxs", [NT, 128, 2, 2048], F32, kind="Internal").ap()
        self.sbt = dt("sbt", [NT, 128, 3072], BF16, kind="Internal").ap()
        self.sdt = dt("sdt", [NT, 128, 256], F32, kind="Internal").ap()
        self.xdA = dt("xdA", [128, 8, T], F32, kind="ExternalOutput" if debug else "Internal").ap()
        self.xdB = dt("xdB", [128, 8, T], F32, kind="ExternalOutput" if debug else "Internal").ap()
        self.yout = dt("yout", [128, 8, NLAT], F32, kind="ExternalOutput").ap()
        self.bank = [P.ps("bk%d" % i, [128, 512], F32) for i in range(8)]
        self.ones = P.sb("ones", [128, 128], F32)
        self.ones_bf = P.sb("ones_bf", [128, 128], BF16)
        self.ident = P.sb("ident", [128, 128], BF16)
        self.der = P.sb("der", [128, 4, 4, 8, 2], F32)
        self.modv = P.sb("modv", [128, 4, 48, 2], F32)
        self.xt = [P.sb("xt%d" % i, [128, 8, 258], F32) for i in range(2)]
        self.hT = [P.sb("hT%d" % i, [128, 8, 258], BF16) for i in range(2)]
        self.sq = [P.sb("sq%d" % i, [128, 258], BF16) for i in range(4)]
        self.sqp = P.sb("sqp", [128, 8, 258], BF16)
        self.rstd = [P.sb("rstd%d" % i, [128, 258], F32) for i in range(2)]
        self.tmpn = [P.sb("tmpn%d" % i, [128, 258], F32) for i in range(3)]
        self.epsc = P.sb("epsc", [128, 1], F32)
        self.arena = P.sb("arena", [128, self.ARENA // 2], BF16)
        self.cnt = {}
        self.xsrc = (self.xin, 'xi')
        self.xnext = [(self.xdA, 'xa'), (self.xdB, 'xb')]
        self.final_keys = []
        self.emit_consts()
        self.prologue()
        for l in layers:
            last = (l == 3)
            if do_mix:
                if l % 2 == 0:
                    self.even_mixer(l, last)
                else:
                    self.odd_mixer(l, last)
            if do_ffn:
                self.ffn(l, last)
        P.finish()

    def phase(self, resid=True):
        self.P.barrier()
        self.aoff = 0
        if resid:
            self.mtmp = self.carve([128, 8, 256], F32)
            self.rs2 = self.carve([128, 256], F32)

    def carve(self, shape, dtype):
        n = 1
        for d in shape[1:]:
            n *= d
        nb = n * (4 if dtype == F32 else 2)
        nb = (nb + 31) // 32 * 32
        assert self.aoff + nb <= self.ARENA, ("arena overflow", self.aoff, nb)
        ap = self.arena[:, self.aoff // 2:(self.aoff + nb) // 2]
        self.aoff += nb
        if dtype == F32:
            ap = ap.bitcast(F32)
        ap = ap[:, 0:n]
        if len(shape) == 3:
            ap = ap.rearrange("p (a b) -> p a b", a=shape[1])
        elif len(shape) == 4:
            ap = ap.rearrange("p (a b c) -> p a b c", a=shape[1], b=shape[2])
        return ap

    def E(self, eng, method, reads, writes, **kw):
        self.P.op(eng, lambda e: getattr(e, method)(**kw), reads, writes)

    def MM(self, out, pairs, reads, writes, first=True, **kw):
        pairs = list(pairs)

        def fn(e):
            n = len(pairs)
            ins = None
            for i, (l, r) in enumerate(pairs):
                ins = e.matmul(out, lhsT=l, rhs=r, start=(first and i == 0), stop=(i == n - 1), **kw)
            return ins
        self.P.op('tensor', fn, reads, writes)

    def rot(self, name, n):
        i = self.cnt.get(name, 0)
        self.cnt[name] = i + 1
        return i % n

    def emit_consts(self):
        self.E('gpsimd', 'memset', [], ['ones'], ap=self.ones[:], constant=1.0)
        self.E('gpsimd', 'memset', [], ['ones'], ap=self.ones_bf[:], constant=1.0)
        self.E('gpsimd', 'memset', [], ['epsc'], ap=self.epsc[:], constant=EPS)
        self.E('gpsimd', 'memset', [], ['ident'], ap=self.ident[:], constant=1.0)
        self.E('gpsimd', 'affine_select', ['ident'], ['ident'], out=self.ident[:], in_=self.ident[:],
               pattern=[[-1, 128]], compare_op=ALU.is_equal, fill=0.0, base=0, channel_multiplier=1)

    def prologue(self):
        P = self.P
        self.phase()
        cs = self.carve([128, 8, 2], F32)
        sc = self.carve([128, 8, 2], BF16)
        NWB = 4
        wb = [self.carve([128, 8, 512], BF16) for i in range(NWB)]
        mb = self.carve([128, 4, 48], F32)
        nw = self.carve([128, 4, 4, 8], F32)
        P.dma('sync', cs, self.cc, [], ['cs'], buf='cs')
        self.E('scalar', 'activation', ['cs'], ['scs'], out=sc, in_=cs, func=AF.Silu)
        for l in self.layers:
            P.dma('sync', mb[:, l, :], self.modb[l], [], ['mbias%d' % l], buf='mbias%d' % l)
            P.dma('sync', nw[:, l], self.normw[l], [], ['nw%d' % l], buf='nw%d' % l)
        k = 0
        for l in self.layers:
            for nb in range(12):
                w = wb[k % NWB]
                wk = 'mwb%d' % (k % NWB)
                P.dma('gpsimd', w, self.modw[l, nb], [], [wk], buf=wk)
                bk = 'B%d' % (k % 2)
                for m in range(4):
                    self.MM(self.bank[k % 2][:, 2 * m:2 * m + 2],
                            [(w[:, kc, m * 128:(m + 1) * 128], sc[:, kc, :]) for kc in range(8)],
                            [wk, 'scs'], [bk])
                self.E('vector', 'tensor_tensor', ['mbias%d' % l], [bk, 'modv%d' % l],
                       out=self.modv[:, l, nb * 4:nb * 4 + 4, :],
                       in0=self.bank[k % 2][:, 0:8].rearrange("p (a b) -> p a b", b=2),
                       in1=mb[:, l, nb * 4:nb * 4 + 4].unsqueeze(2).to_broadcast([128, 4, 2]), op=ALU.add)
                k += 1
            mv = self.modv
            dk = 'der%d' % l
            rk = ['modv%d' % l, 'nw%d' % l]

            def nwb(i):
                return nw[:, l, i, :].unsqueeze(2).to_broadcast([128, 8, 2])
            self.E('vector', 'scalar_tensor_tensor', rk, [dk], out=self.der[:, l, 0], in0=mv[:, l, 8:16, :],
                   scalar=1.0, in1=nwb(0), op0=ALU.add, op1=ALU.mult)
            self.E('vector', 'tensor_tensor', rk, [dk], out=self.der[:, l, 1], in0=mv[:, l, 16:24, :], in1=nwb(1), op=ALU.mult)
            self.E('vector', 'scalar_tensor_tensor', rk, [dk], out=self.der[:, l, 2], in0=mv[:, l, 32:40, :],
                   scalar=1.0, in1=nwb(2), op0=ALU.add, op1=ALU.mult)
            self.E('vector', 'tensor_tensor', rk, [dk], out=self.der[:, l, 3], in0=mv[:, l, 40:48, :], in1=nwb(3), op=ALU.mult)

    def next_stream(self):
        return self.xnext[0]

    def swap_stream(self):
        d = self.xnext.pop(0)
        if self.xsrc[1] != 'xi':
            self.xnext.append(self.xsrc)
        self.xsrc = d

    def load_x(self, i, src, halo=True):
        t0, lo, hi, s = tile_info(i)
        if not halo:
            lo, hi = 1, 257
        b = self.rot('xt', getattr(self, 'nxt', 2))
        self.P.dma('sync', self.xt[b][:, :, lo:hi], src[0][:, :, t0 - 1 + lo:t0 - 1 + hi],
                   [src[1] + str(j) for j in (i - 1, i, i + 1) if 0 <= j < NT], ['xt%d' % b], buf='xt%d' % b)
        return b

    def rstd_from_bank(self, bk, bkey, lo, hi, out, okey, n, extra_w=()):
        self.E('scalar', 'activation', ['epsc'], [bkey, okey] + list(extra_w), out=out[:, lo:hi], in_=bk[:, lo:hi],
               func=AF.Sqrt, scale=1.0 / n, bias=self.epsc[:, 0:1])
        self.E('vector', 'reciprocal', [okey], [okey], out=out[:, lo:hi], in_=out[:, lo:hi])

    def prenorm(self, l, i, xb, which, halo=True, part=None):
        t0, lo, hi, s = tile_info(i)
        if not halo:
            lo, hi = 1, 257
        xt = self.xt[xb]
        hb = self.rot('hT', 2)
        hT = self.hT[hb]
        rb = self.rot('rstd', 2)
        rstd = self.rstd[rb]
        bk = self.bank[0]
        if part == 'A':
            for c in range(8):
                self.E('scalar', 'activation', ['xt%d' % xb], ['sqp'], out=self.sqp[:, c, lo:hi], in_=xt[:, c, lo:hi], func=AF.Square)
            return None
        for c in range(8):
            if part == 'B':
                r, rk = self.sqp[:, c, lo:hi], 'sqp'
            else:
                q = self.rot('sq', 4)
                self.E('scalar', 'activation', ['xt%d' % xb], ['sq%d' % q], out=self.sq[q][:, lo:hi], in_=xt[:, c, lo:hi], func=AF.Square)
                r, rk = self.sq[q][:, lo:hi], 'sq%d' % q
            self.P.op('tensor', (lambda e, o=bk[:, lo:hi], r=r, c=c: e.matmul(o, lhsT=self.ones_bf[:], rhs=r, start=(c == 0), stop=(c == 7))),
                      ['ones', rk], ['B0'])
        self.rstd_from_bank(bk, 'B0', lo, hi, rstd, 'rstd%d' % rb, D)
        Ai = 0 if which == 0 else 2
        Boff = 0 if which == 0 else 24
        for c in range(8):
            q = self.rot('tmpn', 3)
            self.E('vector', 'tensor_tensor', ['xt%d' % xb, 'rstd%d' % rb], ['tmpn%d' % q], out=self.tmpn[q][:, lo:hi],
                   in0=xt[:, c, lo:hi], in1=rstd[:, lo:hi], op=ALU.mult)
            self.E('scalar', 'activation', ['tmpn%d' % q, 'der%d' % l, 'modv%d' % l], ['hT%d' % hb], out=hT[:, c, lo:hi], in_=self.tmpn[q][:, lo:hi],
                   func=AF.Identity, scale=self.der[:, l, Ai, c, s:s + 1], bias=self.modv[:, l, Boff + c, s:s + 1])
        return hb

    def res_tail(self, l, i, which, xb, to_out):
        t0, lo, hi, s = tile_info(i)
        Gi = 1 if which == 0 else 3
        xt = self.xt[xb]
        tail = []
        tail.append(lambda: self.rstd_from_bank(self.bank[7], 'B7', 0, 256, self.rs2, 'rs2', D))

        def pair(c):
            q = self.rot('tmpn', 3)
            self.E('vector', 'scalar_tensor_tensor', ['mtmp', 'rs2', 'der%d' % l], ['tmpn%d' % q], out=self.tmpn[q][:, 0:256], in0=self.mtmp[:, c, :],
                   scalar=self.der[:, l, Gi, c, s:s + 1], in1=self.rs2, op0=ALU.mult, op1=ALU.mult)
            self.E('gpsimd', 'tensor_tensor', ['tmpn%d' % q, 'xt%d' % xb], ['xt%d' % xb], out=xt[:, c, 1:257], in0=xt[:, c, 1:257], in1=self.tmpn[q][:, 0:256], op=ALU.add)
        for c in range(8):
            tail.append(lambda c=c: pair(c))

        def store():
            if to_out:
                self.P.dma('sync', self.yout[:, :, t0 - NCTX:t0 - NCTX + TILE], xt[:, :, 1:257], ['xt%d' % xb], ['yo%d' % i], buf='yo')
            else:
                dst = self.next_stream()
                self.P.dma('sync', dst[0][:, :, t0:t0 + TILE], xt[:, :, 1:257], ['xt%d' % xb], [dst[1] + str(i)], buf='xst%d' % xb)
        tail.append(store)
        return tail

    def residual(self, l, i, which, get_chunk, xb, to_out, hook=None, defer=False):
        t0, lo, hi, s = tile_info(i)
        Gi = 1 if which == 0 else 3
        xt = self.xt[xb]
        pend = None

        def stat(q, c):
            self.P.op('tensor', (lambda e, o=self.bank[7][:, 0:256], r=self.sq[q][:, 0:256], c=c: e.matmul(o, lhsT=self.ones_bf[:], rhs=r, start=(c == 0), stop=(c == 7))),
                      ['ones', 'sq%d' % q], ['B7'])
        for c in range(8):
            bap, bkey = get_chunk(c)
            if pend is not None:
                stat(*pend)
            self.E('vector', 'tensor_copy', [], [bkey, 'mtmp'], out=self.mtmp[:, c, :], in_=bap)
            q = self.rot('sq', 4)
            self.E('scalar', 'activation', [], [bkey, 'sq%d' % q], out=self.sq[q][:, 0:256], in_=bap, func=AF.Square)
            pend = (q, c)
            if hook is not None and c == 1:
                stat(*pend)
                pend = None
                hook()
        stat(*pend)
        tail = self.res_tail(l, i, which, xb, to_out)
        if defer:
            return tail
        for f in tail:
            f()
        return []

    def ffn(self, l, last):
        P = self.P
        self.phase()
        wup_sb = self.carve([128, NJ, 8, 256], BF16)
        wdn_sb = self.carve([128, NJ, 1024], BF16)
        fcw_sb = self.carve([128, NJ * 6], F32).rearrange("p (j a k) -> p j a k", j=NJ, a=2)
        abuf = self.carve([128, NJ, 256], BF16)
        ya = [self.carve([128, 256], F32) for i in range(2)]
        yg = [self.carve([128, 256], F32) for i in range(2)]
        sa = [self.carve([128, 256], F32) for i in range(2)]
        P.dma('sync', fcw_sb, self.fcw[l], [], ['fcw'], buf='fcw')
        for g0 in range(0, NJ, 4):
            P.dma_group('gpsimd', [(wup_sb[:, j], self.wup[l, j], 'wup%d' % j) for j in range(g0, min(NJ, g0 + 4))], 'wupg%d' % (g0 // 4))
        for g0 in range(0, NJ, 6):
            P.dma_group('gpsimd', [(wdn_sb[:, j], self.wdn[l, :, j], 'wdn%d' % j) for j in range(g0, min(NJ, g0 + 6))], 'wdng%d' % (g0 // 6))
        tiles = self.tiles if self.tiles is not None else list(range(NT))
        tiles = [i for i in tiles if not (last and i == 0)]

        def do_pre(i):
            xb = self.load_x(i, self.xsrc)
            return xb, self.prenorm(l, i, xb, 1)
        cur = do_pre(tiles[0])
        deferred = []
        for idx, i in enumerate(tiles):
            t0, lo, hi, s = tile_info(i)
            xb, hb = cur
            hT = self.hT[hb]
            for j in range(NJ):
                pb = j % 2
                banks = (1 + 2 * pb, 2 + 2 * pb)
                for ag in range(2):
                    bi = banks[ag]
                    bk = self.bank[bi]
                    bkey = 'B%d' % bi
                    self.MM(bk[:, lo:hi], [(wup_sb[:, j, kc, ag * 128:(ag + 1) * 128], hT[:, kc, lo:hi]) for kc in range(8)],
                            ['wup%d' % j, 'hT%d' % hb], [bkey])
                    y = (ya if ag == 0 else yg)[pb]
                    ykey = ('ya%d' if ag == 0 else 'yg%d') % pb
                    w = fcw_sb[:, j, ag, :]
                    self.E('scalar', 'activation', ['fcw'], [bkey, ykey], out=y[:, 0:256], in_=bk[:, 1:257], func=AF.Identity, scale=w[:, 1:2])
                    o0 = 0 if lo == 0 else 1
                    self.E('vector', 'scalar_tensor_tensor', ['fcw'], [bkey, ykey], out=y[:, o0:256], in0=bk[:, o0:256], scalar=w[:, 0:1],
                           in1=y[:, o0:256], op0=ALU.mult, op1=ALU.add)
                    o1 = 256 if hi == 258 else 255
                    self.E('vector', 'scalar_tensor_tensor', ['fcw'], [bkey, ykey], out=y[:, 0:o1], in0=bk[:, 2:2 + o1], scalar=w[:, 2:3],
                           in1=y[:, 0:o1], op0=ALU.mult, op1=ALU.add)
                self.E('scalar', 'activation', ['ya%d' % pb], ['sa%d' % pb], out=sa[pb], in_=ya[pb], func=AF.Silu)
                self.E('gpsimd', 'tensor_tensor', ['sa%d' % pb, 'yg%d' % pb], ['abuf%d' % j], out=abuf[:, j, :], in0=sa[pb], in1=yg[pb], op=ALU.mult)
                if deferred and j >= 2:
                    deferred.pop(0)()
                if j == 17 and idx + 1 < len(tiles):
                    nxb = self.load_x(tiles[idx + 1], self.xsrc)
                    self.prenorm(l, tiles[idx + 1], nxb, 1, part='A')
            while deferred:
                deferred.pop(0)()
            nxt = None
            if idx + 1 < len(tiles):
                nxt = (nxb, self.prenorm(l, tiles[idx + 1], nxb, 1, part='B'))
            NE = NJ - 2

            def dgroup(m, j0, j1):
                bi = (5, 6, 1, 2, 3, 4)[m % 6]

                def fn(e, m=m, j0=j0, j1=j1, bi=bi):
                    ins = None
                    for j in range(j0, j1):
                        ins = e.matmul(self.bank[bi][:, 0:256], lhsT=wdn_sb[:, j, m * 128:(m + 1) * 128], rhs=abuf[:, j, :], start=(j == 0), stop=(j == NJ - 1))
                    return ins
                self.P.op('tensor', fn, ['wdn%d' % j for j in range(j0, j1)] + ['abuf%d' % j for j in range(j0, j1)], ['B%d' % bi])
                return bi

            def evac(c):
                bi = (5, 6, 1, 2, 3, 4)[c % 6]
                self.E('vector', 'tensor_copy', [], ['B%d' % bi, 'mtmp'], out=self.mtmp[:, c, :], in_=self.bank[bi][:, 0:256])
                self.E('scalar', 'activation', [], ['B%d' % bi, 'sqp'], out=self.sqp[:, c, 0:256], in_=self.bank[bi][:, 0:256], func=AF.Square)
            dgroup(0, 0, NE)
            dgroup(1, 0, NE)
            dgroup(0, NE, NJ)
            dgroup(1, NE, NJ)
            for m in range(2, 6):
                dgroup(m, 0, NJ)
            evac(0)
            dgroup(6, 0, NJ)
            evac(1)
            dgroup(7, 0, NJ)
            for c in range(2, 8):
                evac(c)
            for c in range(8):
                self.P.op('tensor', (lambda e, o=self.bank[7][:, 0:256], r=self.sqp[:, c, 0:256], c=c: e.matmul(o, lhsT=self.ones_bf[:], rhs=r, start=(c == 0), stop=(c == 7))),
                          ['ones', 'sqp'], ['B7'])
            deferred = self.res_tail(l, i, 1, xb, last)
            cur = nxt
        while deferred:
            deferred.pop(0)()
        if not last:
            self.swap_stream()

    def even_mixer(self, l, last):
        P = self.P
        E = self.E
        i2 = l // 2
        self.phase()
        bank = self.bank
        cqnT = self.carve([128, 3, T], BF16)
        ckvnT = self.carve([128, 2, T], BF16)
        krT = self.carve([128, T], BF16)
        scT = self.carve([128, 4, T], BF16)
        attnT = self.carve([128, 4, T], BF16)
        wuq_sb = self.carve([128, 3, 1024], BF16)
        wukv_sb = self.carve([128, 2, 1024], BF16)
        qn_sb = self.carve([128, 3], F32)
        kvn_sb = self.carve([128, 2], F32)
        scw_sb = self.carve([128, 4, 3], F32)
        mark = self.aoff
        win_sb = self.carve([128, 8, 2240], BF16)
        rt = self.carve([128, 4, 256], F32)
        craw = self.carve([128, 3, 256], F32)
        t1 = self.carve([128, 256], F32)
        t2 = self.carve([128, 256], F32)
        gcs = self.carve([128, 258], F32)
        vb = self.carve([128, 258], F32)
        ysc = self.carve([128, 256], F32)
        rq = self.carve([128, 256], F32)
        P.dma('sync', qn_sb, self.hqn[i2], [], ['qnw'], buf='qnw')
        P.dma('sync', kvn_sb, self.hkvn[i2], [], ['kvnw'], buf='kvnw')
        P.dma('sync', scw_sb, self.hscw[i2], [], ['scw'], buf='scw')
        P.dma_group('gpsimd', [(win_sb[:, kc], self.hwin[i2, :, kc], 'win%d' % kc) for kc in range(8)], 'wing')
        P.dma('gpsimd', wuq_sb, self.hwuq[i2], [], ['wuq'], buf='wuq')
        P.dma('gpsimd', wukv_sb, self.hwukv[i2], [], ['wukv'], buf='wukv')
        WIN = ['win%d' % kc for kc in range(8)]
        def do_pre(i):
            xb = self.load_x(i, self.xsrc)
            return xb, self.prenorm(l, i, xb, 0)
        cur = do_pre(0)
        for i in range(NT):
            t0, lo, hi, s = tile_info(i)
            xb, hb = cur
            hT = self.hT[hb]
            hk = 'hT%d' % hb
            P.dma('sync', rt[64:96, :, :], self.rope[:, :, t0:t0 + 256], [], ['rt'], buf='rt')
            for (col0, nch, dst, nrm, bi) in ((0, 3, cqnT, qn_sb, 1), (384, 2, ckvnT, kvn_sb, 2)):
                bst = bank[3]
                pend = None

                def stat(q, c, n):
                    self.P.op('tensor', (lambda e, o=bst[:, 0:256], r=self.sq[q][:, 0:256], c=c, n=n: e.matmul(o, lhsT=self.ones_bf[:], rhs=r, start=(c == 0), stop=(c == n - 1))),
                              ['ones', 'sq%d' % q], ['B3'])
                for c in range(nch):
                    pb = bank[bi] if c % 2 == 0 else bank[7]
                    pk = ('B%d' % bi) if c % 2 == 0 else 'B7'
                    self.MM(pb[:, 0:256], [(win_sb[:, kc, col0 + c * 128:col0 + (c + 1) * 128], hT[:, kc, 1:257]) for kc in range(8)],
                            WIN + [hk], [pk])
                    if pend is not None:
                        stat(*pend)
                    E('vector', 'tensor_copy', [], [pk, 'craw'], out=craw[:, c, :], in_=pb[:, 0:256])
                    q = self.rot('sq', 4)
                    E('scalar', 'activation', [], [pk, 'sq%d' % q], out=self.sq[q][:, 0:256], in_=pb[:, 0:256], func=AF.Square)
                    pend = (q, c, nch)
                stat(*pend)
                self.rstd_from_bank(bst, 'B3', 0, 256, rq, 'rq', nch * 128)
                for c in range(nch):
                    E('vector', 'scalar_tensor_tensor', ['craw', 'rq', 'qnw', 'kvnw'], ['cn%d_%d' % (bi, i)], out=dst[:, c, t0:t0 + 256], in0=craw[:, c, :],
                      scalar=nrm[:, c:c + 1], in1=rq, op0=ALU.mult, op1=ALU.mult)
            if i + 1 < NT:
                cur = do_pre(i + 1)
            pb = bank[4]
            self.MM(pb[64:96, 0:256], [(win_sb[:, kc, 640:672], hT[:, kc, 1:257]) for kc in range(8)], WIN + [hk], ['B4'])
            self.MM(pb[64:96, 256:512], [(win_sb[:, kc, 2208:2240], hT[:, kc, 1:257]) for kc in range(8)], WIN + [hk], ['B4'])
            E('vector', 'tensor_tensor', ['rt'], ['B4', 't1'], out=t1[64:96, :], in0=pb[64:96, 0:256], in1=rt[64:96, 0, :], op=ALU.mult)
            E('vector', 'tensor_tensor', ['rt'], ['B4', 't2'], out=t2[64:96, :], in0=pb[64:96, 256:512], in1=rt[64:96, 1, :], op=ALU.mult)
            E('gpsimd', 'tensor_tensor', ['t1', 't2'], ['kr%d' % i], out=krT[64:96, t0:t0 + 256], in0=t1[64:96, :], in1=t2[64:96, :], op=ALU.add)
            for c in range(4):
                b5, b6 = ((5, 6), (3, 4))[c % 2]
                k5, k6 = 'B%d' % b5, 'B%d' % b6
                pgc, pxv, pgb = bank[b5], bank[b6], bank[1 + (c % 2)]
                kgb = 'B%d' % (1 + (c % 2))
                self.MM(pgc[:, lo:hi], [(win_sb[:, kc, 1184 + c * 128:1184 + (c + 1) * 128], hT[:, kc, lo:hi]) for kc in range(8)], WIN + [hk], [k5])
                self.MM(pxv[:, lo:hi], [(win_sb[:, kc, 1696 + c * 128:1696 + (c + 1) * 128], hT[:, kc, lo:hi]) for kc in range(8)], WIN + [hk], [k6])
                self.MM(pgb[:, 0:256], [(win_sb[:, kc, 672 + c * 128:672 + (c + 1) * 128], hT[:, kc, 1:257]) for kc in range(8)], WIN + [hk], [kgb])
                E('scalar', 'activation', [], [k5, 'gcs'], out=gcs[:, lo:hi], in_=pgc[:, lo:hi], func=AF.Identity)
                E('vector', 'tensor_tensor', ['gcs'], [k6, 'vb'], out=vb[:, lo:hi], in0=pxv[:, lo:hi], in1=gcs[:, lo:hi], op=ALU.mult)
                w = scw_sb[:, c, :]
                E('vector', 'tensor_scalar', ['vb', 'scw'], ['ysc'], out=ysc[:, 0:256], in0=vb[:, 1:257], scalar1=w[:, 1:2], scalar2=None, op0=ALU.mult)
                o0 = 0 if lo == 0 else 1
                E('vector', 'scalar_tensor_tensor', ['vb', 'scw'], ['ysc'], out=ysc[:, o0:256], in0=vb[:, o0:256], scalar=w[:, 0:1],
                  in1=ysc[:, o0:256], op0=ALU.mult, op1=ALU.add)
                o1 = 256 if hi == 258 else 255
                E('vector', 'scalar_tensor_tensor', ['vb', 'scw'], ['ysc'], out=ysc[:, 0:o1], in0=vb[:, 2:2 + o1], scalar=w[:, 2:3],
                  in1=ysc[:, 0:o1], op0=ALU.mult, op1=ALU.add)
                E('vector', 'tensor_tensor', ['ysc'], [kgb, 'sc%d' % i], out=scT[:, c, t0:t0 + 256], in0=pgb[:, 0:256], in1=ysc, op=ALU.mult)
        self.P.barrier()
        self.aoff = mark
        K_all = self.carve([128, 8, T], BF16)
        V_all = self.carve([128, 18, 8, 65], BF16)
        Qt = [self.carve([128, 8, 512], BF16) for _ in range(1)]
        PT = [self.carve([128, 512], BF16) for _ in range(4)]
        atok = self.carve([128, 4, 512], BF16)
        rinv = self.carve([128, 4], F32)
        rtq = self.carve([128, 2, 512], F32)
        u1 = self.carve([128, 512], F32)
        u2 = self.carve([128, 512], F32)
        E('gpsimd', 'memset', [], ['V_all'], ap=V_all, constant=1.0)
        for h in range(8):
            E('gpsimd', 'tensor_copy', [], ['K%d' % h], out=K_all[64:96, h, :], in_=krT[64:96, :])
            for n0 in range(0, T, 512):
                n1 = min(T, n0 + 512)
                bi = 1 + (h * 5 + n0 // 512) % 2
                self.MM(bank[bi][0:64, 0:n1 - n0], [(wukv_sb[:, kc, h * 128:h * 128 + 64], ckvnT[:, kc, n0:n1]) for kc in range(2)], ['wukv'], ['B%d' % bi])
                E('scalar', 'activation', [], ['B%d' % bi, 'K%d' % h], out=K_all[0:64, h, n0:n1], in_=bank[bi][0:64, 0:n1 - n0], func=AF.Identity)
        wv = wukv_sb.rearrange("p k (h x) -> p k h x", x=128)
        for ch in range(18):
            bi = 3 + ch % 2
            self.MM(bank[bi][:, :].rearrange("p (h x) -> p h x", x=64), [(ckvnT[:, kc, ch * 128:(ch + 1) * 128], wv[:, kc, :, 64:128]) for kc in range(2)], ['wukv'], ['B%d' % bi])
            E('vector', 'tensor_copy', [], ['B%d' % bi, 'V_all'], out=V_all[:, ch, :, 0:64], in_=bank[bi][:, :].rearrange("p (h x) -> p h x", x=64))
        qtiles = [(0, 256, 2)] + [(NCTX + 512 * k, 512, 18) for k in range(4)]
        for (q0, nq, nkc) in qtiles:
            qb = 0
            Q = Qt[qb]
            qk = 'Qt%d' % qb
            P.dma('sync', rtq[64:96, :, 0:nq], self.rope[:, 2:4, q0:q0 + nq], [], ['rtq'], buf='rtq')
            for h in range(8):
                pq = bank[1 + h % 2]
                kq = 'B%d' % (1 + h % 2)
                pq2 = bank[3]
                self.MM(pq[0:96, 0:nq], [(wuq_sb[:, kc, h * 96:(h + 1) * 96], cqnT[:, kc, q0:q0 + nq]) for kc in range(3)], ['wuq'], [kq])
                self.MM(pq2[64:96, 0:nq], [(wuq_sb[:, kc, 768 + h * 32:768 + (h + 1) * 32], cqnT[:, kc, q0:q0 + nq]) for kc in range(3)], ['wuq'], ['B3'])
                E('scalar', 'activation', [], [kq, qk], out=Q[0:64, h, 0:nq], in_=pq[0:64, 0:nq], func=AF.Identity, scale=MLA_SCALE)
                E('vector', 'tensor_tensor', ['rtq'], [kq, 'u1'], out=u1[64:96, 0:nq], in0=pq[64:96, 0:nq], in1=rtq[64:96, 0, 0:nq], op=ALU.mult)
                E('vector', 'tensor_tensor', ['rtq'], ['B3', 'u2'], out=u2[64:96, 0:nq], in0=pq2[64:96, 0:nq], in1=rtq[64:96, 1, 0:nq], op=ALU.mult)
                E('gpsimd', 'tensor_tensor', ['u1', 'u2'], [qk], out=Q[64:96, h, 0:nq], in0=u1[64:96, 0:nq], in1=u2[64:96, 0:nq], op=ALU.add)
            nsub = nq // 128
            for h in range(8):
                ob = bank[6 + h % 2]
                ok = 'B%d' % (6 + h % 2)
                pvq = []

                def pv_emit(pi, kc, h=h, ob=ob, ok=ok, nsub=nsub, nkc=nkc):
                    def pv(e, ob=ob, pt=PT[pi], ch=kc, h=h, nsub=nsub, first=(kc == 0), lastk=(kc == nkc - 1)):
                        ins = None
                        for sidx in range(nsub):
                            ins = e.matmul(ob[:, sidx * 65:(sidx + 1) * 65], lhsT=pt[:, sidx * 128:(sidx + 1) * 128], rhs=V_all[:, ch, h, :],
                                           start=(first and sidx == 0), stop=lastk, skip_group_check=True)
                        return ins
                    P.op('tensor', pv, ['PT%d' % pi, 'V_all'], [ok])
                for kc in range(nkc):
                    sb_i = 1 + (kc % 3)
                    ps = bank[sb_i]
                    self.MM(ps[:, 0:nq], [(K_all[0:96, h, kc * 128:(kc + 1) * 128], Q[0:96, h, 0:nq])], ['K%d' % h, qk], ['B%d' % sb_i])
                    if len(pvq) >= 2:
                        pv_emit(*pvq.pop(0))
                    pi = self.rot('PT', 4)
                    E('scalar', 'activation', [], ['B%d' % sb_i, 'PT%d' % pi], out=PT[pi][:, 0:nq], in_=ps[:, 0:nq], func=AF.Exp)
                    pvq.append((pi, kc))
                while pvq:
                    pv_emit(*pvq.pop(0))
                ov = ob[:, 0:nsub * 65].rearrange("p (s x) -> p s x", x=65)
                E('vector', 'reciprocal', [], [ok, 'rinv'], out=rinv[:, 0:nsub], in_=ov[:, :, 64])
                E('vector', 'tensor_tensor', ['rinv'], [ok, 'atok'], out=atok[:, 0:nsub, h * 64:(h + 1) * 64], in0=ov[:, :, 0:64],
                  in1=rinv[:, 0:nsub].unsqueeze(2).to_broadcast([128, nsub, 64]), op=ALU.mult)
            for sidx in range(nsub):
                pt = bank[4][:, :].bitcast(BF16)
                for fc in range(4):
                    E('tensor', 'transpose', ['atok', 'ident'], ['B4'], out=pt[:, fc * 128:(fc + 1) * 128], in_=atok[:, sidx, fc * 128:(fc + 1) * 128], identity=self.ident[:])
                E('vector', 'tensor_copy', [], ['B4', 'attnT'], out=attnT[:, :, q0 + sidx * 128:q0 + (sidx + 1) * 128],
                  in_=pt[:, 0:512].rearrange("p (f x) -> p f x", x=128))
        self.P.barrier()
        self.aoff = mark
        wout_sb = self.carve([128, 8, 1024], BF16)
        P.dma('gpsimd', wout_sb, self.hwout[i2], [], ['wout'], buf='wout')
        nxb = self.load_x(0, self.xsrc, halo=False)
        deferred = []
        obanks = (0, 1, 2, 3, 4, 5, 6)
        for i in range(NT):
            t0, lo, hi, s = tile_info(i)
            xb = nxb

            def ogroup(m, t0=t0):
                bi = obanks[m % 7]
                pairs = [(wout_sb[:, kc, m * 128:(m + 1) * 128], attnT[:, kc, t0:t0 + 256]) for kc in range(4)]
                pairs += [(wout_sb[:, 4 + kc, m * 128:(m + 1) * 128], scT[:, kc, t0:t0 + 256]) for kc in range(4)]
                self.MM(self.bank[bi][:, 0:256], pairs, ['wout'], ['B%d' % bi])

            def evac(c):
                bi = obanks[c % 7]
                self.E('vector', 'tensor_copy', [], ['B%d' % bi, 'mtmp'], out=self.mtmp[:, c, :], in_=self.bank[bi][:, 0:256])
                self.E('scalar', 'activation', [], ['B%d' % bi, 'sqp'], out=self.sqp[:, c, 0:256], in_=self.bank[bi][:, 0:256], func=AF.Square)
            for m in range(7):
                ogroup(m)
            while deferred:
                deferred.pop(0)()
            if i + 1 < NT:
                nxb = self.load_x(i + 1, self.xsrc, halo=False)
            evac(0)
            ogroup(7)
            for c in range(1, 8):
                evac(c)
            for c in range(8):
                self.P.op('tensor', (lambda e, o=self.bank[7][:, 0:256], r=self.sqp[:, c, 0:256], c=c: e.matmul(o, lhsT=self.ones_bf[:], rhs=r, start=(c == 0), stop=(c == 7))),
                          ['ones', 'sqp'], ['B7'])
            deferred = self.res_tail(l, i, 0, xb, False)
        while deferred:
            deferred.pop(0)()
        self.swap_stream()

    def mask(self, ap, key, cm, pat, cmp):
        self.E('gpsimd', 'memset', [], [key], ap=ap, constant=1.0)
        self.E('gpsimd', 'affine_select', [key], [key], out=ap, in_=ap, pattern=[[pat, 128]], compare_op=cmp, fill=0.0, base=0, channel_multiplier=cm)

    def odd_mixer(self, l, last):
        i2 = l // 2
        self.ssd_projpass(l, i2)
        for d in (0, 1):
            self.ssd_sweep(l, i2, d)
        self.ssd_out(l, i2, last)
        self.swap_stream()

    def ssd_projpass(self, l, i2):
        P = self.P
        E = self.E
        self.phase(resid=False)
        wx = self.carve([128, 8, 3136], BF16)
        cv = self.carve([128, 24, 4], F32)
        sbc = self.carve([128, 160], F32)
        negA = self.carve([128, 64], F32)
        identf = self.carve([128, 128], F32)
        xs_toks = [self.carve([128, 2, 2048], F32) for _ in range(2)]
        bcts = [self.carve([128, 3072], BF16) for _ in range(2)]
        dtcs = [self.carve([128, 256], F32) for _ in range(2)]
        yc = [self.carve([128, 256], F32) for _ in range(4)]
        sil = [self.carve([128, 256], F32) for _ in range(6)]
        dtr = self.carve([128, 2, 64], F32)
        dta = self.carve([128, 2, 64], F32)
        dte = self.carve([128, 2, 64], F32)
        P.dma_group('gpsimd', [(wx[:, kc], self.swin[i2, :, kc, 2048:5184], 'wx%d' % kc) for kc in range(8)], 'wxg')
        P.dma('sync', cv, self.scv[i2], [], ['cv'], buf='cv')
        P.dma('sync', sbc, self.sbc[i2], [], ['sbc'], buf='sbc')
        E('scalar', 'activation', ['sbc'], ['negA'], out=negA, in_=sbc[:, 0:64], func=AF.Exp)
        E('vector', 'tensor_scalar', ['negA'], ['negA'], out=negA, in0=negA, scalar1=-1.0, scalar2=None, op0=ALU.mult)
        self.mask(identf, 'identf', 1, -1, ALU.is_equal)
        WX = ['wx%d' % kc for kc in range(8)]

        def do_pre(i):
            xb = self.load_x(i, self.xsrc)
            return xb, self.prenorm(l, i, xb, 0)
        cur = do_pre(0)
        for i in range(NT):
            p = i % 2
            xb, hb = cur
            if i + 1 < NT:
                cur = do_pre(i + 1)
            b = bcts[p]
            dc = dtcs[p]
            self.ssd_proj(l, i, self.hT[hb], 'hT%d' % hb, wx, cv, sbc, negA, identf, yc, sil, dtr, dta, dte, xs_toks[p],
                          b[:, 0:1024].rearrange("p (c x) -> p c x", c=2), b[:, 1024:2048].rearrange("p (g x) -> p g x", g=4),
                          b[:, 2048:3072].rearrange("p (g x) -> p g x", g=4), dc[:, 0:128].rearrange("p (c x) -> p c x", c=2),
                          dc[:, 128:256].rearrange("p (c x) -> p c x", c=2), WX, p)
            P.dma('sync', self.sxs[i], xs_toks[p], ['xs_tok%d' % p], ['sxs%d' % i], buf='sxs%d' % p)
            P.dma('sync', self.sbt[i], bcts[p], ['B_tok%d' % p, 'BT%d' % p, 'CT%d' % p], ['sbt%d' % i], buf='sbt%d' % p)
            P.dma('sync', self.sdt[i], dtcs[p], ['dtv%d' % p, 'a_tok%d' % p], ['sdt%d' % i], buf='sdt%d' % p)

    def ssd_sweep(self, l, i2, d):
        P = self.P
        E = self.E
        bank = self.bank
        self.phase(resid=False)
        sbc = self.carve([128, 160], F32)
        negA = self.carve([128, 64], F32)
        Minc = self.carve([128, 128], F32)
        Mstr = self.carve([128, 128], F32)
        nset = 2
        xs_toks = [self.carve([128, 2, 2048], F32) for _ in range(nset)]
        bcts = [self.carve([128, 3072], BF16) for _ in range(nset)]
        dtcs = [self.carve([128, 256], F32) for _ in range(nset)]
        nbs = 2
        ed2 = [self.carve([128, 96], F32) for _ in range(nbs)]
        w22 = [self.carve([128, 32], F32) for _ in range(nbs)]
        RE2 = [[self.carve([128, 8, 128], F32) for _ in range(4)] for _ in range(nbs)]
        MT82 = [[self.carve([128, 8, 128], BF16) for _ in range(4)] for _ in range(nbs)]
        cbm2 = [self.carve([128, 4, 128], F32) for _ in range(nbs)]
        xdt2 = [[self.carve([128, 8, 64], BF16) for _ in range(4)] for _ in range(nbs)]
        xd2 = [[self.carve([128, 8, 64], BF16) for _ in range(4)] for _ in range(nbs)]
        tmpy = [self.carve([128, 512], F32) for _ in range(2)]
        ych = [self.carve([128, 2048], F32) for _ in range(2)]
        S = self.carve([128, 2048], F32)
        S_bf = self.carve([128, 2048], BF16)
        tmps = [self.carve([128, 512], F32) for _ in range(2)]
        yfb = self.carve([128, 2048], F32) if d == 1 else None
        P.dma('sync', sbc, self.sbc[i2], [], ['sbc'], buf='sbc')
        E('scalar', 'activation', ['sbc'], ['negA'], out=negA, in_=sbc[:, 0:64], func=AF.Exp)
        E('vector', 'tensor_scalar', ['negA'], ['negA'], out=negA, in0=negA, scalar1=-1.0, scalar2=None, op0=ALU.mult)
        if d == 0:
            self.mask(Minc, 'Minc', -1, 1, ALU.is_ge)
            self.mask(Mstr, 'Mstr', 1, -1, ALU.is_gt)
        else:
            self.mask(Minc, 'Minc', 1, -1, ALU.is_ge)
            self.mask(Mstr, 'Mstr', -1, 1, ALU.is_gt)
        E('gpsimd', 'memset', [], ['S%d' % g for g in range(4)], ap=S, constant=0.0)
        E('gpsimd', 'memset', [], ['S_bf%d' % g for g in range(4)], ap=S_bf, constant=0.0)
        WX = ['wx%d' % kc for kc in range(8)]
        order = list(range(NT)) if d == 0 else [0] + list(range(NT - 1, 0, -1))
        def do_pre(i):
            xb = self.load_x(i, self.xsrc)
            return xb, self.prenorm(l, i, xb, 0)

        def views(p):
            b = bcts[p]
            dc = dtcs[p]
            return (xs_toks[p], b[:, 0:1024].rearrange("p (c x) -> p c x", c=2), b[:, 1024:2048].rearrange("p (g x) -> p g x", g=4),
                    b[:, 2048:3072].rearrange("p (g x) -> p g x", g=4), dc[:, 0:128].rearrange("p (c x) -> p c x", c=2), dc[:, 128:256].rearrange("p (c x) -> p c x", c=2))

        def do_load(i, p):
            P.dma('sync', xs_toks[p], self.sxs[i], ['sxs%d' % i], ['xs_tok%d' % p], buf='lxs%d' % p)
            P.dma('sync', bcts[p], self.sbt[i], ['sbt%d' % i], ['B_tok%d' % p, 'BT%d' % p, 'CT%d' % p], buf='lbt%d' % p)
            P.dma('sync', dtcs[p], self.sdt[i], ['sdt%d' % i], ['dtv%d' % p, 'a_tok%d' % p], buf='ldt%d' % p)
        do_load(order[0], 0)
        for oi, i in enumerate(order):
            t0, lo, hi, s = tile_info(i)
            p = oi % 2
            xs_tok, B_tok, BT, CT, dtv, a_tok = views(p)
            KXS, KBTOK, KBT, KCT, KDTV, KATOK = 'xs_tok%d' % p, 'B_tok%d' % p, 'BT%d' % p, 'CT%d' % p, 'dtv%d' % p, 'a_tok%d' % p
            if oi + 1 < len(order):
                do_load(order[oi + 1], (oi + 1) % 2)
            def stage_ab(chi, ch, bs):
                    cidx = i * 2 + ch
                    c0 = ch * 128
                    a_c = a_tok[:, ch, d * 32:(d + 1) * 32]
                    dt_c = dtv[:, ch, d * 32:(d + 1) * 32]
                    E('tensor', 'matmul', [KATOK, 'Minc'], ['B0'], out=bank[0][:, 0:32], lhsT=Minc, rhs=a_c, start=True, stop=True)
                    E('tensor', 'matmul', [KATOK, 'Mstr'], ['B0'], out=bank[0][:, 32:64], lhsT=Mstr, rhs=a_c, start=True, stop=True)
                    E('tensor', 'matmul', [KATOK, 'ones'], ['B0'], out=bank[0][:, 64:96], lhsT=self.ones[:], rhs=a_c, start=True, stop=True)
                    for g in range(4):
                        E('tensor', 'matmul', [KBT, KCT], ['B4'], out=bank[4][:, g * 128:(g + 1) * 128], lhsT=BT[:, g, c0:c0 + 128], rhs=CT[:, g, c0:c0 + 128], start=True, stop=True)
                    E('scalar', 'activation', [], ['B0', 'ed%d' % bs], out=ed2[bs], in_=bank[0][:, 0:96], func=AF.Exp)
                    for g in range(4):
                        h0 = g * 8
                        if g % 2 == 0:
                            E('gpsimd', 'tensor_tensor', ['Minc', KATOK], ['RE%d_%d' % (g, bs)], out=RE2[bs][g], in0=Minc.unsqueeze(1).to_broadcast([128, 8, 128]),
                              in1=a_c[:, h0:h0 + 8].unsqueeze(2).to_broadcast([128, 8, 128]), op=ALU.mult)
                        else:
                            for hh in range(8):
                                E('scalar', 'activation', ['Minc', KATOK], ['RE%d_%d' % (g, bs)], out=RE2[bs][g][:, hh, :], in_=Minc, func=AF.Identity, scale=a_c[:, h0 + hh:h0 + hh + 1])
                    E('vector', 'tensor_tensor', ['Minc'], ['B4', 'cbm%d' % bs], out=cbm2[bs], in0=bank[4][:, :].rearrange("p (g x) -> p g x", x=128),
                      in1=Minc.unsqueeze(1).to_broadcast([128, 4, 128]), op=ALU.mult)
                    E('vector', 'tensor_tensor', ['ed%d' % bs, KDTV], ['w2%d' % bs], out=w22[bs], in0=ed2[bs][:, 32:64], in1=dt_c, op=ALU.mult)
                    for g in range(4):
                        h0 = g * 8
                        pair = (1, 2) if g % 2 == 0 else (3, 7)
                        for half in range(2):
                            bi = pair[half]
                            E('tensor', 'matmul', ['RE%d_%d' % (g, bs), 'Mstr'], ['B%d' % bi], out=bank[bi][:, :], lhsT=Mstr, rhs=RE2[bs][g][:, half * 4:half * 4 + 4, :].rearrange("p h x -> p (h x)"), start=True, stop=True)
                        for half in range(2):
                            bi = pair[half]
                            E('scalar', 'activation', [], ['B%d' % bi, 'RE%d_%d' % (g, bs)], out=RE2[bs][g][:, half * 4:half * 4 + 4, :], in_=bank[bi][:, :].rearrange("p (h x) -> p h x", x=128), func=AF.Exp)
                        E('vector', 'tensor_tensor', ['RE%d_%d' % (g, bs), 'cbm%d' % bs], ['MT8%d_%d' % (g, bs)], out=MT82[bs][g], in0=RE2[bs][g], in1=cbm2[bs][:, g, :].unsqueeze(1).to_broadcast([128, 8, 128]), op=ALU.mult)
                        xsg = xs_tok[:, ch, h0 * 64:(h0 + 8) * 64].rearrange("p (h x) -> p h x", x=64)
                        E('gpsimd', 'tensor_tensor', [KXS, KDTV], ['xdt%d_%d' % (g, bs)], out=xdt2[bs][g], in0=xsg, in1=dt_c[:, h0:h0 + 8].unsqueeze(2).to_broadcast([128, 8, 64]), op=ALU.mult)
                        E('gpsimd', 'tensor_tensor', [KXS, 'w2%d' % bs], ['xd%d_%d' % (g, bs)], out=xd2[bs][g], in0=xsg, in1=w22[bs][:, h0:h0 + 8].unsqueeze(2).to_broadcast([128, 8, 64]), op=ALU.mult)

            def stage_c(chi, ch, bs):
                    cidx = i * 2 + ch
                    c0 = ch * 128
                    a_c = a_tok[:, ch, d * 32:(d + 1) * 32]
                    dt_c = dtv[:, ch, d * 32:(d + 1) * 32]
                    yb = cidx % 2
                    ycur = ych[yb]
                    yk = 'ych%d' % yb
                    for g in range(4):
                        h0 = g * 8
                        r = g % 2
                        bd, bo, bst = ((4, 5, 6), (0, 1, 2))[r]
                        for hh in range(8):
                            E('tensor', 'matmul', ['MT8%d_%d' % (g, bs), 'xdt%d_%d' % (g, bs)], ['B%d' % bd], out=bank[bd][:, hh * 64:(hh + 1) * 64], lhsT=MT82[bs][g][:, hh, :], rhs=xdt2[bs][g][:, hh, :], start=True, stop=True)
                        E('tensor', 'matmul', [KCT, 'S_bf%d' % g], ['B%d' % bo], out=bank[bo][:, :], lhsT=CT[:, g, c0:c0 + 128], rhs=S_bf[:, g * 512:(g + 1) * 512], start=True, stop=True)
                        E('tensor', 'matmul', [KBTOK, 'xd%d_%d' % (g, bs)], ['B%d' % bst], out=bank[bst][:, :], lhsT=B_tok[:, ch, g * 128:(g + 1) * 128], rhs=xd2[bs][g].rearrange("p h x -> p (h x)"), start=True, stop=True)
                        ty = tmpy[r]
                        E('vector', 'tensor_tensor', ['ed%d' % bs], ['B%d' % bo, 'tmpy%d' % r], out=ty.rearrange("p (h x) -> p h x", x=64), in0=bank[bo][:, :].rearrange("p (h x) -> p h x", x=64),
                          in1=ed2[bs][:, h0:h0 + 8].unsqueeze(2).to_broadcast([128, 8, 64]), op=ALU.mult)
                        E('vector', 'tensor_tensor', ['tmpy%d' % r], ['B%d' % bd, yk], out=ycur[:, g * 512:(g + 1) * 512], in0=bank[bd][:, :], in1=ty, op=ALU.add)
                        E('gpsimd', 'tensor_tensor', ['S%d' % g, 'ed%d' % bs], ['tmps%d' % r], out=tmps[r].rearrange("p (h x) -> p h x", x=64), in0=S[:, g * 512:(g + 1) * 512].rearrange("p (h x) -> p h x", x=64),
                          in1=ed2[bs][:, 64 + h0:64 + h0 + 8].unsqueeze(2).to_broadcast([128, 8, 64]), op=ALU.mult)
                        E('vector', 'tensor_tensor', ['tmps%d' % r], ['B%d' % bst, 'S%d' % g], out=S[:, g * 512:(g + 1) * 512], in0=bank[bst][:, :], in1=tmps[r], op=ALU.add)
                        E('scalar', 'activation', ['S%d' % g], ['S_bf%d' % g], out=S_bf[:, g * 512:(g + 1) * 512], in_=S[:, g * 512:(g + 1) * 512], func=AF.Identity)
                    if d == 0:
                        P.dma('sync', self.ydram[cidx], ycur, [yk], ['yd%d' % cidx], buf='yst%d' % yb)
                    else:
                        E('vector', 'tensor_tensor', [KXS, 'sbc'], ['yfb'], out=yfb.rearrange("p (h x) -> p h x", x=64), in0=xs_tok[:, ch, :].rearrange("p (h x) -> p h x", x=64),
                          in1=sbc[:, 128:160].unsqueeze(2).to_broadcast([128, 32, 64]), op=ALU.mult)
                        E('vector', 'tensor_tensor', ['yfb', yk], [yk], out=ycur, in0=ycur, in1=yfb, op=ALU.add)
                        P.dma('gpsimd', self.ydram[cidx], ycur, [yk], ['yd%d' % cidx], buf='yst%d' % yb, accum_op=ALU.add)


            chs = (0, 1) if d == 0 else (1, 0)
            if nbs == 2:
                stage_ab(0, chs[0], 0)
                stage_ab(1, chs[1], 1)
                stage_c(0, chs[0], 0)
                stage_c(1, chs[1], 1)
            else:
                stage_ab(0, chs[0], 0)
                stage_c(0, chs[0], 0)
                if d == 0 and oi + 1 < len(order):
                    cur = do_pre(order[oi + 1])
                stage_ab(1, chs[1], 0)
                stage_c(1, chs[1], 0)

    def ssd_proj(self, l, i, hT, hk, wx, cv, sbc, negA, identf, yc, sil, dtr, dta, dte, xs_tok, B_tok, BT, CT, dtv, a_tok, WX, p=0):
        P = self.P
        E = self.E
        bank = self.bank
        t0, lo, hi, s = tile_info(i)
        pend = []

        def post(cch, q):
            sl = sil[q]
            tb = (3, 7, 6)[cch % 3]
            for ch in range(2):
                E('tensor', 'matmul', ['sil%d' % q, 'identf'], ['B%d' % tb], out=bank[tb][:, ch * 128:(ch + 1) * 128], lhsT=sl[:, ch * 128:(ch + 1) * 128], rhs=identf, start=True, stop=True)
            src = bank[tb][:, 0:256].rearrange("p (c x) -> p c x", x=128)
            if cch < 16:
                E('scalar', 'activation', [], ['B%d' % tb, 'xs_tok%d' % p], out=xs_tok[:, :, cch * 128:(cch + 1) * 128], in_=src, func=AF.Identity)
            else:
                g = cch - 16
                E('vector', 'tensor_copy', [], ['B%d' % tb, 'B_tok%d' % p], out=B_tok[:, :, g * 128:(g + 1) * 128], in_=src)
        spend = []

        def do_silu(cch, y, yk):
            if cch < 20:
                q = cch % 6
                E('scalar', 'activation', [yk], ['sil%d' % q], out=sil[q], in_=y, func=AF.Silu)
                if cch >= 16:
                    E('gpsimd', 'tensor_copy', ['sil%d' % q], ['BT%d' % p], out=BT[:, cch - 16, :], in_=sil[q])
                pend.append((cch, q))
            else:
                E('scalar', 'activation', [yk], ['CT%d' % p], out=CT[:, cch - 20, :], in_=y, func=AF.Silu)
        for cch in range(24):
            bi = (1, 2, 4, 5)[cch % 4]
            pb = bank[bi]
            bkey = 'B%d' % bi
            self.MM(pb[:, lo:hi], [(wx[:, kc, cch * 128:(cch + 1) * 128], hT[:, kc, lo:hi]) for kc in range(8)], WX + [hk], [bkey])
            if len(pend) >= 3:
                post(*pend.pop(0))
            q2 = cch % 4
            y = yc[q2]
            yk = 'yc%d' % q2
            w = cv[:, cch, :]
            E('scalar', 'activation', ['cv'], [bkey, yk], out=y[:, 0:256], in_=pb[:, 1:257], func=AF.Identity, scale=w[:, 1:2], bias=w[:, 3:4])
            o0 = 0 if lo == 0 else 1
            E('vector', 'scalar_tensor_tensor', ['cv'], [bkey, yk], out=y[:, o0:256], in0=pb[:, o0:256], scalar=w[:, 0:1], in1=y[:, o0:256], op0=ALU.mult, op1=ALU.add)
            o1 = 256 if hi == 258 else 255
            E('vector', 'scalar_tensor_tensor', ['cv'], [bkey, yk], out=y[:, 0:o1], in0=pb[:, 2:2 + o1], scalar=w[:, 2:3], in1=y[:, 0:o1], op0=ALU.mult, op1=ALU.add)
            if len(spend) >= 2:
                do_silu(*spend.pop(0))
            spend.append((cch, y, yk))
        while spend:
            do_silu(*spend.pop(0))
        while pend:
            post(*pend.pop(0))
        for ch in range(2):
            self.MM(bank[0][:, ch * 64:(ch + 1) * 64], [(hT[:, kc, 1 + ch * 128:1 + (ch + 1) * 128], wx[:, kc, 3072:3136]) for kc in range(8)], WX + [hk], ['B0'])
        E('vector', 'tensor_tensor', ['sbc'], ['B0', 'dtr'], out=dtr, in0=bank[0][:, 0:128].rearrange("p (c x) -> p c x", x=64),
          in1=sbc[:, 64:128].unsqueeze(1).to_broadcast([128, 2, 64]), op=ALU.add)
        E('scalar', 'activation', ['dtr'], ['dta'], out=dta, in_=dtr, func=AF.Abs)
        E('scalar', 'activation', ['dta'], ['dte'], out=dte, in_=dta, func=AF.Exp, scale=-1.0)
        E('scalar', 'activation', ['dte', 'ones'], ['dte'], out=dte, in_=dte, func=AF.Ln, bias=self.ones[:, 0:1])
        E('scalar', 'activation', ['dtr'], ['dta'], out=dta, in_=dtr, func=AF.Relu)
        E('vector', 'tensor_tensor', ['dta', 'dte'], ['dtv%d' % p], out=dtv, in0=dta, in1=dte, op=ALU.add)
        E('vector', 'tensor_tensor', ['dtv%d' % p, 'negA'], ['a_tok%d' % p], out=a_tok, in0=dtv, in1=negA.unsqueeze(1).to_broadcast([128, 2, 64]), op=ALU.mult)

    def ssd_out(self, l, i2, last):
        P = self.P
        E = self.E
        bank = self.bank
        self.phase()
        wz = self.carve([128, 8, 2048], BF16)
        wo = self.carve([128, 16, 1024], BF16)
        stg = [self.carve([128, 1024], F32) for _ in range(2)]
        nrm = self.carve([128, 16], F32)
        yb = [self.carve([128, 2048], F32) for _ in range(2)]
        sz = [self.carve([128, 512], F32) for _ in range(2)]
        gq = self.carve([128, 2048], F32)
        gn = self.carve([128, 2048], BF16)
        gT = self.carve([128, 16, 256], BF16)
        ssq = self.carve([128, 4], F32)
        rt = self.carve([128, 1], F32)
        junk = self.carve([128, 512], F32)
        P.dma_group('gpsimd', [(wz[:, kc], self.swin[i2, :, kc, 0:2048], 'wz%d' % kc) for kc in range(8)], 'wzg')
        P.dma('sync', nrm, self.snorm[i2], [], ['nrm'], buf='nrm')
        for fc in range(16):
            st = stg[fc % 2]
            sk = 'stg%d' % (fc % 2)
            P.dma('sync', st, self.swout[i2, :, fc], [], [sk], buf=sk)
            E('vector', 'tensor_scalar', [sk, 'nrm'], ['wo'], out=wo[:, fc, :], in0=st, scalar1=nrm[:, fc:fc + 1], scalar2=None, op0=ALU.mult)
        otiles = [i for i in range(NT) if not (last and i == 0)]
        xt_save = self.xt
        self.xt = self.xt + [self.carve([128, 8, 258], F32)]
        gq2 = [gq, self.carve([128, 2048], F32)]
        gn2 = [gn, self.carve([128, 2048], BF16)]
        gT2 = [gT, self.carve([128, 16, 256], BF16)]
        ssq2 = [ssq, self.carve([128, 4], F32)]
        rt2 = [rt, self.carve([128, 1], F32)]

        def do_pre(i):
            xb = self.load_x(i, self.xsrc, halo=False)
            return xb, self.prenorm(l, i, xb, 0, halo=False)

        def stage_z(i, ch, hT, hk):
            cidx = i * 2 + ch
            b = cidx % 2
            P.dma('sync', yb[b], self.ydram[cidx], ['yd%d' % cidx], ['yb%d' % b], buf='yb%d' % b)
            E('gpsimd', 'memset', [], ['ssq%d' % b], ap=ssq2[b], constant=0.0)
            pend = None

            def sq(q):
                E('scalar', 'activation', ['gq%d' % b], ['junk', 'ssq%d' % b], out=junk, in_=gq2[b][:, q * 512:(q + 1) * 512], func=AF.Square, accum_out=ssq2[b][:, q:q + 1])
            for q in range(4):
                bi = 1 + q % 2
                self.MM(bank[bi][:, :], [(hT[:, kc, 1 + ch * 128:1 + (ch + 1) * 128], wz[:, kc, q * 512:(q + 1) * 512]) for kc in range(8)], ['wz%d' % kc for kc in range(8)] + [hk], ['B%d' % bi])
                E('scalar', 'activation', [], ['B%d' % bi, 'sz%d' % (q % 2)], out=sz[q % 2], in_=bank[bi][:, :], func=AF.Silu)
                if pend is not None:
                    sq(pend)
                E('vector', 'tensor_tensor', ['sz%d' % (q % 2), 'yb%d' % b], ['gq%d' % b], out=gq2[b][:, q * 512:(q + 1) * 512], in0=yb[b][:, q * 512:(q + 1) * 512], in1=sz[q % 2], op=ALU.mult)
                pend = q
            sq(pend)
            E('vector', 'tensor_reduce', ['ssq%d' % b], ['rt%d' % b], out=rt2[b], in_=ssq2[b], axis=mybir.AxisListType.X, op=ALU.add)
            E('scalar', 'activation', ['rt%d' % b, 'epsc'], ['rt%d' % b], out=rt2[b], in_=rt2[b], func=AF.Sqrt, scale=1.0 / 2048, bias=self.epsc[:, 0:1])
            E('vector', 'reciprocal', ['rt%d' % b], ['rt%d' % b], out=rt2[b], in_=rt2[b])
            E('scalar', 'activation', ['gq%d' % b, 'rt%d' % b], ['gn%d' % b], out=gn2[b], in_=gq2[b], func=AF.Identity, scale=rt2[b][:, 0:1])

        def stage_t(i, ch, gi):
            b = (i * 2 + ch) % 2
            for fb in range(4):
                pt = bank[3 + fb % 2][:, :].bitcast(BF16)
                pk = 'B%d' % (3 + fb % 2)
                for f4 in range(4):
                    fc = fb * 4 + f4
                    E('tensor', 'transpose', ['gn%d' % b, 'ident'], [pk], out=pt[:, f4 * 128:(f4 + 1) * 128], in_=gn2[b][:, fc * 128:(fc + 1) * 128], identity=self.ident[:])
                E('vector', 'tensor_copy', [], [pk, 'gT%d' % gi], out=gT2[gi][:, fb * 4:fb * 4 + 4, ch * 128:(ch + 1) * 128], in_=pt[:, 0:512].rearrange("p (f x) -> p f x", x=128))

        def outproj(i, xb, gi):
            def get_chunk(m):
                bi = 5 + (m % 2)
                self.MM(self.bank[bi][:, 0:256], [(wo[:, fc, m * 128:(m + 1) * 128], gT2[gi][:, fc, :]) for fc in range(16)], ['wo', 'gT%d' % gi], ['B%d' % bi])
                return self.bank[bi][:, 0:256], 'B%d' % bi
            self.residual(l, i, 0, get_chunk, xb, False)
        self.nxt = 3
        cur = do_pre(otiles[0])
        prev = None
        for oi, i in enumerate(otiles):
            xb, hb = cur
            hT = self.hT[hb]
            hk = 'hT%d' % hb
            gi = oi % 2
            stage_z(i, 0, hT, hk)
            stage_z(i, 1, hT, hk)
            if oi + 1 < len(otiles):
                cur = do_pre(otiles[oi + 1])
            stage_t(i, 0, gi)
            if prev is not None:
                outproj(*prev)
            stage_t(i, 1, gi)
            prev = (i, xb, gi)
        outproj(*prev)
        self.xt = xt_save
        self.nxt = 2


def host_layout(inp):
    f = np.float32
    out = {}
    mod_w = inp['mod_w']
    out['modw'] = np.ascontiguousarray(mod_w.reshape(4, 8, 128, 12, 512).transpose(0, 3, 2, 1, 4)).astype(f)
    out['modb'] = np.ascontiguousarray(inp['mod_b'].reshape(4, 48, 128).transpose(0, 2, 1)).astype(f)
    out['normw'] = np.ascontiguousarray(inp['norm_w'].reshape(4, 4, 8, 128).transpose(0, 3, 1, 2)).astype(f)
    wu = inp['ffn_w_up'].reshape(4, 8, 128, 2, NJ, 128)
    out['wup'] = np.ascontiguousarray(wu.transpose(0, 4, 2, 1, 3, 5)).reshape(4, NJ, 128, 8, 256).astype(f)
    out['wdn'] = np.ascontiguousarray(inp['ffn_w_down'].reshape(4, NJ, 128, 1024).transpose(0, 2, 1, 3)).astype(f)
    out['fcw'] = np.ascontiguousarray(inp['ffn_conv_w'].reshape(4, 3, 2, NJ, 128).transpose(0, 4, 3, 2, 1)).astype(f)
    w_in = inp['hyb_w_in']
    krsw = np.concatenate([w_in[:, :, 656:672], w_in[:, :, 640:656]], axis=2)
    w_in2 = np.concatenate([w_in, krsw], axis=2)
    out['hwin'] = np.ascontiguousarray(w_in2.reshape(2, 8, 128, 2240).transpose(0, 2, 1, 3)).astype(f)
    wuq = inp['mla_w_uq']
    sw = []
    for h in range(8):
        sw.append(wuq[:, :, h * 96 + 80:h * 96 + 96])
        sw.append(wuq[:, :, h * 96 + 64:h * 96 + 80])
    wuq2 = np.concatenate([wuq] + sw, axis=2)
    out['hwuq'] = np.ascontiguousarray(wuq2.reshape(2, 3, 128, 1024).transpose(0, 2, 1, 3)).astype(f)
    out['hwukv'] = np.ascontiguousarray(inp['mla_w_ukv'].reshape(2, 2, 128, 1024).transpose(0, 2, 1, 3)).astype(f)
    out['hwout'] = np.ascontiguousarray(inp['hyb_w_out'].reshape(2, 8, 128, 1024).transpose(0, 2, 1, 3)).astype(f)
    out['hqn'] = np.ascontiguousarray(inp['mla_q_norm'].reshape(2, 3, 128).transpose(0, 2, 1)).astype(f)
    out['hkvn'] = np.ascontiguousarray(inp['mla_kv_norm'].reshape(2, 2, 128).transpose(0, 2, 1)).astype(f)
    out['hscw'] = np.ascontiguousarray(inp['sconv_w'].reshape(2, 3, 4, 128).transpose(0, 3, 2, 1)).astype(f)
    out['rope'] = rope_tables()
    out['swin'] = np.ascontiguousarray(inp['ssd_w_in'].reshape(2, 8, 128, 5184).transpose(0, 2, 1, 3)).astype(f)
    cw = np.concatenate([inp['ssd_conv_w'], inp['ssd_conv_b'][:, None, :]], axis=1)
    out['scv'] = np.ascontiguousarray(cw.reshape(2, 4, 24, 128).transpose(0, 3, 2, 1)).astype(f)
    row = np.concatenate([inp['ssd_a_log'].reshape(2, 64), inp['ssd_dt_bias'].reshape(2, 64), inp['ssd_d'].reshape(2, 32)], axis=1)
    out['sbc'] = np.ascontiguousarray(np.broadcast_to(row[:, None, :], (2, 128, 160))).astype(f)
    out['snorm'] = np.ascontiguousarray(inp['ssd_norm'].reshape(2, 16, 128).transpose(0, 2, 1)).astype(f)
    out['swout'] = np.ascontiguousarray(inp['ssd_w_out'].reshape(2, 16, 128, 1024).transpose(0, 2, 1, 3)).astype(f)
    return out


def rope_tables():
    t = np.arange(NLAT)
    row = (t // 64).astype(np.float32)
    col = (t % 64).astype(np.float32)
    nf = 8
    inv = (np.float32(10000.0) ** (-np.arange(nf, dtype=np.float32) / nf)).astype(np.float32)
    ang = np.concatenate([row[:, None] * inv, col[:, None] * inv], axis=-1).astype(np.float32)
    cos = np.cos(ang).astype(np.float32).T
    sin = np.sin(ang).astype(np.float32).T
    tab = np.zeros((32, 4, T), np.float32)
    tab[:, 0, :NCTX] = 1.0
    tab[:, 2, :NCTX] = MLA_SCALE
    tab[0:16, 0, NCTX:] = cos
    tab[16:32, 0, NCTX:] = cos
    tab[0:16, 1, NCTX:] = -sin
    tab[16:32, 1, NCTX:] = sin
    tab[:, 2, NCTX:] = tab[:, 0, NCTX:] * np.float32(MLA_SCALE)
    tab[:, 3, NCTX:] = tab[:, 1, NCTX:] * np.float32(MLA_SCALE)
    return tab


def core_inputs(inp, b):
    xc = np.concatenate([inp['ctx'][b], inp['x'][b]], axis=0)
    xin = np.ascontiguousarray(xc.T.reshape(8, 128, T).transpose(1, 0, 2)).astype(np.float32)
    cc = np.stack([inp['c'][b], inp['c_ctx']], axis=-1)
    cc = np.ascontiguousarray(cc.reshape(8, 128, 2).transpose(1, 0, 2)).astype(np.float32)
    return {'xin': xin, 'cc': cc}


def kernel(**inputs):
    inp = {k: np.asarray(v) for k, v in inputs.items()}
    nb = inp['x'].shape[0]
    shared = host_layout(inp)
    in_maps = []
    for b in range(nb):
        m = dict(shared)
        m.update(core_inputs(inp, b))
        in_maps.append(m)
    B = Builder([0, 1, 2, 3])
    res = run_bass_kernel_spmd(B.nc, in_maps, core_ids=list(range(nb)))
    out = np.empty((nb, NLAT, D), np.float32)
    for b in range(nb):
        y = np.asarray(res.results[b]["yout"])
        out[b] = y.transpose(1, 0, 2).reshape(D, NLAT).T
    return out
```

```python
import numpy as np
from contextlib import ExitStack
import concourse.bass as bass
import concourse.mybir as mybir
from concourse.bass_utils import run_bass_kernel_spmd

F32 = mybir.dt.float32
BF16 = mybir.dt.bfloat16
AF = mybir.ActivationFunctionType
ALU = mybir.AluOpType

ENGINES = ['tensor', 'vector', 'scalar', 'gpsimd', 'sync']
CHUNK = 16000

D = 1024
T = 2304
NCTX = 256
NLAT = 2048
TILE = 256
NT = T // TILE
EPS = 1e-6
DFF = 2816
NJ = DFF // 128
HYB_IN = 2208
MLA_SCALE = 96 ** -0.5


class Prog:
    def __init__(self, nc):
        self.nc = nc
        self.es = ExitStack()
        self.streams = {e: [] for e in ENGINES}
        self.count = {e: 0 for e in ENGINES}
        self.esems = {e: [] for e in ENGINES}
        self.dsems = {}
        self.dcount = {}
        self.lastw = {}
        self.readers = {}
        self.seen = {e: {} for e in ENGINES}
        self.nsem = 0

    def sb(self, name, shape, dtype):
        return self.es.enter_context(self.nc.sbuf_tensor(name, shape, dtype))

    def ps(self, name, shape, dtype):
        return self.es.enter_context(self.nc.psum_tensor(name, shape, dtype))

    def _newsem(self, name):
        self.nsem += 1
        return self.es.enter_context(self.nc.semaphore(name))

    def _semof(self, semkey, val):
        if semkey[0] == 'E':
            eng = semkey[1]
            ep = (val - 1) // CHUNK
            while len(self.esems[eng]) <= ep:
                self.esems[eng].append(self._newsem("s_%s_%d" % (eng, len(self.esems[eng]))))
            return self.esems[eng][ep], (val - 1) % CHUNK + 1
        return self.dsems[semkey[1]], val

    def _deps(self, eng, reads, writes):
        evs = {}

        def add(ev):
            if ev is None:
                return
            k, v = ev
            if evs.get(k, 0) < v:
                evs[k] = v
        for k in reads:
            add(self.lastw.get(k))
        for k in writes:
            add(self.lastw.get(k))
            for ev in self.readers.get(k, {}).items():
                add(ev)
        waits = []
        seen = self.seen[eng]
        for k, v in evs.items():
            if eng == 'tensor' and k == ('E', 'tensor'):
                continue
            if seen.get(k, 0) >= v:
                continue
            seen[k] = v
            waits.append(self._semof(k, v))
        return waits

    def _commit(self, ev, reads, writes):
        k, v = ev
        for r in reads:
            d = self.readers.setdefault(r, {})
            if d.get(k, 0) < v:
                d[k] = v
        for w in writes:
            self.lastw[w] = ev
            self.readers[w] = {}

    def op(self, eng, fn, reads=(), writes=()):
        waits = self._deps(eng, reads, writes)
        self.count[eng] += 1
        n = self.count[eng]
        ev = (('E', eng), n)
        sem, val = self._semof(ev[0], n)
        self.streams[eng].append((waits, fn, (sem, 1)))
        self._commit(ev, reads, writes)

    def dma(self, queue, out, in_, reads=(), writes=(), buf=None, **kw):
        if buf not in self.dsems:
            self.dsems[buf] = self._newsem("d_%s" % buf)
            self.dcount[buf] = 0
        waits = self._deps(queue, reads, writes)
        self.dcount[buf] += 16
        ev = (('D', buf), self.dcount[buf])
        sem = self.dsems[buf]
        self.streams[queue].append((waits, lambda e: e.dma_start(out=out, in_=in_, **kw), (sem, 16)))
        self._commit(ev, reads, writes)

    def dma_group(self, queue, items, buf, **kw):
        if buf not in self.dsems:
            self.dsems[buf] = self._newsem("d_%s" % buf)
            self.dcount[buf] = 0
        keys = [k for (_, _, k) in items]
        waits = self._deps(queue, [], keys)
        sem = self.dsems[buf]
        first = True
        for (out, in_, k) in items:
            self.dcount[buf] += 16
            self.streams[queue].append((waits if first else [], (lambda e, out=out, in_=in_: e.dma_start(out=out, in_=in_, **kw)), (sem, 16)))
            first = False
        ev = (('D', buf), self.dcount[buf])
        self._commit(ev, [], keys)

    def barrier(self):
        evs = [(('E', e), self.count[e]) for e in ENGINES if self.count[e] > 0]
        evs += [(('D', b), c) for b, c in self.dcount.items()]
        for eng in ENGINES:
            waits = []
            seen = self.seen[eng]
            for k, v in evs:
                if seen.get(k, 0) >= v:
                    continue
                seen[k] = v
                waits.append(self._semof(k, v))
            if waits:
                self.streams[eng].append((waits, None, None))
        self.lastw = {}
        self.readers = {}

    def finish(self, final_waits=()):
        self.barrier()
        nc = self.nc
        with nc.Block() as block:
            for eng in ENGINES:
                stream = self.streams[eng]
                if not stream:
                    continue

                def body(e, stream=stream):
                    for waits, fn, inc in stream:
                        for sem, val in waits:
                            e.wait_ge(sem, val)
                        if fn is None:
                            continue
                        ins = fn(e)
                        if inc is not None:
                            ins.then_inc(inc[0], inc[1])
                getattr(block, eng)(body)
        self.es.close()


def tile_info(i):
    t0 = i * TILE
    has_left = i >= 2
    has_right = 1 <= i <= NT - 2
    return t0, (0 if has_left else 1), (258 if has_right else 257), (1 if i == 0 else 0)


class Builder:
    ARENA = 164 * 1024

    def __init__(self, layers, do_mix=True, do_ffn=True, tiles=None, debug=False):
        self.layers = layers
        self.tiles = tiles
        nc = bass.Bass("TRN2", target_bir_lowering=False)
        self.nc = nc
        self.P = Prog(nc)
        P = self.P
        dt = nc.dram_tensor
        self.xin = dt("xin", [128, 8, T], F32, kind="ExternalInput").ap()
        self.cc = dt("cc", [128, 8, 2], F32, kind="ExternalInput").ap()
        self.modw = dt("modw", [4, 12, 128, 8, 512], F32, kind="ExternalInput").ap()
        self.modb = dt("modb", [4, 128, 48], F32, kind="ExternalInput").ap()
        self.normw = dt("normw", [4, 128, 4, 8], F32, kind="ExternalInput").ap()
        self.wup = dt("wup", [4, NJ, 128, 8, 256], F32, kind="ExternalInput").ap()
        self.wdn = dt("wdn", [4, 128, NJ, 1024], F32, kind="ExternalInput").ap()
        self.fcw = dt("fcw", [4, 128, NJ, 2, 3], F32, kind="ExternalInput").ap()
        self.hwin = dt("hwin", [2, 128, 8, 2240], F32, kind="ExternalInput").ap()
        self.hwuq = dt("hwuq", [2, 128, 3, 1024], F32, kind="ExternalInput").ap()
        self.hwukv = dt("hwukv", [2, 128, 2, 1024], F32, kind="ExternalInput").ap()
        self.hwout = dt("hwout", [2, 128, 8, 1024], F32, kind="ExternalInput").ap()
        self.hqn = dt("hqn", [2, 128, 3], F32, kind="ExternalInput").ap()
        self.hkvn = dt("hkvn", [2, 128, 2], F32, kind="ExternalInput").ap()
        self.hscw = dt("hscw", [2, 128, 4, 3], F32, kind="ExternalInput").ap()
        self.rope = dt("rope", [32, 4, T], F32, kind="ExternalInput").ap()
        self.swin = dt("swin", [2, 128, 8, 5184], F32, kind="ExternalInput").ap()
        self.scv = dt("scv", [2, 128, 24, 4], F32, kind="ExternalInput").ap()
        self.sbc = dt("sbc", [2, 128, 160], F32, kind="ExternalInput").ap()
        self.snorm = dt("snorm", [2, 128, 16], F32, kind="ExternalInput").ap()
        self.swout = dt("swout", [2, 128, 16, 1024], F32, kind="ExternalInput").ap()
        self.ydram = dt("ydram", [18, 128, 2048], F32, kind="Internal").ap()
        self.sxs = dt("sxs", [NT, 128, 2, 2048], F32, kind="Internal").ap()
        self.sbt = dt("sbt", [NT, 128, 3072], BF16, kind="Internal").ap()
        self.sdt = dt("sdt", [NT, 128, 256], F32, kind="Internal").ap()
        self.xdA = dt("xdA", [128, 8, T], F32, kind="ExternalOutput" if debug else "Internal").ap()
        self.xdB = dt("xdB", [128, 8, T], F32, kind="ExternalOutput" if debug else "Internal").ap()
        self.yout = dt("yout", [128, 8, NLAT], F32, kind="ExternalOutput").ap()
        self.bank = [P.ps("bk%d" % i, [128, 512], F32) for i in range(8)]
        self.ones = P.sb("ones", [128, 128], F32)
        self.ones_bf = P.sb("ones_bf", [128, 128], BF16)
        self.ident = P.sb("ident", [128, 128], BF16)
        self.der = P.sb("der", [128, 4, 4, 8, 2], F32)
        self.modv = P.sb("modv", [128, 4, 48, 2], F32)
        self.xt = [P.sb("xt%d" % i, [128, 8, 258], F32) for i in range(2)]
        self.hT = [P.sb("hT%d" % i, [128, 8, 258], BF16) for i in range(2)]
        self.sq = [P.sb("sq%d" % i, [128, 258], BF16) for i in range(4)]
        self.sqp = P.sb("sqp", [128, 8, 258], BF16)
        self.rstd = [P.sb("rstd%d" % i, [128, 258], F32) for i in range(2)]
        self.tmpn = [P.sb("tmpn%d" % i, [128, 258], F32) for i in range(3)]
        self.epsc = P.sb("epsc", [128, 1], F32)
        self.arena = P.sb("arena", [128, self.ARENA // 2], BF16)
        self.cnt = {}
        self.xsrc = (self.xin, 'xi')
        self.xnext = [(self.xdA, 'xa'), (self.xdB, 'xb')]
        self.final_keys = []
        self.emit_consts()
        self.prologue()
        for l in layers:
            last = (l == 3)
            if do_mix:
                if l % 2 == 0:
                    self.even_mixer(l, last)
                else:
                    self.odd_mixer(l, last)
            if do_ffn:
                self.ffn(l, last)
        P.finish()

    def phase(self, resid=True):
        self.P.barrier()
        self.aoff = 0
        if resid:
            self.mtmp = self.carve([128, 8, 256], F32)
            self.rs2 = self.carve([128, 256], F32)

    def carve(self, shape, dtype):
        n = 1
        for d in shape[1:]:
            n *= d
        nb = n * (4 if dtype == F32 else 2)
        nb = (nb + 31) // 32 * 32
        assert self.aoff + nb <= self.ARENA, ("arena overflow", self.aoff, nb)
        ap = self.arena[:, self.aoff // 2:(self.aoff + nb) // 2]
        self.aoff += nb
        if dtype == F32:
            ap = ap.bitcast(F32)
        ap = ap[:, 0:n]
        if len(shape) == 3:
            ap = ap.rearrange("p (a b) -> p a b", a=shape[1])
        elif len(shape) == 4:
            ap = ap.rearrange("p (a b c) -> p a b c", a=shape[1], b=shape[2])
        return ap

    def E(self, eng, method, reads, writes, **kw):
        self.P.op(eng, lambda e: getattr(e, method)(**kw), reads, writes)

    def MM(self, out, pairs, reads, writes, first=True, **kw):
        pairs = list(pairs)

        def fn(e):
            n = len(pairs)
            ins = None
            for i, (l, r) in enumerate(pairs):
                ins = e.matmul(out, lhsT=l, rhs=r, start=(first and i == 0), stop=(i == n - 1), **kw)
            return ins
        self.P.op('tensor', fn, reads, writes)

    def rot(self, name, n):
        i = self.cnt.get(name, 0)
        self.cnt[name] = i + 1
        return i % n

    def emit_consts(self):
        self.E('gpsimd', 'memset', [], ['ones'], ap=self.ones[:], constant=1.0)
        self.E('gpsimd', 'memset', [], ['ones'], ap=self.ones_bf[:], constant=1.0)
        self.E('gpsimd', 'memset', [], ['epsc'], ap=self.epsc[:], constant=EPS)
        self.E('gpsimd', 'memset', [], ['ident'], ap=self.ident[:], constant=1.0)
        self.E('gpsimd', 'affine_select', ['ident'], ['ident'], out=self.ident[:], in_=self.ident[:],
               pattern=[[-1, 128]], compare_op=ALU.is_equal, fill=0.0, base=0, channel_multiplier=1)

    def prologue(self):
        P = self.P
        self.phase()
        cs = self.carve([128, 8, 2], F32)
        sc = self.carve([128, 8, 2], BF16)
        NWB = 4
        wb = [self.carve([128, 8, 512], BF16) for i in range(NWB)]
        mb = self.carve([128, 4, 48], F32)
        nw = self.carve([128, 4, 4, 8], F32)
        P.dma('sync', cs, self.cc, [], ['cs'], buf='cs')
        self.E('scalar', 'activation', ['cs'], ['scs'], out=sc, in_=cs, func=AF.Silu)
        for l in self.layers:
            P.dma('sync', mb[:, l, :], self.modb[l], [], ['mbias%d' % l], buf='mbias%d' % l)
            P.dma('sync', nw[:, l], self.normw[l], [], ['nw%d' % l], buf='nw%d' % l)
        k = 0
        for l in self.layers:
            for nb in range(12):
                w = wb[k % NWB]
                wk = 'mwb%d' % (k % NWB)
                P.dma('gpsimd', w, self.modw[l, nb], [], [wk], buf=wk)
                bk = 'B%d' % (k % 2)
                for m in range(4):
                    self.MM(self.bank[k % 2][:, 2 * m:2 * m + 2],
                            [(w[:, kc, m * 128:(m + 1) * 128], sc[:, kc, :]) for kc in range(8)],
                            [wk, 'scs'], [bk])
                self.E('vector', 'tensor_tensor', ['mbias%d' % l], [bk, 'modv%d' % l],
                       out=self.modv[:, l, nb * 4:nb * 4 + 4, :],
                       in0=self.bank[k % 2][:, 0:8].rearrange("p (a b) -> p a b", b=2),
                       in1=mb[:, l, nb * 4:nb * 4 + 4].unsqueeze(2).to_broadcast([128, 4, 2]), op=ALU.add)
                k += 1
            mv = self.modv
            dk = 'der%d' % l
            rk = ['modv%d' % l, 'nw%d' % l]

            def nwb(i):
                return nw[:, l, i, :].unsqueeze(2).to_broadcast([128, 8, 2])
            self.E('vector', 'scalar_tensor_tensor', rk, [dk], out=self.der[:, l, 0], in0=mv[:, l, 8:16, :],
                   scalar=1.0, in1=nwb(0), op0=ALU.add, op1=ALU.mult)
            self.E('vector', 'tensor_tensor', rk, [dk], out=self.der[:, l, 1], in0=mv[:, l, 16:24, :], in1=nwb(1), op=ALU.mult)
            self.E('vector', 'scalar_tensor_tensor', rk, [dk], out=self.der[:, l, 2], in0=mv[:, l, 32:40, :],
                   scalar=1.0, in1=nwb(2), op0=ALU.add, op1=ALU.mult)
            self.E('vector', 'tensor_tensor', rk, [dk], out=self.der[:, l, 3], in0=mv[:, l, 40:48, :], in1=nwb(3), op=ALU.mult)

    def next_stream(self):
        return self.xnext[0]

    def swap_stream(self):
        d = self.xnext.pop(0)
        if self.xsrc[1] != 'xi':
            self.xnext.append(self.xsrc)
        self.xsrc = d

    def load_x(self, i, src, halo=True):
        t0, lo, hi, s = tile_info(i)
        if not halo:
            lo, hi = 1, 257
        b = self.rot('xt', getattr(self, 'nxt', 2))
        self.P.dma('sync', self.xt[b][:, :, lo:hi], src[0][:, :, t0 - 1 + lo:t0 - 1 + hi],
                   [src[1] + str(j) for j in (i - 1, i, i + 1) if 0 <= j < NT], ['xt%d' % b], buf='xt%d' % b)
        return b

    def rstd_from_bank(self, bk, bkey, lo, hi, out, okey, n, extra_w=()):
        self.E('scalar', 'activation', ['epsc'], [bkey, okey] + list(extra_w), out=out[:, lo:hi], in_=bk[:, lo:hi],
               func=AF.Sqrt, scale=1.0 / n, bias=self.epsc[:, 0:1])
        self.E('vector', 'reciprocal', [okey], [okey], out=out[:, lo:hi], in_=out[:, lo:hi])

    def prenorm(self, l, i, xb, which, halo=True, part=None):
        t0, lo, hi, s = tile_info(i)
        if not halo:
            lo, hi = 1, 257
        xt = self.xt[xb]
        hb = self.rot('hT', 2)
        hT = self.hT[hb]
        rb = self.rot('rstd', 2)
        rstd = self.rstd[rb]
        bk = self.bank[0]
        if part == 'A':
            for c in range(8):
                self.E('scalar', 'activation', ['xt%d' % xb], ['sqp'], out=self.sqp[:, c, lo:hi], in_=xt[:, c, lo:hi], func=AF.Square)
            return None
        for c in range(8):
            if part == 'B':
                r, rk = self.sqp[:, c, lo:hi], 'sqp'
            else:
                q = self.rot('sq', 4)
                self.E('scalar', 'activation', ['xt%d' % xb], ['sq%d' % q], out=self.sq[q][:, lo:hi], in_=xt[:, c, lo:hi], func=AF.Square)
                r, rk = self.sq[q][:, lo:hi], 'sq%d' % q
            self.P.op('tensor', (lambda e, o=bk[:, lo:hi], r=r, c=c: e.matmul(o, lhsT=self.ones_bf[:], rhs=r, start=(c == 0), stop=(c == 7))),
                      ['ones', rk], ['B0'])
        self.rstd_from_bank(bk, 'B0', lo, hi, rstd, 'rstd%d' % rb, D)
        Ai = 0 if which == 0 else 2
        Boff = 0 if which == 0 else 24
        for c in range(8):
            q = self.rot('tmpn', 3)
            self.E('vector', 'tensor_tensor', ['xt%d' % xb, 'rstd%d' % rb], ['tmpn%d' % q], out=self.tmpn[q][:, lo:hi],
                   in0=xt[:, c, lo:hi], in1=rstd[:, lo:hi], op=ALU.mult)
            self.E('scalar', 'activation', ['tmpn%d' % q, 'der%d' % l, 'modv%d' % l], ['hT%d' % hb], out=hT[:, c, lo:hi], in_=self.tmpn[q][:, lo:hi],
                   func=AF.Identity, scale=self.der[:, l, Ai, c, s:s + 1], bias=self.modv[:, l, Boff + c, s:s + 1])
        return hb

    def res_tail(self, l, i, which, xb, to_out):
        t0, lo, hi, s = tile_info(i)
        Gi = 1 if which == 0 else 3
        xt = self.xt[xb]
        tail = []
        tail.append(lambda: self.rstd_from_bank(self.bank[7], 'B7', 0, 256, self.rs2, 'rs2', D))

        def pair(c):
            q = self.rot('tmpn', 3)
            self.E('vector', 'scalar_tensor_tensor', ['mtmp', 'rs2', 'der%d' % l], ['tmpn%d' % q], out=self.tmpn[q][:, 0:256], in0=self.mtmp[:, c, :],
                   scalar=self.der[:, l, Gi, c, s:s + 1], in1=self.rs2, op0=ALU.mult, op1=ALU.mult)
            self.E('gpsimd', 'tensor_tensor', ['tmpn%d' % q, 'xt%d' % xb], ['xt%d' % xb], out=xt[:, c, 1:257], in0=xt[:, c, 1:257], in1=self.tmpn[q][:, 0:256], op=ALU.add)
        for c in range(8):
            tail.append(lambda c=c: pair(c))

        def store():
            if to_out:
                self.P.dma('sync', self.yout[:, :, t0 - NCTX:t0 - NCTX + TILE], xt[:, :, 1:257], ['xt%d' % xb], ['yo%d' % i], buf='yo')
            else:
                dst = self.next_stream()
                self.P.dma('sync', dst[0][:, :, t0:t0 + TILE], xt[:, :, 1:257], ['xt%d' % xb], [dst[1] + str(i)], buf='xst%d' % xb)
        tail.append(store)
        return tail

    def residual(self, l, i, which, get_chunk, xb, to_out, hook=None, defer=False):
        t0, lo, hi, s = tile_info(i)
        Gi = 1 if which == 0 else 3
        xt = self.xt[xb]
        pend = None

        def stat(q, c):
            self.P.op('tensor', (lambda e, o=self.bank[7][:, 0:256], r=self.sq[q][:, 0:256], c=c: e.matmul(o, lhsT=self.ones_bf[:], rhs=r, start=(c == 0), stop=(c == 7))),
                      ['ones', 'sq%d' % q], ['B7'])
        for c in range(8):
            bap, bkey = get_chunk(c)
            if pend is not None:
                stat(*pend)
            self.E('vector', 'tensor_copy', [], [bkey, 'mtmp'], out=self.mtmp[:, c, :], in_=bap)
            q = self.rot('sq', 4)
            self.E('scalar', 'activation', [], [bkey, 'sq%d' % q], out=self.sq[q][:, 0:256], in_=bap, func=AF.Square)
            pend = (q, c)
            if hook is not None and c == 1:
                stat(*pend)
                pend = None
                hook()
        stat(*pend)
        tail = self.res_tail(l, i, which, xb, to_out)
        if defer:
            return tail
        for f in tail:
            f()
        return []

    def ffn(self, l, last):
        P = self.P
        self.phase()
        wup_sb = self.carve([128, NJ, 8, 256], BF16)
        wdn_sb = self.carve([128, NJ, 1024], BF16)
        fcw_sb = self.carve([128, NJ * 6], F32).rearrange("p (j a k) -> p j a k", j=NJ, a=2)
        abuf = self.carve([128, NJ, 256], BF16)
        ya = [self.carve([128, 256], F32) for i in range(2)]
        yg = [self.carve([128, 256], F32) for i in range(2)]
        sa = [self.carve([128, 256], F32) for i in range(2)]
        P.dma('sync', fcw_sb, self.fcw[l], [], ['fcw'], buf='fcw')
        for g0 in range(0, NJ, 4):
            P.dma_group('gpsimd', [(wup_sb[:, j], self.wup[l, j], 'wup%d' % j) for j in range(g0, min(NJ, g0 + 4))], 'wupg%d' % (g0 // 4))
        for g0 in range(0, NJ, 6):
            P.dma_group('gpsimd', [(wdn_sb[:, j], self.wdn[l, :, j], 'wdn%d' % j) for j in range(g0, min(NJ, g0 + 6))], 'wdng%d' % (g0 // 6))
        tiles = self.tiles if self.tiles is not None else list(range(NT))
        tiles = [i for i in tiles if not (last and i == 0)]

        def do_pre(i):
            xb = self.load_x(i, self.xsrc)
            return xb, self.prenorm(l, i, xb, 1)
        cur = do_pre(tiles[0])
        deferred = []
        for idx, i in enumerate(tiles):
            t0, lo, hi, s = tile_info(i)
            xb, hb = cur
            hT = self.hT[hb]
            for j in range(NJ):
                pb = j % 2
                banks = (1 + 2 * pb, 2 + 2 * pb)
                for ag in range(2):
                    bi = banks[ag]
                    bk = self.bank[bi]
                    bkey = 'B%d' % bi
                    self.MM(bk[:, lo:hi], [(wup_sb[:, j, kc, ag * 128:(ag + 1) * 128], hT[:, kc, lo:hi]) for kc in range(8)],
                            ['wup%d' % j, 'hT%d' % hb], [bkey])
                    y = (ya if ag == 0 else yg)[pb]
                    ykey = ('ya%d' if ag == 0 else 'yg%d') % pb
                    w = fcw_sb[:, j, ag, :]
                    self.E('scalar', 'activation', ['fcw'], [bkey, ykey], out=y[:, 0:256], in_=bk[:, 1:257], func=AF.Identity, scale=w[:, 1:2])
                    o0 = 0 if lo == 0 else 1
                    self.E('vector', 'scalar_tensor_tensor', ['fcw'], [bkey, ykey], out=y[:, o0:256], in0=bk[:, o0:256], scalar=w[:, 0:1],
                           in1=y[:, o0:256], op0=ALU.mult, op1=ALU.add)
                    o1 = 256 if hi == 258 else 255
                    self.E('vector', 'scalar_tensor_tensor', ['fcw'], [bkey, ykey], out=y[:, 0:o1], in0=bk[:, 2:2 + o1], scalar=w[:, 2:3],
                           in1=y[:, 0:o1], op0=ALU.mult, op1=ALU.add)
                self.E('scalar', 'activation', ['ya%d' % pb], ['sa%d' % pb], out=sa[pb], in_=ya[pb], func=AF.Silu)
                self.E('gpsimd', 'tensor_tensor', ['sa%d' % pb, 'yg%d' % pb], ['abuf%d' % j], out=abuf[:, j, :], in0=sa[pb], in1=yg[pb], op=ALU.mult)
                if deferred and j >= 2:
                    deferred.pop(0)()
                if j == 17 and idx + 1 < len(tiles):
                    nxb = self.load_x(tiles[idx + 1], self.xsrc)
                    self.prenorm(l, tiles[idx + 1], nxb, 1, part='A')
            while deferred:
                deferred.pop(0)()
            nxt = None
            if idx + 1 < len(tiles):
                nxt = (nxb, self.prenorm(l, tiles[idx + 1], nxb, 1, part='B'))
            NE = NJ - 2

            def dgroup(m, j0, j1):
                bi = (5, 6, 1, 2, 3, 4)[m % 6]

                def fn(e, m=m, j0=j0, j1=j1, bi=bi):
                    ins = None
                    for j in range(j0, j1):
                        ins = e.matmul(self.bank[bi][:, 0:256], lhsT=wdn_sb[:, j, m * 128:(m + 1) * 128], rhs=abuf[:, j, :], start=(j == 0), stop=(j == NJ - 1))
                    return ins
                self.P.op('tensor', fn, ['wdn%d' % j for j in range(j0, j1)] + ['abuf%d' % j for j in range(j0, j1)], ['B%d' % bi])
                return bi

            def evac(c):
                bi = (5, 6, 1, 2, 3, 4)[c % 6]
                self.E('vector', 'tensor_copy', [], ['B%d' % bi, 'mtmp'], out=self.mtmp[:, c, :], in_=self.bank[bi][:, 0:256])
                self.E('scalar', 'activation', [], ['B%d' % bi, 'sqp'], out=self.sqp[:, c, 0:256], in_=self.bank[bi][:, 0:256], func=AF.Square)
            dgroup(0, 0, NE)
            dgroup(1, 0, NE)
            dgroup(0, NE, NJ)
            dgroup(1, NE, NJ)
            for m in range(2, 6):
                dgroup(m, 0, NJ)
            evac(0)
            dgroup(6, 0, NJ)
            evac(1)
            dgroup(7, 0, NJ)
            for c in range(2, 8):
                evac(c)
            for c in range(8):
                self.P.op('tensor', (lambda e, o=self.bank[7][:, 0:256], r=self.sqp[:, c, 0:256], c=c: e.matmul(o, lhsT=self.ones_bf[:], rhs=r, start=(c == 0), stop=(c == 7))),
                          ['ones', 'sqp'], ['B7'])
            deferred = self.res_tail(l, i, 1, xb, last)
            cur = nxt
        while deferred:
            deferred.pop(0)()
        if not last:
            self.swap_stream()

    def even_mixer(self, l, last):
        P = self.P
        E = self.E
        i2 = l // 2
        self.phase()
        bank = self.bank
        cqnT = self.carve([128, 3, T], BF16)
        off_ckv = self.aoff
        ckvnT = self.carve([128, 2, T], BF16)
        krT = self.carve([128, T], BF16)
        scT = self.carve([128, 4, T], BF16)
        attnT = self.carve([128, 4, T], BF16)
        wuq_sb = self.carve([128, 3, 1024], BF16)
        wukv_sb = self.carve([128, 2, 1024], BF16)
        qn_sb = self.carve([128, 3], F32)
        kvn_sb = self.carve([128, 2], F32)
        scw_sb = self.carve([128, 4, 3], F32)
        mark = self.aoff
        win_sb = self.carve([128, 8, 2240], BF16)
        rt = self.carve([128, 4, 256], F32)
        craw = self.carve([128, 3, 256], F32)
        t1 = self.carve([128, 256], F32)
        t2 = self.carve([128, 256], F32)
        gcs = self.carve([128, 258], F32)
        vb = self.carve([128, 258], F32)
        ysc = self.carve([128, 256], F32)
        rq = self.carve([128, 256], F32)
        P.dma('sync', qn_sb, self.hqn[i2], [], ['qnw'], buf='qnw')
        P.dma('sync', kvn_sb, self.hkvn[i2], [], ['kvnw'], buf='kvnw')
        P.dma('sync', scw_sb, self.hscw[i2], [], ['scw'], buf='scw')
        P.dma_group('gpsimd', [(win_sb[:, kc], self.hwin[i2, :, kc], 'win%d' % kc) for kc in range(8)], 'wing')
        P.dma('gpsimd', wuq_sb, self.hwuq[i2], [], ['wuq'], buf='wuq')
        P.dma('gpsimd', wukv_sb, self.hwukv[i2], [], ['wukv'], buf='wukv')
        WIN = ['win%d' % kc for kc in range(8)]
        def do_pre(i):
            xb = self.load_x(i, self.xsrc)
            return xb, self.prenorm(l, i, xb, 0)
        cur = do_pre(0)
        for i in range(NT):
            t0, lo, hi, s = tile_info(i)
            xb, hb = cur
            hT = self.hT[hb]
            hk = 'hT%d' % hb
            P.dma('sync', rt[64:96, :, :], self.rope[:, :, t0:t0 + 256], [], ['rt'], buf='rt')
            for (col0, nch, dst, nrm, bi) in ((0, 3, cqnT, qn_sb, 1), (384, 2, ckvnT, kvn_sb, 2)):
                bst = bank[3]
                pend = None

                def stat(q, c, n):
                    self.P.op('tensor', (lambda e, o=bst[:, 0:256], r=self.sq[q][:, 0:256], c=c, n=n: e.matmul(o, lhsT=self.ones_bf[:], rhs=r, start=(c == 0), stop=(c == n - 1))),
                              ['ones', 'sq%d' % q], ['B3'])
                for c in range(nch):
                    pb = bank[bi] if c % 2 == 0 else bank[7]
                    pk = ('B%d' % bi) if c % 2 == 0 else 'B7'
                    self.MM(pb[:, 0:256], [(win_sb[:, kc, col0 + c * 128:col0 + (c + 1) * 128], hT[:, kc, 1:257]) for kc in range(8)],
                            WIN + [hk], [pk])
                    if pend is not None:
                        stat(*pend)
                    E('vector', 'tensor_copy', [], [pk, 'craw'], out=craw[:, c, :], in_=pb[:, 0:256])
                    q = self.rot('sq', 4)
                    E('scalar', 'activation', [], [pk, 'sq%d' % q], out=self.sq[q][:, 0:256], in_=pb[:, 0:256], func=AF.Square)
                    pend = (q, c, nch)
                stat(*pend)
                self.rstd_from_bank(bst, 'B3', 0, 256, rq, 'rq', nch * 128)
                for c in range(nch):
                    E('vector', 'scalar_tensor_tensor', ['craw', 'rq', 'qnw', 'kvnw'], ['cn%d_%d' % (bi, i)], out=dst[:, c, t0:t0 + 256], in0=craw[:, c, :],
                      scalar=nrm[:, c:c + 1], in1=rq, op0=ALU.mult, op1=ALU.mult)
            if i + 1 < NT:
                cur = do_pre(i + 1)
            pb = bank[4]
            self.MM(pb[64:96, 0:256], [(win_sb[:, kc, 640:672], hT[:, kc, 1:257]) for kc in range(8)], WIN + [hk], ['B4'])
            self.MM(pb[64:96, 256:512], [(win_sb[:, kc, 2208:2240], hT[:, kc, 1:257]) for kc in range(8)], WIN + [hk], ['B4'])
            E('vector', 'tensor_tensor', ['rt'], ['B4', 't1'], out=t1[64:96, :], in0=pb[64:96, 0:256], in1=rt[64:96, 0, :], op=ALU.mult)
            E('vector', 'tensor_tensor', ['rt'], ['B4', 't2'], out=t2[64:96, :], in0=pb[64:96, 256:512], in1=rt[64:96, 1, :], op=ALU.mult)
            E('gpsimd', 'tensor_tensor', ['t1', 't2'], ['kr%d' % i], out=krT[64:96, t0:t0 + 256], in0=t1[64:96, :], in1=t2[64:96, :], op=ALU.add)
            for c in range(4):
                b5, b6 = ((5, 6), (3, 4))[c % 2]
                k5, k6 = 'B%d' % b5, 'B%d' % b6
                pgc, pxv, pgb = bank[b5], bank[b6], bank[1 + (c % 2)]
                kgb = 'B%d' % (1 + (c % 2))
                self.MM(pgc[:, lo:hi], [(win_sb[:, kc, 1184 + c * 128:1184 + (c + 1) * 128], hT[:, kc, lo:hi]) for kc in range(8)], WIN + [hk], [k5])
                self.MM(pxv[:, lo:hi], [(win_sb[:, kc, 1696 + c * 128:1696 + (c + 1) * 128], hT[:, kc, lo:hi]) for kc in range(8)], WIN + [hk], [k6])
                self.MM(pgb[:, 0:256], [(win_sb[:, kc, 672 + c * 128:672 + (c + 1) * 128], hT[:, kc, 1:257]) for kc in range(8)], WIN + [hk], [kgb])
                E('scalar', 'activation', [], [k5, 'gcs'], out=gcs[:, lo:hi], in_=pgc[:, lo:hi], func=AF.Identity)
                E('vector', 'tensor_tensor', ['gcs'], [k6, 'vb'], out=vb[:, lo:hi], in0=pxv[:, lo:hi], in1=gcs[:, lo:hi], op=ALU.mult)
                w = scw_sb[:, c, :]
                E('vector', 'tensor_scalar', ['vb', 'scw'], ['ysc'], out=ysc[:, 0:256], in0=vb[:, 1:257], scalar1=w[:, 1:2], scalar2=None, op0=ALU.mult)
                o0 = 0 if lo == 0 else 1
                E('vector', 'scalar_tensor_tensor', ['vb', 'scw'], ['ysc'], out=ysc[:, o0:256], in0=vb[:, o0:256], scalar=w[:, 0:1],
                  in1=ysc[:, o0:256], op0=ALU.mult, op1=ALU.add)
                o1 = 256 if hi == 258 else 255
                E('vector', 'scalar_tensor_tensor', ['vb', 'scw'], ['ysc'], out=ysc[:, 0:o1], in0=vb[:, 2:2 + o1], scalar=w[:, 2:3],
                  in1=ysc[:, 0:o1], op0=ALU.mult, op1=ALU.add)
                E('vector', 'tensor_tensor', ['ysc'], [kgb, 'sc%d' % i], out=scT[:, c, t0:t0 + 256], in0=pgb[:, 0:256], in1=ysc, op=ALU.mult)
        self.P.barrier()
        self.aoff = mark
        K_all = self.carve([128, 8, T], BF16)
        V_all = self.carve([128, 18, 8, 65], BF16)
        Qt = [self.carve([128, 8, 512], BF16) for _ in range(1)]
        PT = [self.carve([128, 512], BF16) for _ in range(4)]
        atok = self.carve([128, 4, 512], BF16)
        rinv = self.carve([128, 4], F32)
        rtq = self.carve([128, 2, 512], F32)
        u1 = self.carve([128, 512], F32)
        u2 = self.carve([128, 512], F32)
        E('gpsimd', 'memset', [], ['V_all'], ap=V_all, constant=1.0)
        for h in range(8):
            E('gpsimd', 'tensor_copy', [], ['K%d' % h], out=K_all[64:96, h, :], in_=krT[64:96, :])
            for n0 in range(0, T, 512):
                n1 = min(T, n0 + 512)
                bi = 1 + (h * 5 + n0 // 512) % 2
                self.MM(bank[bi][0:64, 0:n1 - n0], [(wukv_sb[:, kc, h * 128:h * 128 + 64], ckvnT[:, kc, n0:n1]) for kc in range(2)], ['wukv'], ['B%d' % bi])
                E('scalar', 'activation', [], ['B%d' % bi, 'K%d' % h], out=K_all[0:64, h, n0:n1], in_=bank[bi][0:64, 0:n1 - n0], func=AF.Identity)
        wv = wukv_sb.rearrange("p k (h x) -> p k h x", x=128)
        for ch in range(18):
            bi = 3 + ch % 2
            self.MM(bank[bi][:, :].rearrange("p (h x) -> p h x", x=64), [(ckvnT[:, kc, ch * 128:(ch + 1) * 128], wv[:, kc, :, 64:128]) for kc in range(2)], ['wukv'], ['B%d' % bi])
            E('vector', 'tensor_copy', [], ['B%d' % bi, 'V_all'], out=V_all[:, ch, :, 0:64], in_=bank[bi][:, :].rearrange("p (h x) -> p h x", x=64))
        self.P.barrier()
        Qt = [Qt[0], self.arena[:, off_ckv // 2:off_ckv // 2 + 4096].rearrange("p (a b) -> p a b", a=8)]
        qtiles = [(0, 256, 2)] + [(NCTX + 512 * k, 512, 18) for k in range(4)]

        def qproj(q0, nq, qb):
            Q = Qt[qb]
            qk = 'Qt%d' % qb
            P.dma('sync', rtq[64:96, :, 0:nq], self.rope[:, 2:4, q0:q0 + nq], [], ['rtq'], buf='rtq')
            for h in range(8):
                pq, kq, pq2 = bank[5], 'B5', bank[0]
                self.MM(pq[0:96, 0:nq], [(wuq_sb[:, kc, h * 96:(h + 1) * 96], cqnT[:, kc, q0:q0 + nq]) for kc in range(3)], ['wuq'], [kq])
                self.MM(pq2[64:96, 0:nq], [(wuq_sb[:, kc, 768 + h * 32:768 + (h + 1) * 32], cqnT[:, kc, q0:q0 + nq]) for kc in range(3)], ['wuq'], ['B0'])
                E('scalar', 'activation', [], [kq, qk], out=Q[0:64, h, 0:nq], in_=pq[0:64, 0:nq], func=AF.Identity, scale=MLA_SCALE)
                E('vector', 'tensor_tensor', ['rtq'], [kq, 'u1'], out=u1[64:96, 0:nq], in0=pq[64:96, 0:nq], in1=rtq[64:96, 0, 0:nq], op=ALU.mult)
                E('vector', 'tensor_tensor', ['rtq'], ['B0', 'u2'], out=u2[64:96, 0:nq], in0=pq2[64:96, 0:nq], in1=rtq[64:96, 1, 0:nq], op=ALU.mult)
                E('gpsimd', 'tensor_tensor', ['u1', 'u2'], [qk], out=Q[64:96, h, 0:nq], in0=u1[64:96, 0:nq], in1=u2[64:96, 0:nq], op=ALU.add)
        qproj(qtiles[0][0], qtiles[0][1], 0)
        for qi, (q0, nq, nkc) in enumerate(qtiles):
            qb = qi % 2
            Q = Qt[qb]
            qk = 'Qt%d' % qb
            nsub = nq // 128
            for h in range(8):
                if h == 4 and qi + 1 < len(qtiles):
                    qproj(qtiles[qi + 1][0], qtiles[qi + 1][1], 1 - qb)
                ob = bank[6 + h % 2]
                ok = 'B%d' % (6 + h % 2)
                pvq = []

                def pv_emit(pi, kc, h=h, ob=ob, ok=ok, nsub=nsub, nkc=nkc):
                    def pv(e, ob=ob, pt=PT[pi], ch=kc, h=h, nsub=nsub, first=(kc == 0), lastk=(kc == nkc - 1)):
                        ins = None
                        for sidx in range(nsub):
                            ins = e.matmul(ob[:, sidx * 65:(sidx + 1) * 65], lhsT=pt[:, sidx * 128:(sidx + 1) * 128], rhs=V_all[:, ch, h, :],
                                           start=(first and sidx == 0), stop=lastk, skip_group_check=True)
                        return ins
                    P.op('tensor', pv, ['PT%d' % pi, 'V_all'], [ok])
                for kc in range(nkc):
                    sb_i = 1 + (kc % 3)
                    ps = bank[sb_i]
                    self.MM(ps[:, 0:nq], [(K_all[0:96, h, kc * 128:(kc + 1) * 128], Q[0:96, h, 0:nq])], ['K%d' % h, qk], ['B%d' % sb_i])
                    if len(pvq) >= 2:
                        pv_emit(*pvq.pop(0))
                    pi = self.rot('PT', 4)
                    E('scalar', 'activation', [], ['B%d' % sb_i, 'PT%d' % pi], out=PT[pi][:, 0:nq], in_=ps[:, 0:nq], func=AF.Exp)
                    pvq.append((pi, kc))
                while pvq:
                    pv_emit(*pvq.pop(0))
                ov = ob[:, 0:nsub * 65].rearrange("p (s x) -> p s x", x=65)
                E('vector', 'reciprocal', [], [ok, 'rinv'], out=rinv[:, 0:nsub], in_=ov[:, :, 64])
                E('vector', 'tensor_tensor', ['rinv'], [ok, 'atok'], out=atok[:, 0:nsub, h * 64:(h + 1) * 64], in0=ov[:, :, 0:64],
                  in1=rinv[:, 0:nsub].unsqueeze(2).to_broadcast([128, nsub, 64]), op=ALU.mult)
            for sidx in range(nsub):
                pt = bank[4][:, :].bitcast(BF16)
                for fc in range(4):
                    E('tensor', 'transpose', ['atok', 'ident'], ['B4'], out=pt[:, fc * 128:(fc + 1) * 128], in_=atok[:, sidx, fc * 128:(fc + 1) * 128], identity=self.ident[:])
                E('vector', 'tensor_copy', [], ['B4', 'attnT'], out=attnT[:, :, q0 + sidx * 128:q0 + (sidx + 1) * 128],
                  in_=pt[:, 0:512].rearrange("p (f x) -> p f x", x=128))
        self.P.barrier()
        self.aoff = mark
        wout_sb = self.carve([128, 8, 1024], BF16)
        P.dma('gpsimd', wout_sb, self.hwout[i2], [], ['wout'], buf='wout')
        nxb = self.load_x(0, self.xsrc, halo=False)
        deferred = []
        obanks = (0, 1, 2, 3, 4, 5, 6)
        for i in range(NT):
            t0, lo, hi, s = tile_info(i)
            xb = nxb

            def ogroup(m, t0=t0):
                bi = obanks[m % 7]
                pairs = [(wout_sb[:, kc, m * 128:(m + 1) * 128], attnT[:, kc, t0:t0 + 256]) for kc in range(4)]
                pairs += [(wout_sb[:, 4 + kc, m * 128:(m + 1) * 128], scT[:, kc, t0:t0 + 256]) for kc in range(4)]
                self.MM(self.bank[bi][:, 0:256], pairs, ['wout'], ['B%d' % bi])

            def evac(c):
                bi = obanks[c % 7]
                self.E('vector', 'tensor_copy', [], ['B%d' % bi, 'mtmp'], out=self.mtmp[:, c, :], in_=self.bank[bi][:, 0:256])
                self.E('scalar', 'activation', [], ['B%d' % bi, 'sqp'], out=self.sqp[:, c, 0:256], in_=self.bank[bi][:, 0:256], func=AF.Square)
            for m in range(7):
                ogroup(m)
            while deferred:
                deferred.pop(0)()
            if i + 1 < NT:
                nxb = self.load_x(i + 1, self.xsrc, halo=False)
            evac(0)
            ogroup(7)
            for c in range(1, 8):
                evac(c)
            for c in range(8):
                self.P.op('tensor', (lambda e, o=self.bank[7][:, 0:256], r=self.sqp[:, c, 0:256], c=c: e.matmul(o, lhsT=self.ones_bf[:], rhs=r, start=(c == 0), stop=(c == 7))),
                          ['ones', 'sqp'], ['B7'])
            deferred = self.res_tail(l, i, 0, xb, False)
        while deferred:
            deferred.pop(0)()
        self.swap_stream()

    def mask(self, ap, key, cm, pat, cmp):
        self.E('gpsimd', 'memset', [], [key], ap=ap, constant=1.0)
        self.E('gpsimd', 'affine_select', [key], [key], out=ap, in_=ap, pattern=[[pat, 128]], compare_op=cmp, fill=0.0, base=0, channel_multiplier=cm)

    def odd_mixer(self, l, last):
        i2 = l // 2
        self.ssd_projpass(l, i2)
        for d in (0, 1):
            self.ssd_sweep(l, i2, d)
        self.ssd_out(l, i2, last)
        self.swap_stream()

    def ssd_projpass(self, l, i2):
        P = self.P
        E = self.E
        self.phase(resid=False)
        wx = self.carve([128, 8, 3136], BF16)
        cv = self.carve([128, 24, 4], F32)
        sbc = self.carve([128, 160], F32)
        negA = self.carve([128, 64], F32)
        identf = self.carve([128, 128], F32)
        xs_toks = [self.carve([128, 2, 2048], F32) for _ in range(2)]
        bcts = [self.carve([128, 3072], BF16) for _ in range(2)]
        dtcs = [self.carve([128, 256], F32) for _ in range(2)]
        yc = [self.carve([128, 256], F32) for _ in range(5)]
        sil = [self.carve([128, 256], F32) for _ in range(8)]
        dtr = self.carve([128, 2, 64], F32)
        dta = self.carve([128, 2, 64], F32)
        dte = self.carve([128, 2, 64], F32)
        P.dma_group('gpsimd', [(wx[:, kc], self.swin[i2, :, kc, 2048:5184], 'wx%d' % kc) for kc in range(8)], 'wxg')
        P.dma('sync', cv, self.scv[i2], [], ['cv'], buf='cv')
        P.dma('sync', sbc, self.sbc[i2], [], ['sbc'], buf='sbc')
        E('scalar', 'activation', ['sbc'], ['negA'], out=negA, in_=sbc[:, 0:64], func=AF.Exp)
        E('vector', 'tensor_scalar', ['negA'], ['negA'], out=negA, in0=negA, scalar1=-1.0, scalar2=None, op0=ALU.mult)
        self.mask(identf, 'identf', 1, -1, ALU.is_equal)
        WX = ['wx%d' % kc for kc in range(8)]

        def do_pre(i):
            xb = self.load_x(i, self.xsrc)
            return xb, self.prenorm(l, i, xb, 0)
        cur = do_pre(0)
        for i in range(NT):
            p = i % 2
            xb, hb = cur
            if i + 1 < NT:
                cur = do_pre(i + 1)
            b = bcts[p]
            dc = dtcs[p]
            self.ssd_proj(l, i, self.hT[hb], 'hT%d' % hb, wx, cv, sbc, negA, identf, yc, sil, dtr, dta, dte, xs_toks[p],
                          b[:, 0:1024].rearrange("p (c x) -> p c x", c=2), b[:, 1024:2048].rearrange("p (g x) -> p g x", g=4),
                          b[:, 2048:3072].rearrange("p (g x) -> p g x", g=4), dc[:, 0:128].rearrange("p (c x) -> p c x", c=2),
                          dc[:, 128:256].rearrange("p (c x) -> p c x", c=2), WX, p)
            P.dma('sync', self.sxs[i], xs_toks[p], ['xs_tok%d' % p], ['sxs%d' % i], buf='sxs%d' % p)
            P.dma('sync', self.sbt[i], bcts[p], ['B_tok%d' % p, 'BT%d' % p, 'CT%d' % p], ['sbt%d' % i], buf='sbt%d' % p)
            P.dma('sync', self.sdt[i], dtcs[p], ['dtv%d' % p, 'a_tok%d' % p], ['sdt%d' % i], buf='sdt%d' % p)

    def ssd_sweep(self, l, i2, d):
        P = self.P
        E = self.E
        bank = self.bank
        self.phase(resid=False)
        sbc = self.carve([128, 160], F32)
        negA = self.carve([128, 64], F32)
        Minc = self.carve([128, 128], F32)
        Mstr = self.carve([128, 128], F32)
        nset = 2
        xs_toks = [self.carve([128, 2, 2048], F32) for _ in range(nset)]
        bcts = [self.carve([128, 3072], BF16) for _ in range(nset)]
        dtcs = [self.carve([128, 256], F32) for _ in range(nset)]
        nbs = 2
        ed2 = [self.carve([128, 96], F32) for _ in range(nbs)]
        w22 = [self.carve([128, 32], F32) for _ in range(nbs)]
        RE2 = [[self.carve([128, 8, 128], F32) for _ in range(4)] for _ in range(nbs)]
        MT82 = [[self.carve([128, 8, 128], BF16) for _ in range(4)] for _ in range(nbs)]
        cbm2 = [self.carve([128, 4, 128], F32) for _ in range(nbs)]
        xdt2 = [[self.carve([128, 8, 64], BF16) for _ in range(4)] for _ in range(nbs)]
        xd2 = [[self.carve([128, 8, 64], BF16) for _ in range(4)] for _ in range(nbs)]
        tmpy = [self.carve([128, 512], F32) for _ in range(2)]
        ych = [self.carve([128, 2048], F32) for _ in range(2)]
        S = self.carve([128, 2048], F32)
        S_bf = self.carve([128, 2048], BF16)
        tmps = [self.carve([128, 512], F32) for _ in range(2)]
        yfb = self.carve([128, 2048], F32) if d == 1 else None
        P.dma('sync', sbc, self.sbc[i2], [], ['sbc'], buf='sbc')
        E('scalar', 'activation', ['sbc'], ['negA'], out=negA, in_=sbc[:, 0:64], func=AF.Exp)
        E('vector', 'tensor_scalar', ['negA'], ['negA'], out=negA, in0=negA, scalar1=-1.0, scalar2=None, op0=ALU.mult)
        if d == 0:
            self.mask(Minc, 'Minc', -1, 1, ALU.is_ge)
            self.mask(Mstr, 'Mstr', 1, -1, ALU.is_gt)
        else:
            self.mask(Minc, 'Minc', 1, -1, ALU.is_ge)
            self.mask(Mstr, 'Mstr', -1, 1, ALU.is_gt)
        E('gpsimd', 'memset', [], ['S%d' % g for g in range(4)], ap=S, constant=0.0)
        E('gpsimd', 'memset', [], ['S_bf%d' % g for g in range(4)], ap=S_bf, constant=0.0)
        WX = ['wx%d' % kc for kc in range(8)]
        order = list(range(NT)) if d == 0 else [0] + list(range(NT - 1, 0, -1))
        def do_pre(i):
            xb = self.load_x(i, self.xsrc)
            return xb, self.prenorm(l, i, xb, 0)

        def views(p):
            b = bcts[p]
            dc = dtcs[p]
            return (xs_toks[p], b[:, 0:1024].rearrange("p (c x) -> p c x", c=2), b[:, 1024:2048].rearrange("p (g x) -> p g x", g=4),
                    b[:, 2048:3072].rearrange("p (g x) -> p g x", g=4), dc[:, 0:128].rearrange("p (c x) -> p c x", c=2), dc[:, 128:256].rearrange("p (c x) -> p c x", c=2))

        def do_load(i, p):
            P.dma('sync', xs_toks[p], self.sxs[i], ['sxs%d' % i], ['xs_tok%d' % p], buf='lxs%d' % p)
            P.dma('sync', bcts[p], self.sbt[i], ['sbt%d' % i], ['B_tok%d' % p, 'BT%d' % p, 'CT%d' % p], buf='lbt%d' % p)
            P.dma('sync', dtcs[p], self.sdt[i], ['sdt%d' % i], ['dtv%d' % p, 'a_tok%d' % p], buf='ldt%d' % p)
        do_load(order[0], 0)
        for oi, i in enumerate(order):
            t0, lo, hi, s = tile_info(i)
            p = oi % 2
            xs_tok, B_tok, BT, CT, dtv, a_tok = views(p)
            KXS, KBTOK, KBT, KCT, KDTV, KATOK = 'xs_tok%d' % p, 'B_tok%d' % p, 'BT%d' % p, 'CT%d' % p, 'dtv%d' % p, 'a_tok%d' % p
            if oi + 1 < len(order):
                do_load(order[oi + 1], (oi + 1) % 2)
            def stage_ab(chi, ch, bs):
                    cidx = i * 2 + ch
                    c0 = ch * 128
                    a_c = a_tok[:, ch, d * 32:(d + 1) * 32]
                    dt_c = dtv[:, ch, d * 32:(d + 1) * 32]
                    E('tensor', 'matmul', [KATOK, 'Minc'], ['B0'], out=bank[0][:, 0:32], lhsT=Minc, rhs=a_c, start=True, stop=True)
                    E('tensor', 'matmul', [KATOK, 'Mstr'], ['B0'], out=bank[0][:, 32:64], lhsT=Mstr, rhs=a_c, start=True, stop=True)
                    E('tensor', 'matmul', [KATOK, 'ones'], ['B0'], out=bank[0][:, 64:96], lhsT=self.ones[:], rhs=a_c, start=True, stop=True)
                    for g in range(4):
                        E('tensor', 'matmul', [KBT, KCT], ['B4'], out=bank[4][:, g * 128:(g + 1) * 128], lhsT=BT[:, g, c0:c0 + 128], rhs=CT[:, g, c0:c0 + 128], start=True, stop=True)
                    E('scalar', 'activation', [], ['B0', 'ed%d' % bs], out=ed2[bs], in_=bank[0][:, 0:96], func=AF.Exp)
                    for g in range(4):
                        h0 = g * 8
                        if g % 2 == 0:
                            E('gpsimd', 'tensor_tensor', ['Minc', KATOK], ['RE%d_%d' % (g, bs)], out=RE2[bs][g], in0=Minc.unsqueeze(1).to_broadcast([128, 8, 128]),
                              in1=a_c[:, h0:h0 + 8].unsqueeze(2).to_broadcast([128, 8, 128]), op=ALU.mult)
                        else:
                            for hh in range(8):
                                E('scalar', 'activation', ['Minc', KATOK], ['RE%d_%d' % (g, bs)], out=RE2[bs][g][:, hh, :], in_=Minc, func=AF.Identity, scale=a_c[:, h0 + hh:h0 + hh + 1])
                    E('vector', 'tensor_tensor', ['Minc'], ['B4', 'cbm%d' % bs], out=cbm2[bs], in0=bank[4][:, :].rearrange("p (g x) -> p g x", x=128),
                      in1=Minc.unsqueeze(1).to_broadcast([128, 4, 128]), op=ALU.mult)
                    E('vector', 'tensor_tensor', ['ed%d' % bs, KDTV], ['w2%d' % bs], out=w22[bs], in0=ed2[bs][:, 32:64], in1=dt_c, op=ALU.mult)
                    for g in range(4):
                        h0 = g * 8
                        pair = (1, 2) if g % 2 == 0 else (3, 7)
                        for half in range(2):
                            bi = pair[half]
                            E('tensor', 'matmul', ['RE%d_%d' % (g, bs), 'Mstr'], ['B%d' % bi], out=bank[bi][:, :], lhsT=Mstr, rhs=RE2[bs][g][:, half * 4:half * 4 + 4, :].rearrange("p h x -> p (h x)"), start=True, stop=True)
                        for half in range(2):
                            bi = pair[half]
                            E('scalar', 'activation', [], ['B%d' % bi, 'RE%d_%d' % (g, bs)], out=RE2[bs][g][:, half * 4:half * 4 + 4, :], in_=bank[bi][:, :].rearrange("p (h x) -> p h x", x=128), func=AF.Exp)
                        E('vector', 'tensor_tensor', ['RE%d_%d' % (g, bs), 'cbm%d' % bs], ['MT8%d_%d' % (g, bs)], out=MT82[bs][g], in0=RE2[bs][g], in1=cbm2[bs][:, g, :].unsqueeze(1).to_broadcast([128, 8, 128]), op=ALU.mult)
                        xsg = xs_tok[:, ch, h0 * 64:(h0 + 8) * 64].rearrange("p (h x) -> p h x", x=64)
                        E('gpsimd', 'tensor_tensor', [KXS, KDTV], ['xdt%d_%d' % (g, bs)], out=xdt2[bs][g], in0=xsg, in1=dt_c[:, h0:h0 + 8].unsqueeze(2).to_broadcast([128, 8, 64]), op=ALU.mult)
                        E('gpsimd', 'tensor_tensor', [KXS, 'w2%d' % bs], ['xd%d_%d' % (g, bs)], out=xd2[bs][g], in0=xsg, in1=w22[bs][:, h0:h0 + 8].unsqueeze(2).to_broadcast([128, 8, 64]), op=ALU.mult)

            def stage_c(chi, ch, bs):
                    cidx = i * 2 + ch
                    c0 = ch * 128
                    a_c = a_tok[:, ch, d * 32:(d + 1) * 32]
                    dt_c = dtv[:, ch, d * 32:(d + 1) * 32]
                    yb = cidx % 2
                    ycur = ych[yb]
                    yk = 'ych%d' % yb
                    for g in range(4):
                        h0 = g * 8
                        r = g % 2
                        bd, bo, bst = ((4, 5, 6), (0, 1, 2))[r]
                        for hh in range(8):
                            E('tensor', 'matmul', ['MT8%d_%d' % (g, bs), 'xdt%d_%d' % (g, bs)], ['B%d' % bd], out=bank[bd][:, hh * 64:(hh + 1) * 64], lhsT=MT82[bs][g][:, hh, :], rhs=xdt2[bs][g][:, hh, :], start=True, stop=True)
                        E('tensor', 'matmul', [KCT, 'S_bf%d' % g], ['B%d' % bo], out=bank[bo][:, :], lhsT=CT[:, g, c0:c0 + 128], rhs=S_bf[:, g * 512:(g + 1) * 512], start=True, stop=True)
                        E('tensor', 'matmul', [KBTOK, 'xd%d_%d' % (g, bs)], ['B%d' % bst], out=bank[bst][:, :], lhsT=B_tok[:, ch, g * 128:(g + 1) * 128], rhs=xd2[bs][g].rearrange("p h x -> p (h x)"), start=True, stop=True)
                        ty = tmpy[r]
                        E('vector', 'tensor_tensor', ['ed%d' % bs], ['B%d' % bo, 'tmpy%d' % r], out=ty.rearrange("p (h x) -> p h x", x=64), in0=bank[bo][:, :].rearrange("p (h x) -> p h x", x=64),
                          in1=ed2[bs][:, h0:h0 + 8].unsqueeze(2).to_broadcast([128, 8, 64]), op=ALU.mult)
                        E('vector', 'tensor_tensor', ['tmpy%d' % r], ['B%d' % bd, yk], out=ycur[:, g * 512:(g + 1) * 512], in0=bank[bd][:, :], in1=ty, op=ALU.add)
                        E('gpsimd', 'tensor_tensor', ['S%d' % g, 'ed%d' % bs], ['tmps%d' % r], out=tmps[r].rearrange("p (h x) -> p h x", x=64), in0=S[:, g * 512:(g + 1) * 512].rearrange("p (h x) -> p h x", x=64),
                          in1=ed2[bs][:, 64 + h0:64 + h0 + 8].unsqueeze(2).to_broadcast([128, 8, 64]), op=ALU.mult)
                        E('vector', 'tensor_tensor', ['tmps%d' % r], ['B%d' % bst, 'S%d' % g], out=S[:, g * 512:(g + 1) * 512], in0=bank[bst][:, :], in1=tmps[r], op=ALU.add)
                        E('scalar', 'activation', ['S%d' % g], ['S_bf%d' % g], out=S_bf[:, g * 512:(g + 1) * 512], in_=S[:, g * 512:(g + 1) * 512], func=AF.Identity)
                    if d == 0:
                        P.dma('sync', self.ydram[cidx], ycur, [yk], ['yd%d' % cidx], buf='yst%d' % yb)
                    else:
                        E('vector', 'tensor_tensor', [KXS, 'sbc'], ['yfb'], out=yfb.rearrange("p (h x) -> p h x", x=64), in0=xs_tok[:, ch, :].rearrange("p (h x) -> p h x", x=64),
                          in1=sbc[:, 128:160].unsqueeze(2).to_broadcast([128, 32, 64]), op=ALU.mult)
                        E('vector', 'tensor_tensor', ['yfb', yk], [yk], out=ycur, in0=ycur, in1=yfb, op=ALU.add)
                        P.dma('gpsimd', self.ydram[cidx], ycur, [yk], ['yd%d' % cidx], buf='yst%d' % yb, accum_op=ALU.add)


            chs = (0, 1) if d == 0 else (1, 0)
            if nbs == 2:
                stage_ab(0, chs[0], 0)
                stage_ab(1, chs[1], 1)
                stage_c(0, chs[0], 0)
                stage_c(1, chs[1], 1)
            else:
                stage_ab(0, chs[0], 0)
                stage_c(0, chs[0], 0)
                if d == 0 and oi + 1 < len(order):
                    cur = do_pre(order[oi + 1])
                stage_ab(1, chs[1], 0)
                stage_c(1, chs[1], 0)

    def ssd_proj(self, l, i, hT, hk, wx, cv, sbc, negA, identf, yc, sil, dtr, dta, dte, xs_tok, B_tok, BT, CT, dtv, a_tok, WX, p=0):
        P = self.P
        E = self.E
        bank = self.bank
        t0, lo, hi, s = tile_info(i)
        pend = []

        def post(cch, q):
            sl = sil[q]
            tb = (3, 7, 6)[cch % 3]
            for ch in range(2):
                E('tensor', 'matmul', ['sil%d' % q, 'identf'], ['B%d' % tb], out=bank[tb][:, ch * 128:(ch + 1) * 128], lhsT=sl[:, ch * 128:(ch + 1) * 128], rhs=identf, start=True, stop=True)
            src = bank[tb][:, 0:256].rearrange("p (c x) -> p c x", x=128)
            if cch < 16:
                E('scalar', 'activation', [], ['B%d' % tb, 'xs_tok%d' % p], out=xs_tok[:, :, cch * 128:(cch + 1) * 128], in_=src, func=AF.Identity)
            else:
                g = cch - 16
                E('vector', 'tensor_copy', [], ['B%d' % tb, 'B_tok%d' % p], out=B_tok[:, :, g * 128:(g + 1) * 128], in_=src)
        spend = []

        def do_silu(cch, y, yk):
            if cch < 20:
                q = cch % 8
                E('scalar', 'activation', [yk], ['sil%d' % q], out=sil[q], in_=y, func=AF.Silu)
                if cch >= 16:
                    E('gpsimd', 'tensor_copy', ['sil%d' % q], ['BT%d' % p], out=BT[:, cch - 16, :], in_=sil[q])
                pend.append((cch, q))
            else:
                E('scalar', 'activation', [yk], ['CT%d' % p], out=CT[:, cch - 20, :], in_=y, func=AF.Silu)
        for cch in range(24):
            bi = (1, 2, 4, 5, 0)[cch % 5]
            pb = bank[bi]
            bkey = 'B%d' % bi
            self.MM(pb[:, lo:hi], [(wx[:, kc, cch * 128:(cch + 1) * 128], hT[:, kc, lo:hi]) for kc in range(8)], WX + [hk], [bkey])
            if len(pend) >= 4:
                post(*pend.pop(0))
            q2 = cch % 5
            y = yc[q2]
            yk = 'yc%d' % q2
            w = cv[:, cch, :]
            E('scalar', 'activation', ['cv'], [bkey, yk], out=y[:, 0:256], in_=pb[:, 1:257], func=AF.Identity, scale=w[:, 1:2], bias=w[:, 3:4])
            o0 = 0 if lo == 0 else 1
            E('vector', 'scalar_tensor_tensor', ['cv'], [bkey, yk], out=y[:, o0:256], in0=pb[:, o0:256], scalar=w[:, 0:1], in1=y[:, o0:256], op0=ALU.mult, op1=ALU.add)
            o1 = 256 if hi == 258 else 255
            E('vector', 'scalar_tensor_tensor', ['cv'], [bkey, yk], out=y[:, 0:o1], in0=pb[:, 2:2 + o1], scalar=w[:, 2:3], in1=y[:, 0:o1], op0=ALU.mult, op1=ALU.add)
            if len(spend) >= 3:
                do_silu(*spend.pop(0))
            spend.append((cch, y, yk))
        while spend:
            do_silu(*spend.pop(0))
        while pend:
            post(*pend.pop(0))
        for ch in range(2):
            self.MM(bank[0][:, ch * 64:(ch + 1) * 64], [(hT[:, kc, 1 + ch * 128:1 + (ch + 1) * 128], wx[:, kc, 3072:3136]) for kc in range(8)], WX + [hk], ['B0'])
        E('vector', 'tensor_tensor', ['sbc'], ['B0', 'dtr'], out=dtr, in0=bank[0][:, 0:128].rearrange("p (c x) -> p c x", x=64),
          in1=sbc[:, 64:128].unsqueeze(1).to_broadcast([128, 2, 64]), op=ALU.add)
        E('scalar', 'activation', ['dtr'], ['dta'], out=dta, in_=dtr, func=AF.Abs)
        E('scalar', 'activation', ['dta'], ['dte'], out=dte, in_=dta, func=AF.Exp, scale=-1.0)
        E('scalar', 'activation', ['dte', 'ones'], ['dte'], out=dte, in_=dte, func=AF.Ln, bias=self.ones[:, 0:1])
        E('scalar', 'activation', ['dtr'], ['dta'], out=dta, in_=dtr, func=AF.Relu)
        E('vector', 'tensor_tensor', ['dta', 'dte'], ['dtv%d' % p], out=dtv, in0=dta, in1=dte, op=ALU.add)
        E('vector', 'tensor_tensor', ['dtv%d' % p, 'negA'], ['a_tok%d' % p], out=a_tok, in0=dtv, in1=negA.unsqueeze(1).to_broadcast([128, 2, 64]), op=ALU.mult)

    def ssd_out(self, l, i2, last):
        P = self.P
        E = self.E
        bank = self.bank
        self.phase()
        wz = self.carve([128, 8, 2048], BF16)
        wo = self.carve([128, 16, 1024], BF16)
        stg = [self.carve([128, 1024], F32) for _ in range(2)]
        nrm = self.carve([128, 16], F32)
        yb = [self.carve([128, 2048], F32) for _ in range(2)]
        sz = [self.carve([128, 512], F32) for _ in range(2)]
        gq = self.carve([128, 2048], F32)
        gn = self.carve([128, 2048], BF16)
        gT = self.carve([128, 16, 256], BF16)
        ssq = self.carve([128, 4], F32)
        rt = self.carve([128, 1], F32)
        junk = self.carve([128, 512], F32)
        P.dma_group('gpsimd', [(wz[:, kc], self.swin[i2, :, kc, 0:2048], 'wz%d' % kc) for kc in range(8)], 'wzg')
        P.dma('sync', nrm, self.snorm[i2], [], ['nrm'], buf='nrm')
        for fc in range(16):
            st = stg[fc % 2]
            sk = 'stg%d' % (fc % 2)
            P.dma('sync', st, self.swout[i2, :, fc], [], [sk], buf=sk)
            E('vector', 'tensor_scalar', [sk, 'nrm'], ['wo'], out=wo[:, fc, :], in0=st, scalar1=nrm[:, fc:fc + 1], scalar2=None, op0=ALU.mult)
        otiles = [i for i in range(NT) if not (last and i == 0)]
        xt_save = self.xt
        self.xt = self.xt + [self.carve([128, 8, 258], F32)]
        gq2 = [gq, self.carve([128, 2048], F32)]
        gn2 = [gn, self.carve([128, 2048], BF16)]
        gT2 = [gT, self.carve([128, 16, 256], BF16)]
        ssq2 = [ssq, self.carve([128, 4], F32)]
        rt2 = [rt, self.carve([128, 1], F32)]

        def do_pre(i):
            xb = self.load_x(i, self.xsrc, halo=False)
            return xb, self.prenorm(l, i, xb, 0, halo=False)

        def stage_z(i, ch, hT, hk):
            cidx = i * 2 + ch
            b = cidx % 2
            P.dma('sync', yb[b], self.ydram[cidx], ['yd%d' % cidx], ['yb%d' % b], buf='yb%d' % b)
            E('gpsimd', 'memset', [], ['ssq%d' % b], ap=ssq2[b], constant=0.0)
            pend = None

            def sq(q):
                E('scalar', 'activation', ['gq%d' % b], ['junk', 'ssq%d' % b], out=junk, in_=gq2[b][:, q * 512:(q + 1) * 512], func=AF.Square, accum_out=ssq2[b][:, q:q + 1])
            for q in range(4):
                bi = 1 + q % 2
                self.MM(bank[bi][:, :], [(hT[:, kc, 1 + ch * 128:1 + (ch + 1) * 128], wz[:, kc, q * 512:(q + 1) * 512]) for kc in range(8)], ['wz%d' % kc for kc in range(8)] + [hk], ['B%d' % bi])
                E('scalar', 'activation', [], ['B%d' % bi, 'sz%d' % (q % 2)], out=sz[q % 2], in_=bank[bi][:, :], func=AF.Silu)
                if pend is not None:
                    sq(pend)
                E('vector', 'tensor_tensor', ['sz%d' % (q % 2), 'yb%d' % b], ['gq%d' % b], out=gq2[b][:, q * 512:(q + 1) * 512], in0=yb[b][:, q * 512:(q + 1) * 512], in1=sz[q % 2], op=ALU.mult)
                pend = q
            sq(pend)
            E('vector', 'tensor_reduce', ['ssq%d' % b], ['rt%d' % b], out=rt2[b], in_=ssq2[b], axis=mybir.AxisListType.X, op=ALU.add)
            E('scalar', 'activation', ['rt%d' % b, 'epsc'], ['rt%d' % b], out=rt2[b], in_=rt2[b], func=AF.Sqrt, scale=1.0 / 2048, bias=self.epsc[:, 0:1])
            E('vector', 'reciprocal', ['rt%d' % b], ['rt%d' % b], out=rt2[b], in_=rt2[b])
            E('scalar', 'activation', ['gq%d' % b, 'rt%d' % b], ['gn%d' % b], out=gn2[b], in_=gq2[b], func=AF.Identity, scale=rt2[b][:, 0:1])

        def stage_t(i, ch, gi):
            b = (i * 2 + ch) % 2
            for fb in range(4):
                pt = bank[3 + fb % 2][:, :].bitcast(BF16)
                pk = 'B%d' % (3 + fb % 2)
                for f4 in range(4):
                    fc = fb * 4 + f4
                    E('tensor', 'transpose', ['gn%d' % b, 'ident'], [pk], out=pt[:, f4 * 128:(f4 + 1) * 128], in_=gn2[b][:, fc * 128:(fc + 1) * 128], identity=self.ident[:])
                E('vector', 'tensor_copy', [], [pk, 'gT%d' % gi], out=gT2[gi][:, fb * 4:fb * 4 + 4, ch * 128:(ch + 1) * 128], in_=pt[:, 0:512].rearrange("p (f x) -> p f x", x=128))

        def outproj(i, xb, gi):
            def get_chunk(m):
                bi = 5 + (m % 2)
                self.MM(self.bank[bi][:, 0:256], [(wo[:, fc, m * 128:(m + 1) * 128], gT2[gi][:, fc, :]) for fc in range(16)], ['wo', 'gT%d' % gi], ['B%d' % bi])
                return self.bank[bi][:, 0:256], 'B%d' % bi
            self.residual(l, i, 0, get_chunk, xb, False)
        self.nxt = 3
        cur = do_pre(otiles[0])
        prev = None
        for oi, i in enumerate(otiles):
            xb, hb = cur
            hT = self.hT[hb]
            hk = 'hT%d' % hb
            gi = oi % 2
            stage_z(i, 0, hT, hk)
            stage_z(i, 1, hT, hk)
            if oi + 1 < len(otiles):
                cur = do_pre(otiles[oi + 1])
            stage_t(i, 0, gi)
            if prev is not None:
                outproj(*prev)
            stage_t(i, 1, gi)
            prev = (i, xb, gi)
        outproj(*prev)
        self.xt = xt_save
        self.nxt = 2


def host_layout(inp):
    f = np.float32
    out = {}
    mod_w = inp['mod_w']
    out['modw'] = np.ascontiguousarray(mod_w.reshape(4, 8, 128, 12, 512).transpose(0, 3, 2, 1, 4)).astype(f)
    out['modb'] = np.ascontiguousarray(inp['mod_b'].reshape(4, 48, 128).transpose(0, 2, 1)).astype(f)
    out['normw'] = np.ascontiguousarray(inp['norm_w'].reshape(4, 4, 8, 128).transpose(0, 3, 1, 2)).astype(f)
    wu = inp['ffn_w_up'].reshape(4, 8, 128, 2, NJ, 128)
    out['wup'] = np.ascontiguousarray(wu.transpose(0, 4, 2, 1, 3, 5)).reshape(4, NJ, 128, 8, 256).astype(f)
    out['wdn'] = np.ascontiguousarray(inp['ffn_w_down'].reshape(4, NJ, 128, 1024).transpose(0, 2, 1, 3)).astype(f)
    out['fcw'] = np.ascontiguousarray(inp['ffn_conv_w'].reshape(4, 3, 2, NJ, 128).transpose(0, 4, 3, 2, 1)).astype(f)
    w_in = inp['hyb_w_in']
    krsw = np.concatenate([w_in[:, :, 656:672], w_in[:, :, 640:656]], axis=2)
    w_in2 = np.concatenate([w_in, krsw], axis=2)
    out['hwin'] = np.ascontiguousarray(w_in2.reshape(2, 8, 128, 2240).transpose(0, 2, 1, 3)).astype(f)
    wuq = inp['mla_w_uq']
    sw = []
    for h in range(8):
        sw.append(wuq[:, :, h * 96 + 80:h * 96 + 96])
        sw.append(wuq[:, :, h * 96 + 64:h * 96 + 80])
    wuq2 = np.concatenate([wuq] + sw, axis=2)
    out['hwuq'] = np.ascontiguousarray(wuq2.reshape(2, 3, 128, 1024).transpose(0, 2, 1, 3)).astype(f)
    out['hwukv'] = np.ascontiguousarray(inp['mla_w_ukv'].reshape(2, 2, 128, 1024).transpose(0, 2, 1, 3)).astype(f)
    out['hwout'] = np.ascontiguousarray(inp['hyb_w_out'].reshape(2, 8, 128, 1024).transpose(0, 2, 1, 3)).astype(f)
    out['hqn'] = np.ascontiguousarray(inp['mla_q_norm'].reshape(2, 3, 128).transpose(0, 2, 1)).astype(f)
    out['hkvn'] = np.ascontiguousarray(inp['mla_kv_norm'].reshape(2, 2, 128).transpose(0, 2, 1)).astype(f)
    out['hscw'] = np.ascontiguousarray(inp['sconv_w'].reshape(2, 3, 4, 128).transpose(0, 3, 2, 1)).astype(f)
    out['rope'] = rope_tables()
    out['swin'] = np.ascontiguousarray(inp['ssd_w_in'].reshape(2, 8, 128, 5184).transpose(0, 2, 1, 3)).astype(f)
    cw = np.concatenate([inp['ssd_conv_w'], inp['ssd_conv_b'][:, None, :]], axis=1)
    out['scv'] = np.ascontiguousarray(cw.reshape(2, 4, 24, 128).transpose(0, 3, 2, 1)).astype(f)
    row = np.concatenate([inp['ssd_a_log'].reshape(2, 64), inp['ssd_dt_bias'].reshape(2, 64), inp['ssd_d'].reshape(2, 32)], axis=1)
    out['sbc'] = np.ascontiguousarray(np.broadcast_to(row[:, None, :], (2, 128, 160))).astype(f)
    out['snorm'] = np.ascontiguousarray(inp['ssd_norm'].reshape(2, 16, 128).transpose(0, 2, 1)).astype(f)
    out['swout'] = np.ascontiguousarray(inp['ssd_w_out'].reshape(2, 16, 128, 1024).transpose(0, 2, 1, 3)).astype(f)
    return out


def rope_tables():
    t = np.arange(NLAT)
    row = (t // 64).astype(np.float32)
    col = (t % 64).astype(np.float32)
    nf = 8
    inv = (np.float32(10000.0) ** (-np.arange(nf, dtype=np.float32) / nf)).astype(np.float32)
    ang = np.concatenate([row[:, None] * inv, col[:, None] * inv], axis=-1).astype(np.float32)
    cos = np.cos(ang).astype(np.float32).T
    sin = np.sin(ang).astype(np.float32).T
    tab = np.zeros((32, 4, T), np.float32)
    tab[:, 0, :NCTX] = 1.0
    tab[:, 2, :NCTX] = MLA_SCALE
    tab[0:16, 0, NCTX:] = cos
    tab[16:32, 0, NCTX:] = cos
    tab[0:16, 1, NCTX:] = -sin
    tab[16:32, 1, NCTX:] = sin
    tab[:, 2, NCTX:] = tab[:, 0, NCTX:] * np.float32(MLA_SCALE)
    tab[:, 3, NCTX:] = tab[:, 1, NCTX:] * np.float32(MLA_SCALE)
    return tab


def core_inputs(inp, b):
    xc = np.concatenate([inp['ctx'][b], inp['x'][b]], axis=0)
    xin = np.ascontiguousarray(xc.T.reshape(8, 128, T).transpose(1, 0, 2)).astype(np.float32)
    cc = np.stack([inp['c'][b], inp['c_ctx']], axis=-1)
    cc = np.ascontiguousarray(cc.reshape(8, 128, 2).transpose(1, 0, 2)).astype(np.float32)
    return {'xin': xin, 'cc': cc}


def kernel(**inputs):
    inp = {k: np.asarray(v) for k, v in inputs.items()}
    nb = inp['x'].shape[0]
    shared = host_layout(inp)
    in_maps = []
    for b in range(nb):
        m = dict(shared)
        m.update(core_inputs(inp, b))
        in_maps.append(m)
    B = Builder([0, 1, 2, 3])
    res = run_bass_kernel_spmd(B.nc, in_maps, core_ids=list(range(nb)))
    out = np.empty((nb, NLAT, D), np.float32)
    for b in range(nb):
        y = np.asarray(res.results[b]["yout"])
        out[b] = y.transpose(1, 0, 2).reshape(D, NLAT).T
    return out
```
